# Optimizing a Trainium2 kernel written in Bass

```python
import math
import jax, jax.numpy as jnp
from jax import lax
import numpy as np

D_MODEL = 2048
BATCH = 4
SEQ = 2048
DEPTH = 2

EPS = 1e-6
DSW_HEADS = 8
DSW_HEAD_DIM = D_MODEL // 16
DSW_WIDTH = DSW_HEADS * DSW_HEAD_DIM
DSW_PATTERNS = ((128, 1), (512, 4), (2048, 16))
DSW_BLOCK = 128
GLA_HEADS = 4
GLA_DK = D_MODEL // 16
GLA_DV = D_MODEL // 8
GLA_QK_WIDTH = GLA_HEADS * GLA_DK
GLA_WIDTH = GLA_HEADS * GLA_DV
GLA_RANK = 16
GLA_TAU = 16.0
S5_WIDTH = D_MODEL // 2
S5_GROUP = 16
S5_GROUPS = S5_WIDTH // S5_GROUP
S5_STATE = 64
S5_DT_MIN = 1e-3
S5_DT_MAX = 1e-1
HGRN_HEADS = 8
HGRN_DIM = D_MODEL // 16
HGRN_WIDTH = HGRN_HEADS * HGRN_DIM
CHUNK = 64
AB_WIDTH = DSW_WIDTH + GLA_WIDTH
CD_WIDTH = S5_WIDTH + HGRN_WIDTH
AB_SIZES = (DSW_WIDTH, DSW_WIDTH, DSW_WIDTH, GLA_QK_WIDTH, GLA_QK_WIDTH, GLA_WIDTH, GLA_RANK, AB_WIDTH)
CD_SIZES = (S5_WIDTH, HGRN_WIDTH, HGRN_WIDTH, HGRN_WIDTH, CD_WIDTH)
AB_IN = sum(AB_SIZES)
CD_IN = sum(CD_SIZES)
AB_SPLIT = tuple(int(v) for v in np.cumsum(AB_SIZES)[:-1])
CD_SPLIT = tuple(int(v) for v in np.cumsum(CD_SIZES)[:-1])
N_EVEN = (DEPTH + 1) // 2
N_ODD = DEPTH // 2

kernel_name = 'hybrid_dilated_gla_s5_hgrn2_trunk'


def rmsnorm(x, g):
    xf = x.astype(jnp.float32)
    return xf * lax.rsqrt(jnp.mean(xf * xf, axis=-1, keepdims=True) + EPS) * g


def head_rmsnorm(o, g):
    o = o * lax.rsqrt(jnp.mean(o * o, axis=-1, keepdims=True) + EPS)
    return o.reshape(o.shape[0], o.shape[1], -1) * g


def dilated_pattern(q, k, v, window, dilation):
    B, S, H, E = q.shape
    d = dilation
    M = S // d
    wc = window // d
    blk = DSW_BLOCK
    nb = -(-M // blk)
    Mp = nb * blk

    def strided(t):
        return jnp.pad(t.reshape(B, M, d, H, E), ((0, 0), (0, Mp - M), (0, 0), (0, 0), (0, 0)))

    def band(t):
        tp = jnp.pad(t, ((0, 0), (blk, 0), (0, 0), (0, 0), (0, 0)))
        prev = tp[:, :Mp].reshape(B, nb, blk, d, H, E)
        cur = t.reshape(B, nb, blk, d, H, E)
        return jnp.concatenate([prev, cur], axis=2)

    qb = strided(q).reshape(B, nb, blk, d, H, E)
    kb = band(strided(k))
    vb = band(strided(v))
    s = jnp.einsum('bnidhe,bnjdhe->bndhij', qb, kb) * (E ** -0.5)
    i_idx = np.arange(blk)[:, None]
    j_idx = np.arange(2 * blk)[None, :]
    n_idx = np.arange(nb)[:, None, None]
    offset = blk + i_idx - j_idx
    key_pos = (n_idx - 1) * blk + j_idx
    mask = (offset >= 0) & (offset <= wc) & (key_pos >= 0)
    s = jnp.where(mask[None, :, None, None], s, -jnp.inf)
    m = jnp.max(s, axis=-1, keepdims=True)
    p = jnp.exp(s - m)
    den = jnp.sum(p, axis=-1)
    o = jnp.einsum('bndhij,bnjdhe->bnidhe', p, vb) / jnp.transpose(den, (0, 1, 4, 2, 3))[..., None]
    lse = jnp.transpose(m[..., 0] + jnp.log(den), (0, 1, 4, 2, 3))
    o = o.reshape(B, Mp, d, H, E)[:, :M].reshape(B, S, H, E)
    lse = lse.reshape(B, Mp, d, H)[:, :M].reshape(B, S, H)
    return o, lse


def dilated_attention(q, k, v):
    outs, lses = [], []
    for window, dilation in DSW_PATTERNS:
        o, lse = dilated_pattern(q, k, v, window, dilation)
        outs.append(o)
        lses.append(lse)
    w = jax.nn.softmax(jnp.stack(lses, axis=0), axis=0)
    return jnp.einsum('pbsh,pbshe->bshe', w, jnp.stack(outs, axis=0))


def chunked_gated_linear_attention(q, k, v, log_g):
    B, S, H, K = q.shape
    V = v.shape[-1]
    n = S // CHUNK

    def blocks(t):
        return jnp.moveaxis(t.reshape(B, n, CHUNK, H, t.shape[-1]), 1, 0)

    qc, kc, vc = blocks(q), blocks(k), blocks(v)
    bc = jnp.cumsum(blocks(log_g), axis=2)
    causal = np.tril(np.ones((CHUNK, CHUNK), dtype=bool))[None, :, :, None, None]

    def step(state, inp):
        qi, ki, vi, bi = inp
        inter = jnp.einsum('bihk,bhkv->bihv', qi * jnp.exp(bi), state)
        diff = bi[:, :, None] - bi[:, None, :]
        decay = jnp.exp(jnp.where(causal, diff, -jnp.inf))
        scores = jnp.einsum('bihk,bjhk,bijhk->bhij', qi, ki, decay)
        intra = jnp.einsum('bhij,bjhv->bihv', scores, vi)
        b_last = bi[:, -1]
        k_dec = ki * jnp.exp(b_last[:, None] - bi)
        state = state * jnp.exp(b_last)[..., None] + jnp.einsum('bjhk,bjhv->bhkv', k_dec, vi)
        return state, inter + intra

    state0 = jnp.zeros((B, H, K, V), jnp.float32)
    _, out = lax.scan(step, state0, (qc, kc, vc, bc))
    return jnp.moveaxis(out, 0, 1).reshape(B, S, H, V)


def _cmul(ar, ai, br, bi):
    return ar * br - ai * bi, ar * bi + ai * br


def _s5_combine(e1, e2):
    a1r, a1i, b1r, b1i = e1
    a2r, a2i, b2r, b2i = e2
    ar, ai = _cmul(a2r, a2i, a1r, a1i)
    br, bi = _cmul(a2r, a2i, b1r, b1i)
    return ar, ai, br + b2r, bi + b2i


def s5_mixer(u, lam_re, lam_im, log_dt, b_re, b_im, c_re, c_im, d_skip, w_glu, b_glu):
    B, S, _ = u.shape
    f32 = jnp.float32
    lam_re, lam_im = lam_re.astype(f32), lam_im.astype(f32)
    b_re, b_im, c_re, c_im = b_re.astype(f32), b_im.astype(f32), c_re.astype(f32), c_im.astype(f32)
    ug = u.reshape(B, S, S5_GROUPS, S5_GROUP)
    dt = jnp.exp(log_dt.astype(f32))[:, None]
    mag = jnp.exp(lam_re * dt)
    ang = lam_im * dt
    ab_re, ab_im = mag * jnp.cos(ang), mag * jnp.sin(ang)
    inv = 1.0 / (lam_re * lam_re + lam_im * lam_im)
    nr, ni = ab_re - 1.0, ab_im
    co_re = (nr * lam_re + ni * lam_im) * inv
    co_im = (ni * lam_re - nr * lam_im) * inv
    bb_re, bb_im = _cmul(co_re[..., None], co_im[..., None], b_re, b_im)
    ut = jnp.swapaxes(ug, 0, 1)
    bu_re = jnp.einsum('gph,sbgh->sbgp', bb_re, ut)
    bu_im = jnp.einsum('gph,sbgh->sbgp', bb_im, ut)
    a_re = jnp.broadcast_to(ab_re, (S, 1, S5_GROUPS, S5_STATE))
    a_im = jnp.broadcast_to(ab_im, (S, 1, S5_GROUPS, S5_STATE))
    _, _, x_re, x_im = lax.associative_scan(_s5_combine, (a_re, a_im, bu_re, bu_im), axis=0)
    y = (jnp.einsum('ghp,sbgp->bsgh', c_re, x_re) - jnp.einsum('ghp,sbgp->bsgh', c_im, x_im)
         + d_skip * ug)
    y = jax.nn.gelu(y.reshape(B, S, S5_WIDTH))
    return y * jax.nn.sigmoid(y @ w_glu + b_glu)


def ab_layer(h, w_in, gla_w_gate, gla_b_gate, gla_norm_g, w_out):
    B, S, _ = h.shape
    z = h @ w_in
    qa, ka, va, qb, kb, vb, g_low, gate = jnp.split(z, AB_SPLIT, axis=-1)
    sh_a = (B, S, DSW_HEADS, DSW_HEAD_DIM)
    ya = dilated_attention(qa.reshape(sh_a), ka.reshape(sh_a), va.reshape(sh_a)).reshape(B, S, DSW_WIDTH)
    log_a = jax.nn.log_sigmoid(g_low @ gla_w_gate + gla_b_gate) / GLA_TAU
    sh_k = (B, S, GLA_HEADS, GLA_DK)
    ob = chunked_gated_linear_attention((qb * GLA_DK ** -0.5).reshape(sh_k), kb.reshape(sh_k),
                                        vb.reshape(B, S, GLA_HEADS, GLA_DV), log_a.reshape(sh_k))
    yb = head_rmsnorm(ob, gla_norm_g)
    y = jnp.concatenate([ya, yb], axis=-1) * jax.nn.silu(gate)
    return y @ w_out


def cd_layer(h, w_in, lam_re, lam_im, log_dt, b_re, b_im, c_re, c_im, d_skip, w_glu, b_glu,
             lower_bound, hgrn_norm_g, w_out):
    B, S, _ = h.shape
    z = h @ w_in
    u, qd, fd, idd, gate = jnp.split(z, CD_SPLIT, axis=-1)
    yc = s5_mixer(u, lam_re, lam_im, log_dt, b_re, b_im, c_re, c_im, d_skip, w_glu, b_glu)
    f = lower_bound + (1.0 - lower_bound) * jax.nn.sigmoid(fd)
    sh = (B, S, HGRN_HEADS, HGRN_DIM)
    od = chunked_gated_linear_attention(jax.nn.silu(qd).reshape(sh), (1.0 - f).reshape(sh),
                                        idd.reshape(sh), jnp.log(f).reshape(sh))
    yd = head_rmsnorm(od, hgrn_norm_g)
    y = jnp.concatenate([yc, yd], axis=-1) * jax.nn.silu(gate)
    return y @ w_out


def setup_inputs(seed: int = 0) -> dict:
    key = jax.random.key(seed)
    ks = jax.random.split(key, 24)
    f32 = jnp.float32

    def nrm(k, shape, s):
        return s * jax.random.normal(k, shape, f32)

    G, P, Hc = S5_GROUPS, S5_STATE, S5_GROUP
    return {
        'x': nrm(ks[0], (BATCH, SEQ, D_MODEL), 1.0),
        'norm_g': 1.0 + nrm(ks[1], (DEPTH, D_MODEL), 0.02),
        'final_g': 1.0 + nrm(ks[2], (D_MODEL,), 0.02),
        'ab_w_in': nrm(ks[3], (N_EVEN, D_MODEL, AB_IN), D_MODEL ** -0.5),
        'gla_w_gate': nrm(ks[4], (N_EVEN, GLA_RANK, GLA_QK_WIDTH), GLA_RANK ** -0.5),
        'gla_b_gate': nrm(ks[5], (N_EVEN, GLA_QK_WIDTH), 0.1),
        'gla_norm_g': 1.0 + nrm(ks[6], (N_EVEN, GLA_WIDTH), 0.02),
        'ab_w_out': nrm(ks[7], (N_EVEN, AB_WIDTH, D_MODEL), AB_WIDTH ** -0.5),
        'cd_w_in': nrm(ks[8], (N_ODD, D_MODEL, CD_IN), D_MODEL ** -0.5),
        's5_lam_re': -0.5 + nrm(ks[9], (N_ODD, G, P), 0.01),
        's5_lam_im': math.pi * jnp.arange(P, dtype=f32) + nrm(ks[10], (N_ODD, G, P), 0.01),
        's5_log_dt': jax.random.uniform(ks[11], (N_ODD, G), f32, math.log(S5_DT_MIN), math.log(S5_DT_MAX)),
        's5_b_re': nrm(ks[12], (N_ODD, G, P, Hc), (2 * Hc) ** -0.5),
        's5_b_im': nrm(ks[13], (N_ODD, G, P, Hc), (2 * Hc) ** -0.5),
        's5_c_re': nrm(ks[14], (N_ODD, G, Hc, P), 0.5),
        's5_c_im': nrm(ks[15], (N_ODD, G, Hc, P), 0.5),
        's5_d': nrm(ks[16], (N_ODD, G, Hc), 1.0),
        's5_w_glu': nrm(ks[17], (N_ODD, S5_WIDTH, S5_WIDTH), S5_WIDTH ** -0.5),
        's5_b_glu': nrm(ks[18], (N_ODD, S5_WIDTH), 0.1),
        'hgrn_gamma': nrm(ks[19], (DEPTH, HGRN_WIDTH), 1.0),
        'hgrn_norm_g': 1.0 + nrm(ks[20], (N_ODD, HGRN_WIDTH), 0.02),
        'cd_w_out': nrm(ks[21], (N_ODD, CD_WIDTH, D_MODEL), CD_WIDTH ** -0.5),
    }


def reference(x, norm_g, final_g, ab_w_in, gla_w_gate, gla_b_gate, gla_norm_g, ab_w_out,
              cd_w_in, s5_lam_re, s5_lam_im, s5_log_dt, s5_b_re, s5_b_im, s5_c_re, s5_c_im,
              s5_d, s5_w_glu, s5_b_glu, hgrn_gamma, hgrn_norm_g, cd_w_out):
    h = x.astype(jnp.float32)
    sm = jax.nn.softmax(hgrn_gamma.astype(jnp.float32), axis=0)
    lower_bounds = jnp.cumsum(sm, axis=0) - sm[0]
    for l in range(DEPTH):
        hn = rmsnorm(h, norm_g[l])
        j = l // 2
        if l % 2 == 0:
            h = h + ab_layer(hn, ab_w_in[j], gla_w_gate[j], gla_b_gate[j], gla_norm_g[j], ab_w_out[j])
        else:
            h = h + cd_layer(hn, cd_w_in[j], s5_lam_re[j], s5_lam_im[j], s5_log_dt[j], s5_b_re[j],
                             s5_b_im[j], s5_c_re[j], s5_c_im[j], s5_d[j], s5_w_glu[j], s5_b_glu[j],
                             lower_bounds[l], hgrn_norm_g[j], cd_w_out[j])
    return rmsnorm(h, final_g).astype(x.dtype)
```

```python
from contextlib import ExitStack
import concourse.bass as bass
import concourse.mybir as mybir

F32 = mybir.dt.float32
BF16 = mybir.dt.bfloat16
I32 = mybir.dt.int32
AF = mybir.ActivationFunctionType
ALU = mybir.AluOpType

ENGS = ("pe", "act", "dve", "pool", "sp")


class T:
    __slots__ = ("t", "name", "lw", "rd")

    def __init__(self, t, name):
        self.t = t
        self.name = name
        self.lw = None
        self.rd = []

    def __getitem__(self, idx):
        return self.t[idx]


class B:
    def __init__(self, nc, ndma=6):
        self.nc = nc
        self.es = ExitStack()
        self.eng = {"pe": nc.tensor, "act": nc.scalar, "dve": nc.vector,
                    "pool": nc.gpsimd, "sp": nc.sync}
        self.ops = {e: [] for e in ENGS}
        self.sem = {}
        self.cnt = {}
        for e in ENGS:
            self.sem[e] = self.es.enter_context(nc.semaphore("s_" + e))
            self.cnt[e] = 0
        self.ndma = ndma
        self.dq = {}
        for q in ("sp", "act", "pool"):
            sems = [self.es.enter_context(nc.semaphore("d_%s%d" % (q, i))) for i in range(ndma)]
            for i, s in enumerate(sems):
                self.sem[("d", q, i)] = s
            self.dq[q] = 0
        self.seen = {e: {} for e in ENGS}
        self.final_waits = []

    def sb(self, name, shape, dt):
        return T(self.es.enter_context(self.nc.sbuf_tensor(name, list(shape), dt)), name)

    def ps(self, name, shape, dt=F32):
        return T(self.es.enter_context(self.nc.psum_tensor(name, list(shape), dt)), name)

    def view(self, t, name=None):
        return T(t.t if isinstance(t, T) else t, name or "v")

    def _waits(self, e, reads, writes):
        w = {}

        def add(dep):
            k, v, de = dep
            if k not in w or w[k] < v:
                w[k] = v

        for t in reads:
            if t.lw is not None:
                add(t.lw)
        for t in writes:
            if t.lw is not None:
                if not (t.lw[2] == e and e == "pe"):
                    add(t.lw)
            for r in t.rd:
                if r[2] == e and not isinstance(r[0], tuple):
                    continue
                add(r)
        out = []
        seen = self.seen[e]
        for k, v in w.items():
            if seen.get(k, 0) >= v:
                continue
            seen[k] = v
            out.append((k, v))
        return out

    def op(self, e, fn, reads=(), writes=(), inc=True):
        waits = self._waits(e, reads, writes)
        if inc:
            self.cnt[e] += 1
            me = (e, self.cnt[e], e)
        else:
            me = None
        self.ops[e].append((waits, fn, inc))
        if inc:
            for t in reads:
                t.rd.append(me)
            for t in writes:
                t.lw = me
                t.rd = []
        return me

    def mm(self, out_t, items, reads, start=True, stop=True, **kw):
        n = len(items)
        for i, (o, l, r) in enumerate(items):
            st = start and i == 0
            sp = stop and i == n - 1

            def fn(eng, o=o, l=l, r=r, st=st, sp=sp):
                return eng.matmul(o, l, r, start=st, stop=sp, **kw)
            if i == n - 1:
                self.op("pe", fn, reads=reads if n == 1 else (), writes=[out_t])
            else:
                self.op("pe", fn, reads=reads if i == 0 else (), writes=[out_t] if i == 0 else (), inc=False)
        if n > 1:
            me = ("pe", self.cnt["pe"], "pe")
            for t in reads:
                t.rd.append(me)

    def dma(self, q, out_ap, in_ap, reads=(), writes=(), **kw):
        j = self.dq[q]
        self.dq[q] += 1
        slot = j % self.ndma
        key = ("d", q, slot)
        val = 16 * (j // self.ndma + 1)
        waits = self._waits(q, reads, writes)
        prev = 16 * (j // self.ndma)
        if prev > 0 and self.seen[q].get(key, 0) < prev:
            self.seen[q][key] = prev
            waits.append((key, prev))
        sem = self.sem[key]

        def fn(eng, o=out_ap, i=in_ap):
            return eng.dma_start(out=o, in_=i, **kw).then_inc(sem, 16)
        self.ops[q].append((waits, fn, None))
        me = (key, val, "dma_" + q)
        for t in reads:
            t.rd.append(me)
        for t in writes:
            t.lw = me
            t.rd = []
        return me

    def wait_all_outputs(self, e, deps):
        waits = []
        for d in deps:
            waits.append((d[0], d[1]))
        self.ops[e].append((waits, None, None))

    def barrier(self):
        tgt = []
        for e in ENGS:
            if self.cnt[e] > 0:
                tgt.append((e, self.cnt[e]))
        for q in self.dq:
            j = self.dq[q]
            for s_ in range(self.ndma):
                n = (j - s_ + self.ndma - 1) // self.ndma if j > s_ else 0
                if n > 0:
                    tgt.append((("d", q, s_), 16 * n))
        for e in ENGS:
            waits = []
            for k, v in tgt:
                if k == e:
                    continue
                if self.seen[e].get(k, 0) < v:
                    self.seen[e][k] = v
                    waits.append((k, v))
            if waits:
                self.ops[e].append((waits, None, None))

    def carve(self, arena, off, shape, dt, name="c"):
        n = 1
        for s_ in shape[1:]:
            n *= s_
        nbytes = n * (2 if dt == BF16 else 4)
        ncol = (nbytes + 3) // 4
        ap = arena.t[:, off:off + ncol]
        if dt != F32:
            ap = ap.bitcast(dt)
        if len(shape) == 3:
            ap = ap.rearrange("p (a b) -> p a b", b=shape[2])
        if shape[0] != 128:
            ap = ap[0:shape[0]]
        return T(ap, name), off + ncol

    def emit(self):
        nc = self.nc
        with nc.Block() as block:
            for e in ENGS:
                lst = self.ops[e]
                if not lst:
                    continue
                dec = {"pe": block.tensor, "act": block.scalar, "dve": block.vector,
                       "pool": block.gpsimd, "sp": block.sync}[e]
                sem_e = self.sem[e]

                def body(eng, lst=lst, sem_e=sem_e):
                    for waits, fn, inc in lst:
                        for k, v in waits:
                            eng.wait_ge(self.sem[k], v)
                        if fn is None:
                            continue
                        ins = fn(eng)
                        if inc is True:
                            ins.then_inc(sem_e, 1)
                dec(body)
        self.es.close()


def _act(b, out, in_, func, reads, writes, **kw):
    return b.op("act", lambda e: e.activation(out, in_, func, **kw), reads, writes)


def _tt(b, e, out, in0, in1, op, reads, writes):
    return b.op(e, lambda g: g.tensor_tensor(out, in0, in1, op), reads, writes)


def _ts(b, e, out, in0, s1, s2, op0, op1, reads, writes):
    if op1 is None:
        return b.op(e, lambda g: g.tensor_scalar(out, in0, s1, None, op0), reads, writes)
    return b.op(e, lambda g: g.tensor_scalar(out, in0, s1, s2, op0, op1), reads, writes)


def _stt(b, out, in0, scalar, in1, op0, op1, reads, writes):
    return b.op("dve", lambda g: g.scalar_tensor_tensor(out, in0, scalar, in1, op0, op1), reads, writes)


def _copy(b, e, out, in_, reads, writes):
    if e == "act":
        return b.op(e, lambda g: g.copy(out, in_), reads, writes)
    return b.op(e, lambda g: g.tensor_copy(out, in_), reads, writes)


def _recip(b, out, in_, reads, writes):
    return b.op("dve", lambda g: g.reciprocal(out, in_), reads, writes)


def _memset(b, e, ap, val, writes):
    return b.op(e, lambda g: g.memset(ap, val), (), writes)


B.act = _act
B.tt = _tt
B.ts = _ts
B.stt = _stt
B.copy = _copy
B.recip = _recip
B.memset = _memset

import numpy as np
import ml_dtypes
NPBF = ml_dtypes.bfloat16
S = 2048
D = 2048
EPS = 1e-6


def host_consts():
    j = np.arange(128)[:, None]
    i = np.arange(128)[None, :]
    cur = (j <= i).astype(np.float32)
    prev = (j >= i).astype(np.float32)
    mA = np.concatenate([prev, cur, prev, cur], 1)
    mB = np.concatenate([cur, prev, cur, prev], 1)
    mC = np.concatenate([cur] * 4, 1)
    mD = np.concatenate([prev] * 4, 1)
    mE = [np.concatenate([cur[:, 32 * b:32 * b + 32]] * 16, 1) for b in range(4)]
    masks = np.stack([mA, mB, mC, mD] + mE, 1)
    same = (j // 64) == (i // 64)
    tri = ((j <= i) & same).astype(np.float32)
    mid = np.where(same, ((j % 64) <= 31), False).astype(np.float32)
    TRI = np.zeros((128, 132), np.float32)
    TRI[:, :128] = tri - mid
    jj = np.arange(128)
    TRI[:, 128] = ((jj < 64) & (jj % 64 <= 31))
    TRI[:, 129] = ((jj >= 64) & (jj % 64 <= 31))
    TRI[:, 130] = (jj < 64)
    TRI[:, 131] = (jj >= 64)
    UBD = ((j > i) & same).astype(np.float32)
    mBD = ((j <= i) & same).astype(np.float32)
    return {
        "c_ones": np.ones((128, 128), NPBF),
        "c_ident": np.eye(128, dtype=np.float32).astype(NPBF),
        "c_masks": masks.astype(NPBF),
        "c_tri": TRI,
        "c_ubd": UBD,
        "c_mbd": mBD.astype(NPBF),
        "c_onesf": np.ones((128, 128), np.float32),
    }


class Ctx:
    pass


def load_consts(b, nc, names):
    hc = host_consts()
    out = {}
    for n in names:
        a = hc[n]
        dt = BF16 if a.dtype == NPBF else F32
        d = nc.dram_tensor(n, list(a.shape), dt, kind="ExternalInput").ap()
        t = b.sb("sb_" + n, a.shape, dt)
        if a.ndim == 2:
            b.dma("sp", t[:, :], d[:, :], writes=[t])
        else:
            b.dma("sp", t[:, :, :], d[:, :, :], writes=[t])
        out[n] = t
    return out


def rmsnorm_T(b, xT, g_sb, hnT, nchunk, ntok, banks, xs_ring, sq_ring, rstd, ones_bf, eps_t, t0=0, h0=0):
    nb = ntok // 512
    for c in range(nchunk):
        xs = xs_ring[c % len(xs_ring)]
        b.dma("sp", xs[:, :ntok], xT[c * 128:(c + 1) * 128, t0:t0 + ntok], writes=[xs])
        sq = sq_ring[c % len(sq_ring)]
        b.act(sq[:, :ntok], xs[:, :ntok], AF.Square, [xs], [sq])
        for n in range(nb):
            b.mm(banks[n], [(banks[n][:, :], ones_bf[:, :], sq[:, n * 512:(n + 1) * 512])],
                 reads=[sq, ones_bf], start=(c == 0), stop=(c == nchunk - 1))
    for n in range(nb):
        b.act(rstd[:, n * 512:(n + 1) * 512], banks[n][:, :], AF.Sqrt, [banks[n], eps_t], [rstd],
              bias=eps_t[:, 0:1], scale=1.0 / (nchunk * 128))
    b.recip(rstd[:, :ntok], rstd[:, :ntok], [rstd], [rstd])
    for c in range(nchunk):
        xs = xs_ring[c % len(xs_ring)]
        b.dma("sp", xs[:, :ntok], xT[c * 128:(c + 1) * 128, t0:t0 + ntok], writes=[xs])
        b.stt(hnT[:, c, h0:h0 + ntok], xs[:, :ntok], g_sb[:, c:c + 1], rstd[:, :ntok], ALU.mult, ALU.mult,
              [xs, g_sb, rstd], [hnT])


class WStream:
    def __init__(self, b, nchunk, nst=2, nbf=3, name="w"):
        self.b = b
        self.nchunk = nchunk
        self.st = [b.sb("%s_st%d" % (name, i), [128, nchunk, 128], F32) for i in range(nst)]
        self.bf = [b.sb("%s_bf%d" % (name, i), [128, nchunk, 128], BF16) for i in range(nbf)]
        self.i = 0

    def load(self, w_dram, col0, ncols=128, q="sp", cast_eng="pool"):
        b = self.b
        st = self.st[self.i % len(self.st)]
        bf = self.bf[self.i % len(self.bf)]
        self.i += 1
        src = w_dram[:, col0:col0 + ncols].rearrange("(c p) n -> p c n", p=128)
        b.dma(q, st[:, :, :ncols], src, writes=[st])
        b.copy(cast_eng, bf[:, :, :ncols], st[:, :, :ncols], [st], [bf])
        return bf


def proj_fm(b, wbf, mcols, hnT, nchunk, ntok, banks, evac):
    nb = ntok // 512
    for n in range(nb):
        bank = banks[n % len(banks)]
        items = [(bank[:mcols, :], wbf[:, c, :mcols], hnT[:, c, n * 512:(n + 1) * 512]) for c in range(nchunk)]
        b.mm(bank, items, reads=[wbf, hnT])
        evac(n, bank)


NCOL_L0 = 4 * 512 + 2 * 512 + 2 * 384 + 16


def l0_weight_cols(hh):
    cols = []
    GATE = 5136
    for i in range(4):
        hg = 4 * hh + i
        for base in (0, 1024, 2048, GATE):
            cols += list(range(base + hg * 128, base + hg * 128 + 128))
    for i in range(2):
        hg = 2 * hh + i
        cols += list(range(3072 + hg * 128, 3072 + hg * 128 + 128))
        cols += list(range(3584 + hg * 128, 3584 + hg * 128 + 128))
        cols += list(range(GATE + 1024 + hg * 256, GATE + 1024 + hg * 256 + 256))
    for i in range(2):
        hg = 2 * hh + i
        cols += list(range(3584 + hg * 128, 3584 + hg * 128 + 128))
        cols += list(range(4096 + hg * 256, 4096 + hg * 256 + 256))
    cols += list(range(5120, 5136))
    assert len(cols) == NCOL_L0
    return np.array(cols)


def attention_head(b, cx, qT, kT, vT, wg, yout_cb):
    ident, ones_bf, masks = cx.ident, cx.ones_bf, cx.masks
    scale = 128.0 ** -0.5
    Vd = cx.Vd
    for pi, d in enumerate((1, 4, 16)):
        for half in range(2):
            pt = cx.pT8
            for k8 in range(8):
                tix = half * 8 + k8
                if d == 1:
                    src = vT[:, tix * 128:(tix + 1) * 128]
                elif d == 4:
                    r, n_ = tix // 4, tix % 4
                    st = n_ * 512 + r
                    src = vT[:, st:st + 509:4]
                else:
                    r = tix
                    src = vT[:, r:r + 2033:16]
                b.op("pe", lambda e, o=pt[:, k8 * 128:(k8 + 1) * 128], s=src: e.transpose(o, s, ident[:, :]),
                     reads=[vT, ident] if k8 == 0 else (), writes=[pt] if k8 == 0 else (), inc=(k8 == 7))
                if k8 == 7:
                    me = ("pe", b.cnt["pe"], "pe")
                    pt.lw = me
                    pt.rd = []
                    vT.rd.append(me)
            b.copy("dve", Vd[pi][:, half * 8:(half + 1) * 8, :], pt[:, :].rearrange("p (a e) -> p a e", e=128),
                   [pt], [Vd[pi]])
    for bk in range(4):
        O, Dn = cx.pO, cx.pD
        first = [True]
        sbanks = []
        A = []
        if bk > 0:
            kb = 4 * bk - 1
            A.append((kT[:, kb * 128:(kb + 1) * 128], qT[:, bk * 512:bk * 512 + 128], 128, Vd[0][:, kb, :], (0, 128, 1), 0))
        else:
            A.append(None)
        kb = 4 * bk
        A.append((kT[:, kb * 128:(kb + 1) * 128], qT[:, bk * 512:bk * 512 + 256], 256, Vd[0][:, kb, :], (0, 256, 1), 128))
        kb = 4 * bk + 3
        A.append((kT[:, kb * 128:(kb + 1) * 128], qT[:, bk * 512 + 384:bk * 512 + 512], 128, Vd[0][:, kb, :], (384, 128, 1), 384))
        sbanks.append((0, A))
        Bq = []
        for q_ in (1, 2):
            kb = 4 * bk + q_
            Bq.append((kT[:, kb * 128:(kb + 1) * 128], qT[:, bk * 512 + q_ * 128:bk * 512 + q_ * 128 + 256], 256,
                       Vd[0][:, kb, :], (q_ * 128, 256, 1), (q_ - 1) * 256))
        sbanks.append((1, Bq))
        C = []
        Dl = []
        for r in range(4):
            st = bk * 512 + r
            qa = qT[:, st:st + 509:4]
            C.append((kT[:, st:st + 509:4], qa, 128, Vd[1][:, r * 4 + bk, :], (r, 128, 4), r * 128))
            if bk > 0:
                sp_ = (bk - 1) * 512 + r
                Dl.append((kT[:, sp_:sp_ + 509:4], qa, 128, Vd[1][:, r * 4 + bk - 1, :], (r, 128, 4), r * 128))
        sbanks.append((2, C))
        if bk > 0:
            sbanks.append((3, Dl))
        E = []
        for r in range(16):
            st = bk * 512 + r
            E.append((kT[:, r:r + 2033:16], qT[:, st:st + 497:16], 32, Vd[2][:, r, :], (r, 32, 16), r * 32))
        sbanks.append((4 + bk, E))
        nsb = len(sbanks)
        for si, (mi, contribs) in enumerate(sbanks):
            sc = cx.pS[cx.sidx % len(cx.pS)]
            pb = cx.pbuf[cx.sidx % len(cx.pbuf)]
            cx.sidx += 1
            live = [c for c in contribs if c is not None]
            lo = min(c[5] for c in live)
            hi = max(c[5] + c[2] for c in live)
            for ci, c in enumerate(live):
                kap, qap, ncol, vap, ocs, soff = c
                lastc = (ci == len(live) - 1)
                b.op("pe", lambda e, o=sc[:, soff:soff + ncol], l=kap, r=qap: e.matmul(o, l, r, start=True, stop=True),
                     reads=[kT, qT] if ci == 0 else (), writes=[sc] if ci == 0 else (), inc=lastc)
                if lastc:
                    me = ("pe", b.cnt["pe"], "pe")
                    sc.lw = me
                    sc.rd = []
                    kT.rd.append(me)
                    qT.rd.append(me)
            b.act(pb[:, lo:hi], sc[:, lo:hi], AF.Exp, [sc], [pb], scale=scale)
            b.tt("pool", pb[:, lo:hi], pb[:, lo:hi], masks[:, mi, lo:hi], ALU.mult, [pb, masks], [pb])
            for ci, c in enumerate(live):
                kap, qap, ncol, vap, ocs, soff = c
                o0, on, ostep = ocs
                osl = slice(o0, o0 + (on - 1) * ostep + 1, ostep)
                lastc = (si == nsb - 1) and (ci == len(live) - 1)
                st_flag = first[0]
                first[0] = False
                b.op("pe", lambda e, o=O[:, osl], l=vap, r=pb[:, soff:soff + ncol], sf=st_flag, lc=lastc:
                     e.matmul(o, l, r, start=sf, stop=lc, skip_group_check=True),
                     reads=[pb, cx.Vd[0], cx.Vd[1], cx.Vd[2]] if ci == 0 else (), writes=[O] if ci == 0 else (), inc=False)
                last_in_bank = (ci == len(live) - 1)
                b.op("pe", lambda e, o=Dn[:, osl], l=ones_bf[:, :], r=pb[:, soff:soff + ncol], sf=st_flag, lc=lastc:
                     e.matmul(o, l, r, start=sf, stop=lc, skip_group_check=True),
                     reads=(), writes=[Dn] if ci == 0 else (), inc=last_in_bank)
                if last_in_bank:
                    me = ("pe", b.cnt["pe"], "pe")
                    O.lw = me
                    Dn.lw = me
                    O.rd = []
                    Dn.rd = []
                    pb.rd.append(me)
                    for v in cx.Vd:
                        v.rd.append(me)
        rec = cx.rec
        yt = cx.ybuf[cx.yidx % len(cx.ybuf)]
        cx.yidx += 1
        b.recip(rec[:, :], Dn[:, :], [Dn], [rec])
        b.tt("dve", rec[:, :], O[:, :], rec[:, :], ALU.mult, [O, rec], [rec])
        pg = cx.pP[bk % 2]
        b.mm(pg, [(pg[:, :], wg[:, c, :], cx.hnT[:, c, bk * 512:(bk + 1) * 512]) for c in range(16)], reads=[wg, cx.hnT])
        gs = cx.gs
        b.act(gs[:, :], pg[:, :], AF.Silu, [pg], [gs])
        b.tt("dve", yt[:, :], rec[:, :], gs[:, :], ALU.mult, [rec, gs], [yt])
        yout_cb(bk, yt)


def gla_head(b, cx, qTr, kTr, ktok, vtok, logat, dv, fin_cb, nt=16):
    nv = dv // 128
    Sst = cx.Sst
    b.memset("dve", Sst[:, :dv], 0.0, [Sst])
    for t in range(nt):
        pB, pBD, pA = cx.pG[0], cx.pG[1], cx.pG[2]
        b.mm(pB, [(pB[:, :132], logat[:, t, :], cx.tri[:, :])], reads=[logat, cx.tri])
        b.mm(pBD, [(pBD[:, :128], cx.ubd[:, :], logat[:, t, :])], reads=[logat, cx.ubd])
        E1, enb, edec = cx.E1, cx.enb, cx.edec
        b.act(E1[:, :132], pB[:, :132], AF.Exp, [pB], [E1])
        b.act(enb[:, :], pB[:, :128], AF.Exp, [pB], [enb], scale=-1.0)
        b.act(edec[:, :], pBD[:, :128], AF.Exp, [pBD], [edec])
        qs, ks, kd = cx.qs, cx.ks, cx.kd
        b.tt("dve", qs[:, :], qTr[:, t * 128:(t + 1) * 128], E1[:, :128], ALU.mult, [qTr, E1], [qs])
        b.tt("dve", ks[:, :], kTr[:, t * 128:(t + 1) * 128], enb[:, :], ALU.mult, [kTr, enb], [ks])
        b.tt("pool", kd[:, :], ktok[:, t, :], edec[:, :], ALU.mult, [ktok, edec], [kd])
        b.mm(pA, [(pA[:, :128], ks[:, :], qs[:, :])], reads=[ks, qs])
        AT = cx.AT
        b.tt("dve", AT[:, :], pA[:, :128], cx.mbd[:, :], ALU.mult, [pA, cx.mbd], [AT])
        pOo = cx.pGo
        for cc in range(2):
            r0 = 64 * cc
            Smid = cx.Smid
            b.act(Smid[:, :dv], Sst[:, :dv], AF.Copy, [Sst, E1], [Smid], scale=E1[:, 128 + cc:129 + cc])
            for vt in range(nv):
                col = (t % 4) * 128 + r0
                items = [
                    (pOo[vt][:, col:col + 64], vtok[r0:r0 + 64, t, vt * 128:(vt + 1) * 128], AT[r0:r0 + 64, r0:r0 + 64]),
                    (pOo[vt][:, col:col + 64], Smid[:, vt * 128:(vt + 1) * 128], qs[:, r0:r0 + 64]),
                ]
                b.mm(pOo[vt], items, reads=[vtok, AT, Smid, qs], skip_group_check=True)
            pS_ = cx.pGs
            b.mm(pS_, [(pS_[:, :dv], kd[r0:r0 + 64, :], vtok[r0:r0 + 64, t, :dv])], reads=[kd, vtok])
            b.stt(Sst[:, :dv], Sst[:, :dv], E1[:, 130 + cc:131 + cc], pS_[:, :dv], ALU.mult, ALU.add,
                  [Sst, E1, pS_], [Sst])
        if t % 4 == 3:
            fin_cb(t // 4)


def build_l1():
    nc = bass.Bass("TRN2", target_bir_lowering=False)
    b = B(nc)
    xT = nc.dram_tensor("xT", [D, S], F32, kind="ExternalInput").ap()
    wc = nc.dram_tensor("wc", [D, NCOL_L0], F32, kind="ExternalInput").ap()
    ng = nc.dram_tensor("ng", [128, 16], F32, kind="ExternalInput").ap()
    wgate = nc.dram_tensor("wgate", [16, 256], F32, kind="ExternalInput").ap()
    bgate = nc.dram_tensor("bgate", [1, 256], F32, kind="ExternalInput").ap()
    gng = nc.dram_tensor("gng", [128, 4], F32, kind="ExternalInput").ap()
    yT = nc.dram_tensor("yT", [1024, S], BF16, kind="ExternalOutput").ap()
    cs = load_consts(b, nc, ["c_ones", "c_ident", "c_masks", "c_tri", "c_ubd", "c_mbd", "c_onesf"])
    cx = Ctx()
    cx.ones_bf, cx.ident, cx.masks = cs["c_ones"], cs["c_ident"], cs["c_masks"]
    cx.tri, cx.ubd, cx.mbd, onesf = cs["c_tri"], cs["c_ubd"], cs["c_mbd"], cs["c_onesf"]
    g_sb = b.sb("g_sb", [128, 16], F32)
    b.dma("sp", g_sb[:, :], ng[:, :], writes=[g_sb])
    wg_sb = b.sb("wg_sb", [16, 256], F32)
    b.dma("sp", wg_sb[:, :], wgate[:, :], writes=[wg_sb])
    bg_sb = b.sb("bg_sb", [1, 256], F32)
    b.dma("sp", bg_sb[:, :], bgate[:, :], writes=[bg_sb])
    gn_sb = b.sb("gn_sb", [128, 4], F32)
    b.dma("sp", gn_sb[:, :], gng[:, :], writes=[gn_sb])
    eps_t = b.sb("eps_t", [128, 1], F32)
    b.memset("dve", eps_t[:, :], EPS, [eps_t])
    one_t = b.sb("one_t", [128, 1], F32)
    b.memset("dve", one_t[:, :], 1.0, [one_t])
    P = [b.ps("P%d" % i, [128, 512]) for i in range(7)]
    P7 = b.ps("P7", [128, 1024], BF16)
    cx.pP = [P[0], P[1]]
    cx.pS = [P[2], P[3]]
    cx.pO, cx.pD = P[5], P[6]
    cx.pT8 = P7
    cx.pG = [P[2], P[3], P[4]]
    cx.pGo = [P[5], P[6]]
    cx.pGs = P[1]
    hnT = b.sb("hnT", [128, 16, S], BF16)
    cx.hnT = hnT
    ws = WStream(b, 16)
    arena = b.sb("arena", [128, 18432], F32)
    off = 0
    xs_ring = []
    for i in range(3):
        t, off = b.carve(arena, off, [128, 1024], F32, "xs%d" % i)
        xs_ring.append(t)
    sq_ring = []
    for i in range(2):
        t, off = b.carve(arena, off, [128, 1024], BF16, "sq%d" % i)
        sq_ring.append(t)
    rstd, off = b.carve(arena, off, [128, 1024], F32, "rstd")
    for seg in range(2):
        rmsnorm_T(b, xT, g_sb, hnT, 16, 1024, P[0:2], xs_ring, sq_ring, rstd, cx.ones_bf, eps_t,
                  t0=seg * 1024, h0=seg * 1024)
    b.barrier()
    off = 0
    qT, off = b.carve(arena, off, [128, S], BF16, "qT")
    kT, off = b.carve(arena, off, [128, S], BF16, "kT")
    vT, off = b.carve(arena, off, [128, S], BF16, "vT")
    cx.Vd = []
    for i in range(3):
        t, off = b.carve(arena, off, [128, 16, 128], BF16, "Vd%d" % i)
        cx.Vd.append(t)
    cx.pbuf = []
    for i in range(3):
        t, off = b.carve(arena, off, [128, 512], BF16, "pb%d" % i)
        cx.pbuf.append(t)
    cx.rec, off = b.carve(arena, off, [128, 512], F32, "rec")
    cx.gs, off = b.carve(arena, off, [128, 512], F32, "gs")
    cx.ybuf = []
    for i in range(2):
        t, off = b.carve(arena, off, [128, 512], BF16, "yb%d" % i)
        cx.ybuf.append(t)
    cx.sidx = 0
    cx.yidx = 0
    outs = []
    for i in range(4):
        for gi, dst in enumerate((qT, kT, vT)):
            w = ws.load(wc, i * 512 + gi * 128)

            def ev(n, bank, dst=dst):
                b.copy("act", dst[:, n * 512:(n + 1) * 512], bank[:, :], [bank], [dst])
            proj_fm(b, w, 128, hnT, 16, S, cx.pP, ev)
        wg = ws.load(wc, i * 512 + 384)

        def yout(bk, yt, i=i):
            outs.append(b.dma("sp", yT[i * 128:(i + 1) * 128, bk * 512:(bk + 1) * 512], yt[:, :], reads=[yt]))
        attention_head(b, cx, qT, kT, vT, wg, yout)
    b.barrier()
    off = 0
    qTr, off = b.carve(arena, off, [128, S], F32, "qTr")
    kTr, off = b.carve(arena, off, [128, S], F32, "kTr")
    ktok, off = b.carve(arena, off, [128, 16, 128], F32, "ktok")
    vtok, off = b.carve(arena, off, [128, 16, 256], BF16, "vtok")
    logat, off = b.carve(arena, off, [128, 16, 128], F32, "logat")
    glT, off = b.carve(arena, off, [16, S], F32, "glT")
    cx.Sst, off = b.carve(arena, off, [128, 256], F32, "Sst")
    cx.Smid, off = b.carve(arena, off, [128, 256], BF16, "Smid")
    cx.E1, off = b.carve(arena, off, [128, 132], F32, "E1")
    cx.enb, off = b.carve(arena, off, [128, 128], F32, "enb")
    cx.edec, off = b.carve(arena, off, [128, 128], F32, "edec")
    cx.qs, off = b.carve(arena, off, [128, 128], BF16, "qs")
    cx.ks, off = b.carve(arena, off, [128, 128], BF16, "ks")
    cx.kd, off = b.carve(arena, off, [128, 128], BF16, "kd")
    cx.AT, off = b.carve(arena, off, [128, 128], BF16, "AT")
    obS, off = b.carve(arena, off, [128, 2, 512], F32, "obS")
    sqS, off = b.carve(arena, off, [128, 2, 512], BF16, "sqS")
    rs2, off = b.carve(arena, off, [128, 512], F32, "rs2")
    gs2, off = b.carve(arena, off, [128, 512], F32, "gs2")
    tmpE, off = b.carve(arena, off, [128, 512], F32, "tmpE")
    ybs = []
    for i in range(2):
        t, off = b.carve(arena, off, [128, 512], BF16, "ybs%d" % i)
        ybs.append(t)
    assert off <= 18432, off
    wl = ws.load(wc, 4 * 512 + 2 * 512 + 2 * 384, ncols=16)

    def ev_gl(n, bank):
        b.copy("act", glT[:, n * 512:(n + 1) * 512], bank[:16, :], [bank], [glT])
    proj_fm(b, wl, 16, hnT, 16, S, cx.pP, ev_gl)
    yi = [0]
    for i in range(2):
        base = 4 * 512 + i * 512
        w = ws.load(wc, base)

        def ev_q(n, bank):
            b.act(qTr[:, n * 512:(n + 1) * 512], bank[:, :], AF.Copy, [bank], [qTr], scale=128.0 ** -0.5)
        proj_fm(b, w, 128, hnT, 16, S, cx.pP, ev_q)
        w = ws.load(wc, base + 128)

        def ev_k(n, bank):
            b.copy("act", kTr[:, n * 512:(n + 1) * 512], bank[:, :], [bank], [kTr])
        proj_fm(b, w, 128, hnT, 16, S, cx.pP, ev_k)
        tb = 4 * 512 + 2 * 512 + i * 384
        for gi in range(3):
            w = ws.load(wc, tb + gi * 128)
            for n in range(4):
                bank = cx.pP[n % 2]
                for tt_ in range(4):
                    t = n * 4 + tt_
                    items = [(bank[:, tt_ * 128:(tt_ + 1) * 128], hnT[:, c, t * 128:(t + 1) * 128], w[:, c, :]) for c in range(16)]
                    b.mm(bank, items, reads=[w, hnT], skip_group_check=True)
                src = bank[:, :].rearrange("p (a e) -> p a e", e=128)
                if gi == 0:
                    b.copy("act", ktok[:, n * 4:(n + 1) * 4, :], src, [bank], [ktok])
                else:
                    b.copy("act", vtok[:, n * 4:(n + 1) * 4, (gi - 1) * 128:gi * 128], src, [bank], [vtok])
        for n in range(4):
            bank = cx.pP[n % 2]
            for tt_ in range(4):
                t = n * 4 + tt_
                items = [(bank[:, tt_ * 128:(tt_ + 1) * 128], glT[0:16, t * 128:(t + 1) * 128], wg_sb[0:16, i * 128:(i + 1) * 128]),
                         (bank[:, tt_ * 128:(tt_ + 1) * 128], onesf[0:1, 0:128], bg_sb[0:1, i * 128:(i + 1) * 128])]
                b.mm(bank, items, reads=[glT, wg_sb, onesf, bg_sb], skip_group_check=True)
            b.act(tmpE[:, :], bank[:, :], AF.Exp, [bank], [tmpE], scale=-1.0)
            b.act(tmpE[:, :], tmpE[:, :], AF.Ln, [tmpE, one_t], [tmpE], bias=one_t[:, 0:1])
            b.ts("dve", logat[:, n * 4:(n + 1) * 4, :], tmpE[:, :].rearrange("p (a e) -> p a e", e=128), -1.0 / 16.0, None,
                 ALU.mult, None, [tmpE], [logat])
        wga = ws.load(wc, base + 256)
        wgb = ws.load(wc, base + 384)
        wgs = [wga, wgb]

        def fin(n, i=i, wgs=wgs):
            for vt in range(2):
                b.copy("act", obS[:, vt, :], cx.pGo[vt][:, :], [cx.pGo[vt]], [obS])
            b.act(sqS[:, :, :], obS[:, :, :], AF.Square, [obS], [sqS])
            pq = cx.pG[2]
            b.mm(pq, [(pq[:, :], cx.ones_bf[:, :], sqS[:, vt, :]) for vt in range(2)], reads=[sqS, cx.ones_bf])
            b.act(rs2[:, :], pq[:, :], AF.Sqrt, [pq, eps_t], [rs2], bias=eps_t[:, 0:1], scale=1.0 / 256.0)
            b.recip(rs2[:, :], rs2[:, :], [rs2], [rs2])
            for vt in range(2):
                pg = cx.pP[vt]
                b.mm(pg, [(pg[:, :], wgs[vt][:, c, :], hnT[:, c, n * 512:(n + 1) * 512]) for c in range(16)], reads=[wgs[vt], hnT])
                b.act(gs2[:, :], pg[:, :], AF.Silu, [pg], [gs2])
                b.tt("dve", obS[:, vt, :], obS[:, vt, :], rs2[:, :], ALU.mult, [obS, rs2], [obS])
                yb_ = ybs[yi[0] % 2]
                yi[0] += 1
                b.stt(yb_[:, :], obS[:, vt, :], gn_sb[:, i * 2 + vt:i * 2 + vt + 1], gs2[:, :], ALU.mult, ALU.mult,
                      [obS, gn_sb, gs2], [yb_])
                r0 = 512 + i * 256 + vt * 128
                outs.append(b.dma("sp", yT[r0:r0 + 128, n * 512:(n + 1) * 512], yb_[:, :], reads=[yb_]))
        gla_head(b, cx, qTr, kTr, ktok, vtok, logat, 256, fin)
    b.wait_all_outputs("sp", outs)
    b.emit()
    return nc


def l1_inputs(inp, bidx, hh):
    x = np.asarray(inp["x"])[bidx]
    cols = l0_weight_cols(hh)
    d = {
        "xT": np.ascontiguousarray(x.T),
        "wc": np.ascontiguousarray(np.asarray(inp["ab_w_in"])[0][:, cols]),
        "ng": np.ascontiguousarray(np.asarray(inp["norm_g"])[0].reshape(16, 128).T),
        "wgate": np.ascontiguousarray(np.asarray(inp["gla_w_gate"])[0][:, hh * 256:(hh + 1) * 256]),
        "bgate": np.ascontiguousarray(np.asarray(inp["gla_b_gate"])[0][hh * 256:(hh + 1) * 256].reshape(1, 256)),
        "gng": np.ascontiguousarray(np.asarray(inp["gla_norm_g"])[0][hh * 512:(hh + 1) * 512].reshape(4, 128).T),
    }
    hc = host_consts()
    for k in ["c_ones", "c_ident", "c_masks", "c_tri", "c_ubd", "c_mbd", "c_onesf"]:
        d[k] = hc[k]
    return d


NT2 = 1024


def build_outproj(final, glu):
    nc = bass.Bass("TRN2", target_bir_lowering=False)
    b = B(nc)
    yTd = nc.dram_tensor("yT", [D, NT2], BF16, kind="ExternalInput").ap()
    resT = nc.dram_tensor("resT", [D, NT2], F32, kind="ExternalInput").ap()
    wo = nc.dram_tensor("wo", [D, D], F32, kind="ExternalInput").ap()
    ng = nc.dram_tensor("ng", [128, 16], F32, kind="ExternalInput").ap()
    if glu:
        sgTd = nc.dram_tensor("sgT", [1024, NT2], BF16, kind="ExternalInput").ap()
        wglu = nc.dram_tensor("wglu", [1024, 1024], F32, kind="ExternalInput").ap()
        bglu = nc.dram_tensor("bglu", [128, 8], F32, kind="ExternalInput").ap()
    if final:
        outT = nc.dram_tensor("outT", [D, NT2], F32, kind="ExternalOutput").ap()
    else:
        hTd = nc.dram_tensor("hT", [D, NT2], F32, kind="ExternalOutput").ap()
        hnTd = nc.dram_tensor("hnT", [D, NT2], BF16, kind="ExternalOutput").ap()
    cs = load_consts(b, nc, ["c_ones"])
    ones_bf = cs["c_ones"]
    g_sb = b.sb("g_sb", [128, 16], F32)
    b.dma("sp", g_sb[:, :], ng[:, :], writes=[g_sb])
    eps_t = b.sb("eps_t", [128, 1], F32)
    b.memset("dve", eps_t[:, :], EPS, [eps_t])
    P = [b.ps("P%d" % i, [128, 512]) for i in range(8)]
    ysb = b.sb("ysb", [128, 16, NT2], BF16)
    for c in range(16):
        b.dma("sp", ysb[:, c, :], yTd[c * 128:(c + 1) * 128, :], writes=[ysb])
    hT = b.sb("hT_sb", [128, 16, NT2], F32)
    ws = WStream(b, 16)
    outs = []
    if glu:
        bg_sb = b.sb("bg_sb", [128, 8], F32)
        b.dma("sp", bg_sb[:, :], bglu[:, :], writes=[bg_sb])
        ycs = b.sb("ycs", [128, 8, NT2], BF16)
        ws8 = WStream(b, 8, name="wg")
        sgs = [b.sb("sgs%d" % i, [128, NT2], BF16) for i in range(2)]
        sig = b.sb("sig", [128, 512], F32)
        for m in range(8):
            w = ws8.load(wglu, m * 128)
            sg = sgs[m % 2]
            b.dma("sp", sg[:, :], sgTd[m * 128:(m + 1) * 128, :], writes=[sg])
            for n in range(2):
                bank = P[(m * 2 + n) % 2]
                b.mm(bank, [(bank[:, :], w[:, c, :], ysb[:, c, n * 512:(n + 1) * 512]) for c in range(8)], reads=[w, ysb])
                b.act(sig[:, :], bank[:, :], AF.Sigmoid, [bank, bg_sb], [sig], bias=bg_sb[:, m:m + 1])
                b.tt("dve", sig[:, :], sig[:, :], ysb[:, m, n * 512:(n + 1) * 512], ALU.mult, [sig, ysb], [sig])
                b.tt("dve", ycs[:, m, n * 512:(n + 1) * 512], sig[:, :], sg[:, n * 512:(n + 1) * 512], ALU.mult, [sig, sg], [ycs])
    xs_ring = [b.sb("xs%d" % i, [128, NT2], F32) for i in range(2)]
    for m in range(16):
        w = ws.load(wo, m * 128)
        xs = xs_ring[m % 2]
        b.dma("sp", xs[:, :], resT[m * 128:(m + 1) * 128, :], writes=[xs])
        for n in range(2):
            bank = P[2 + (m * 2 + n) % 2]
            items = []
            for c in range(16):
                if glu and c < 8:
                    rhs = ycs[:, c, n * 512:(n + 1) * 512]
                else:
                    rhs = ysb[:, c, n * 512:(n + 1) * 512]
                items.append((bank[:, :], w[:, c, :], rhs))
            b.mm(bank, items, reads=[w, ysb] + ([ycs] if glu else []))
            b.tt("dve", hT[:, m, n * 512:(n + 1) * 512], bank[:, :], xs[:, n * 512:(n + 1) * 512], ALU.add, [bank, xs], [hT])
        if not final:
            outs.append(b.dma("sp", hTd[m * 128:(m + 1) * 128, :], hT[:, m, :], reads=[hT]))
    sq_ring = [b.sb("sq%d" % i, [128, NT2], BF16) for i in range(2)]
    rstd = b.sb("rstd", [128, NT2], F32)
    for c in range(16):
        sq = sq_ring[c % 2]
        b.act(sq[:, :], hT[:, c, :], AF.Square, [hT], [sq])
        for n in range(2):
            b.mm(P[4 + n], [(P[4 + n][:, :], ones_bf[:, :], sq[:, n * 512:(n + 1) * 512])], reads=[sq, ones_bf],
                 start=(c == 0), stop=(c == 15))
    for n in range(2):
        b.act(rstd[:, n * 512:(n + 1) * 512], P[4 + n][:, :], AF.Sqrt, [P[4 + n], eps_t], [rstd],
              bias=eps_t[:, 0:1], scale=1.0 / D)
    b.recip(rstd[:, :], rstd[:, :], [rstd], [rstd])
    if final:
        ob = [b.sb("ob%d" % i, [128, NT2], F32) for i in range(2)]
    else:
        ob = [b.sb("ob%d" % i, [128, NT2], BF16) for i in range(2)]
    for c in range(16):
        o = ob[c % 2]
        b.stt(o[:, :], hT[:, c, :], g_sb[:, c:c + 1], rstd[:, :], ALU.mult, ALU.mult, [hT, g_sb, rstd], [o])
        dst = outT if final else hnTd
        outs.append(b.dma("sp", dst[c * 128:(c + 1) * 128, :], o[:, :], reads=[o]))
    b.wait_all_outputs("sp", outs)
    b.emit()
    return nc


NCOL_L1 = 512 + 512 + 4 * (384 + 256)
TWO_PI = 6.283185307179586


def l1_weight_cols(hh):
    cols = list(range(512 * hh, 512 * hh + 512))
    cols += list(range(4096 + 512 * hh, 4096 + 512 * hh + 512))
    for i in range(4):
        hg = 4 * hh + i
        cols += list(range(1024 + hg * 128, 1024 + hg * 128 + 128))
        cols += list(range(2048 + hg * 128, 2048 + hg * 128 + 128))
        cols += list(range(5120 + hg * 128, 5120 + hg * 128 + 128))
    for i in range(4):
        hg = 4 * hh + i
        cols += list(range(2048 + hg * 128, 2048 + hg * 128 + 128))
        cols += list(range(3072 + hg * 128, 3072 + hg * 128 + 128))
    assert len(cols) == NCOL_L1
    return np.array(cols)


def l3_inputs(inp, hnT_full, hh):
    g0 = 32 * hh
    lam_re = np.asarray(inp["s5_lam_re"])[0][g0:g0 + 32]
    lam_im = np.asarray(inp["s5_lam_im"])[0][g0:g0 + 32]
    logdt = np.asarray(inp["s5_log_dt"])[0][g0:g0 + 32]
    b_re = np.asarray(inp["s5_b_re"])[0][g0:g0 + 32]
    b_im = np.asarray(inp["s5_b_im"])[0][g0:g0 + 32]
    c_re = np.asarray(inp["s5_c_re"])[0][g0:g0 + 32]
    c_im = np.asarray(inp["s5_c_im"])[0][g0:g0 + 32]
    dsk = np.asarray(inp["s5_d"])[0][g0:g0 + 32]
    dup = lambda a: np.ascontiguousarray(np.concatenate([a.T, a.T], 0))
    BA = np.zeros((128, 4, 4, 128), np.float32)
    for gl in range(32):
        ut, q = gl // 8, gl % 8
        r0 = 32 * (q // 2) + 16 * (q % 2)
        slot = (q % 2) if q // 2 < 3 else 2 + (q % 2)
        BA[r0:r0 + 16, ut, slot, 0:64] = b_re[gl].T
        BA[r0:r0 + 16, ut, slot, 64:128] = b_im[gl].T
    cT = lambda c: np.ascontiguousarray(np.concatenate([c.transpose(2, 0, 1)] * 2, 0))
    gam = np.asarray(inp["hgrn_gamma"])[:, 512 * hh:512 * hh + 512]
    d = {
        "hnT": hnT_full,
        "wc": np.ascontiguousarray(np.asarray(inp["cd_w_in"])[0][:, l1_weight_cols(hh)]),
        "lamre": dup(lam_re), "lamim": dup(lam_im),
        "logdt": np.ascontiguousarray(np.broadcast_to(logdt[None, :], (128, 32))),
        "cre": cT(c_re), "cim": cT(c_im), "BA": BA,
        "dsk": np.ascontiguousarray(dsk.reshape(4, 128).T),
        "ttf": np.ascontiguousarray(np.broadcast_to(np.arange(2048, dtype=np.float32)[None, :], (128, 2048))),
        "gamc": np.ascontiguousarray(np.stack([gam[0].reshape(4, 128).T, gam[1].reshape(4, 128).T], 1)),
        "gamr": np.ascontiguousarray(gam.reshape(1, 1024)),
        "hng": np.ascontiguousarray(np.asarray(inp["hgrn_norm_g"])[0][512 * hh:512 * hh + 512].reshape(4, 128).T),
    }
    hc = host_consts()
    for k in ["c_ones", "c_tri", "c_ubd", "c_mbd", "c_onesf"]:
        d[k] = hc[k]
    return d


def build_l3():
    nc = bass.Bass("TRN2", target_bir_lowering=False)
    b = B(nc)
    dr = lambda n, sh, dt=F32: nc.dram_tensor(n, list(sh), dt, kind="ExternalInput").ap()
    hnTd = dr("hnT", [D, S], BF16)
    wc = dr("wc", [D, NCOL_L1])
    lamre_d, lamim_d, logdt_d = dr("lamre", [128, 32]), dr("lamim", [128, 32]), dr("logdt", [128, 32])
    cre_d, cim_d = dr("cre", [128, 32, 16]), dr("cim", [128, 32, 16])
    BA_d = dr("BA", [128, 4, 4, 128])
    dsk_d, ttf_d = dr("dsk", [128, 4]), dr("ttf", [128, S])
    gamc_d, gamr_d, hng_d = dr("gamc", [128, 2, 4]), dr("gamr", [1, 1024]), dr("hng", [128, 4])
    ysT = nc.dram_tensor("ysT", [512, S], BF16, kind="ExternalOutput").ap()
    sgT = nc.dram_tensor("sgT", [512, S], BF16, kind="ExternalOutput").ap()
    ydT = nc.dram_tensor("ydT", [512, S], BF16, kind="ExternalOutput").ap()
    cs = load_consts(b, nc, ["c_ones", "c_tri", "c_ubd", "c_mbd", "c_onesf"])
    cx = Ctx()
    cx.ones_bf = cs["c_ones"]
    cx.tri, cx.ubd, cx.mbd, onesf = cs["c_tri"], cs["c_ubd"], cs["c_mbd"], cs["c_onesf"]
    outs = []

    def ld(name, dap, sh, dt=F32):
        t = b.sb("s_" + name, sh, dt)
        idx = tuple(slice(None) for _ in sh)
        b.dma("sp", t[idx], dap[idx], writes=[t])
        return t
    lamre, lamim, logdt = ld("lamre", lamre_d, [128, 32]), ld("lamim", lamim_d, [128, 32]), ld("logdt", logdt_d, [128, 32])
    cre, cim = ld("cre", cre_d, [128, 32, 16]), ld("cim", cim_d, [128, 32, 16])
    BAf = ld("BAf", BA_d, [128, 4, 4, 128])
    dsk, ttf = ld("dsk", dsk_d, [128, 4]), ld("ttf", ttf_d, [128, S])
    gamc, hng = ld("gamc", gamc_d, [128, 2, 4]), ld("hng", hng_d, [128, 4])
    gamr = ld("gamr", gamr_d, [1, 1024])
    eps_t = b.sb("eps_t", [128, 1], F32)
    b.memset("dve", eps_t[:, :], EPS, [eps_t])
    P = [b.ps("P%d" % i, [128, 512]) for i in range(8)]
    hnT = b.sb("hnT_sb", [128, 16, S], BF16)
    cx.hnT = hnT
    for c in range(16):
        b.dma("sp", hnT[:, c, :], hnTd[c * 128:(c + 1) * 128, :], writes=[hnT])
    ws = WStream(b, 16)
    arena = b.sb("arena", [128, 14336], F32)
    sm = lambda n, sh=(128, 32), dt=F32: b.sb(n, list(sh), dt)
    dt_, rho, th = sm("dt_"), sm("rho"), sm("th")
    b.act(dt_[:, :], logdt[:, :], AF.Exp, [logdt], [dt_])
    t1 = sm("t1")
    b.tt("dve", t1[:, :], lamre[:, :], dt_[:, :], ALU.mult, [lamre, dt_], [t1])
    b.act(rho[:, :], t1[:, :], AF.Exp, [t1], [rho])
    b.tt("dve", th[:, :], lamim[:, :], dt_[:, :], ALU.mult, [lamim, dt_], [th])
    thn = sm("thn")
    b.ts("dve", thn[0:64, :], th[0:64, :], 1.0 / TWO_PI, None, ALU.mult, None, [th], [thn])
    b.ts("dve", thn[64:128, :], th[64:128, :], -1.0 / TWO_PI, None, ALU.mult, None, [th], [thn])
    un, fr, sa, ca = sm("un"), sm("fr"), sm("sa"), sm("ca")
    q25 = sm("q25", (128, 1))
    b.memset("dve", q25[:, :], 0.25, [q25])
    MAGIC = 12582912.0
    b.ts("dve", un[:, :], th[:, :], 1.0 / TWO_PI, None, ALU.mult, None, [th], [un])
    b.ts("dve", fr[:, :], un[:, :], MAGIC, -MAGIC, ALU.add, ALU.add, [un], [fr])
    b.tt("dve", sa[:, :], un[:, :], fr[:, :], ALU.subtract, [un, fr], [sa])
    b.ts("dve", un[:, :], un[:, :], 0.25, None, ALU.add, None, [un], [un])
    b.ts("dve", fr[:, :], un[:, :], MAGIC, -MAGIC, ALU.add, ALU.add, [un], [fr])
    b.tt("dve", ca[:, :], un[:, :], fr[:, :], ALU.subtract, [un, fr], [ca])
    sn, cn = sm("sn"), sm("cn")
    b.act(sn[:, :], sa[:, :], AF.Sin, [sa], [sn], scale=TWO_PI)
    b.act(cn[:, :], ca[:, :], AF.Sin, [ca], [cn], scale=TWO_PI)
    are, aim = sm("are"), sm("aim")
    b.tt("dve", are[:, :], rho[:, :], cn[:, :], ALU.mult, [rho, cn], [are])
    b.tt("dve", aim[:, :], rho[:, :], sn[:, :], ALU.mult, [rho, sn], [aim])
    nr, inv, t2, t3, core_, coim = sm("nr"), sm("inv"), sm("t2"), sm("t3"), sm("core_"), sm("coim")
    b.ts("dve", nr[:, :], are[:, :], -1.0, None, ALU.add, None, [are], [nr])
    b.tt("dve", t2[:, :], lamre[:, :], lamre[:, :], ALU.mult, [lamre], [t2])
    b.tt("dve", t3[:, :], lamim[:, :], lamim[:, :], ALU.mult, [lamim], [t3])
    b.tt("dve", t2[:, :], t2[:, :], t3[:, :], ALU.add, [t2, t3], [t2])
    b.recip(inv[:, :], t2[:, :], [t2], [inv])
    b.tt("dve", t2[:, :], nr[:, :], lamre[:, :], ALU.mult, [nr, lamre], [t2])
    b.tt("dve", t3[:, :], aim[:, :], lamim[:, :], ALU.mult, [aim, lamim], [t3])
    b.tt("dve", t2[:, :], t2[:, :], t3[:, :], ALU.add, [t2, t3], [t2])
    b.tt("dve", core_[:, :], t2[:, :], inv[:, :], ALU.mult, [t2, inv], [core_])
    b.tt("dve", t2[:, :], aim[:, :], lamre[:, :], ALU.mult, [aim, lamre], [t2])
    b.tt("dve", t3[:, :], nr[:, :], lamim[:, :], ALU.mult, [nr, lamim], [t3])
    b.tt("dve", t2[:, :], t2[:, :], t3[:, :], ALU.subtract, [t2, t3], [t2])
    b.tt("dve", coim[:, :], t2[:, :], inv[:, :], ALU.mult, [t2, inv], [coim])
    Ya, Yb, Yc, Yd = sm("Ya"), sm("Yb"), sm("Yc"), sm("Yd")
    top, bot = slice(0, 64), slice(64, 128)
    b.copy("dve", Ya[top, :], core_[top, :], [core_], [Ya])
    b.ts("dve", Ya[bot, :], coim[bot, :], -1.0, None, ALU.mult, None, [coim], [Ya])
    b.ts("dve", Yb[top, :], coim[top, :], -1.0, None, ALU.mult, None, [coim], [Yb])
    b.ts("dve", Yb[bot, :], core_[bot, :], -1.0, None, ALU.mult, None, [core_], [Yb])
    b.ts("dve", Yc[top, :], coim[top, :], -1.0, None, ALU.mult, None, [coim], [Yc])
    b.copy("dve", Yc[bot, :], core_[bot, :], [core_], [Yc])
    b.ts("dve", Yd[top, :], core_[top, :], -1.0, None, ALU.mult, None, [core_], [Yd])
    b.ts("dve", Yd[bot, :], coim[bot, :], -1.0, None, ALU.mult, None, [coim], [Yd])
    L1 = b.sb("L1", [128, 32, 16], F32)
    L2 = b.sb("L2", [128, 32, 16], F32)
    tmpL = sm("tmpL")
    for hp in range(16):
        for (L, Y0, Y1) in ((L1, Ya, Yb), (L2, Yc, Yd)):
            b.tt("dve", L[:, :, hp], cre[:, :, hp], Y0[:, :], ALU.mult, [cre, Y0], [L])
            b.tt("dve", tmpL[:, :], cim[:, :, hp], Y1[:, :], ALU.mult, [cim, Y1], [tmpL])
            b.tt("dve", L[:, :, hp], L[:, :, hp], tmpL[:, :], ALU.add, [L, tmpL], [L])
    LP1 = b.sb("LP1", [128, 32, 128], BF16)
    LP2 = b.sb("LP2", [128, 32, 128], BF16)
    b.memset("pool", LP1[:, :, :], 0.0, [LP1])
    b.memset("pool", LP2[:, :, :], 0.0, [LP2])
    for gl in range(32):
        q = gl % 8
        b.copy("dve", LP1[:, gl, 16 * q:16 * q + 16], L1[:, gl, :], [L1], [LP1])
        b.copy("dve", LP2[:, gl, 16 * q:16 * q + 16], L2[:, gl, :], [L2], [LP2])
    BAb = b.sb("BAb", [128, 4, 4, 128], BF16)
    b.copy("pool", BAb[:, :, :, :], BAf[:, :, :, :], [BAf], [BAb])
    off = 0
    uT, off = b.carve(arena, off, [128, 4, S], BF16, "uT")
    for ut in range(4):
        w = ws.load(wc, ut * 128)

        def ev_u(n, bank, ut=ut):
            b.copy("act", uT[:, ut, n * 512:(n + 1) * 512], bank[:, :], [bank], [uT])
        proj_fm(b, w, 128, hnT, 16, S, P[0:2], ev_u)
    sgb = []
    for i in range(2):
        t, off = b.carve(arena, off, [128, 512], BF16, "sgb%d" % i)
        sgb.append(t)
    k_ = [0]
    for ut in range(4):
        w = ws.load(wc, 512 + ut * 128)

        def ev_g(n, bank, ut=ut):
            o = sgb[k_[0] % 2]
            k_[0] += 1
            b.act(o[:, :], bank[:, :], AF.Silu, [bank], [o])
            outs.append(b.dma("sp", sgT[ut * 128:(ut + 1) * 128, n * 512:(n + 1) * 512], o[:, :], reads=[o]))
        proj_fm(b, w, 128, hnT, 16, S, P[0:2], ev_g)
    C512 = lambda nm, dt=F32: b.carve(arena, 0, [128, 512], dt, nm)
    rhoT, off = b.carve(arena, off, [128, 512], F32, "rhoT")
    un5, off = b.carve(arena, off, [128, 512], F32, "un5")
    ki5, off = b.carve(arena, off, [128, 512], F32, "ki5")
    un6, off = b.carve(arena, off, [128, 512], F32, "un6")
    fr5, off = b.carve(arena, off, [128, 512], F32, "fr5")
    sa5, off = b.carve(arena, off, [128, 512], F32, "sa5")
    ca5, off = b.carve(arena, off, [128, 512], F32, "ca5")
    CS1, off = b.carve(arena, off, [128, 512], F32, "CS1")
    CS2, off = b.carve(arena, off, [128, 512], F32, "CS2")
    ta, off = b.carve(arena, off, [128, 512], F32, "ta")
    tb, off = b.carve(arena, off, [128, 512], F32, "tb")
    wst = []
    for i in range(2):
        t, off = b.carve(arena, off, [128, 512], F32, "wst%d" % i)
        wst.append(t)
    P1b, off = b.carve(arena, off, [128, 512], BF16, "P1b")
    P2b, off = b.carve(arena, off, [128, 512], BF16, "P2b")
    yp, off = b.carve(arena, off, [128, 512], F32, "yp")
    x3, off = b.carve(arena, off, [128, 512], F32, "x3")
    sgm, off = b.carve(arena, off, [128, 512], F32, "sgm")
    yob = []
    for i in range(2):
        t, off = b.carve(arena, off, [128, 512], BF16, "yob%d" % i)
        yob.append(t)
    assert off <= 14336, off
    widx = [0]
    for ut in range(4):
        ybanks = P[4:8]
        for q in range(8):
            gl = ut * 8 + q
            if q // 2 < 3:
                rows = slice(32 * (q // 2), 32 * (q // 2) + 32)
                slot = q % 2
            else:
                rows = slice(64, 128)
                slot = 2 + q % 2
            b.ts("dve", rhoT[:, :], ttf[:, 0:512], 0.0, rho[:, gl:gl + 1], ALU.mult, ALU.add, [ttf, rho], [rhoT])
            prev = None
            for n in range(4):
                pa, pb_ = P[(n % 2) * 2], P[(n % 2) * 2 + 1]
                ur = uT[rows, ut, n * 512:(n + 1) * 512]
                b.mm(pa, [(pa[:, :], BAb[rows, ut, slot, :], ur)], reads=[BAb, uT])
                b.op("pe", lambda e, o=pb_[0:64, :], l=BAb[rows, ut, slot, 64:128], r=ur: e.matmul(o, l, r, start=True, stop=True, skip_group_check=True),
                     reads=[BAb, uT], writes=[pb_], inc=False)
                b.op("pe", lambda e, o=pb_[64:128, :], l=BAb[rows, ut, slot, 0:64], r=ur: e.matmul(o, l, r, start=True, stop=True, skip_group_check=True),
                     reads=(), writes=[pb_])
                b.act(un5[:, :], ttf[:, n * 512:(n + 1) * 512], AF.Copy, [ttf, thn], [un5], scale=thn[:, gl:gl + 1])
                b.act(un6[:, :], ttf[:, n * 512:(n + 1) * 512], AF.Identity, [ttf, thn, q25], [un6], scale=thn[:, gl:gl + 1], bias=q25[:, 0:1])
                b.ts("pool", fr5[:, :], un5[:, :], MAGIC, -MAGIC, ALU.add, ALU.add, [un5], [fr5])
                b.tt("pool", sa5[:, :], un5[:, :], fr5[:, :], ALU.subtract, [un5, fr5], [sa5])
                b.ts("dve", ki5[:, :], un6[:, :], MAGIC, -MAGIC, ALU.add, ALU.add, [un6], [ki5])
                b.tt("dve", ca5[:, :], un6[:, :], ki5[:, :], ALU.subtract, [un6, ki5], [ca5])
                b.act(CS2[:, :], sa5[:, :], AF.Sin, [sa5], [CS2], scale=TWO_PI)
                b.act(CS1[:, :], ca5[:, :], AF.Sin, [ca5], [CS1], scale=TWO_PI)
                b.tt("dve", ta[:, :], pa[:, :], CS1[:, :], ALU.mult, [pa, CS1], [ta])
                b.tt("dve", tb[:, :], pb_[:, :], CS2[:, :], ALU.mult, [pb_, CS2], [tb])
                b.tt("pool", ta[:, :], ta[:, :], tb[:, :], ALU.add, [ta, tb], [ta])
                wt = wst[widx[0] % 2]
                widx[0] += 1
                if prev is None:
                    b.op("dve", lambda g, wt=wt: g.tensor_tensor_scan(wt[:, :], rhoT[:, :], ta[:, :], 0.0, ALU.mult, ALU.add),
                         [rhoT, ta], [wt])
                else:
                    b.op("dve", lambda g, wt=wt, prev=prev: g.tensor_tensor_scan(wt[:, :], rhoT[:, :], ta[:, :], prev[:, 511:512], ALU.mult, ALU.add),
                         [rhoT, ta, prev], [wt])
                prev = wt
                b.tt("pool", P1b[:, :], wt[:, :], CS1[:, :], ALU.mult, [wt, CS1], [P1b])
                b.tt("pool", P2b[:, :], wt[:, :], CS2[:, :], ALU.mult, [wt, CS2], [P2b])
                yb = ybanks[n]
                b.mm(yb, [(yb[:, :], LP1[:, gl, :], P1b[:, :]), (yb[:, :], LP2[:, gl, :], P2b[:, :])],
                     reads=[LP1, LP2, P1b, P2b], start=(q == 0), stop=(q == 7))
        for n in range(4):
            yb = ybanks[n]
            b.stt(yp[:, :], uT[:, ut, n * 512:(n + 1) * 512], dsk[:, ut:ut + 1], yb[:, :], ALU.mult, ALU.add, [uT, dsk, yb], [yp])
            b.tt("dve", x3[:, :], yp[:, :], yp[:, :], ALU.mult, [yp], [x3])
            b.ts("dve", x3[:, :], x3[:, :], 0.044715, 1.0, ALU.mult, ALU.add, [x3], [x3])
            b.tt("dve", x3[:, :], x3[:, :], yp[:, :], ALU.mult, [x3, yp], [x3])
            b.act(sgm[:, :], x3[:, :], AF.Sigmoid, [x3], [sgm], scale=2.0 * 0.7978845608028654)
            o = yob[(ut * 4 + n) % 2]
            b.tt("dve", o[:, :], yp[:, :], sgm[:, :], ALU.mult, [yp, sgm], [o])
            outs.append(b.dma("sp", ysT[ut * 128:(ut + 1) * 128, n * 512:(n + 1) * 512], o[:, :], reads=[o]))
    b.barrier()
    lbc, omlc, nomlc = sm("lbc", (128, 4)), sm("omlc", (128, 4)), sm("nomlc", (128, 4))
    b.tt("dve", lbc[:, :], gamc[:, 1, :], gamc[:, 0, :], ALU.subtract, [gamc], [lbc])
    b.act(lbc[:, :], lbc[:, :], AF.Sigmoid, [lbc], [lbc])
    b.ts("dve", omlc[:, :], lbc[:, :], -1.0, 1.0, ALU.mult, ALU.add, [lbc], [omlc])
    b.ts("dve", nomlc[:, :], omlc[:, :], -1.0, None, ALU.mult, None, [omlc], [nomlc])
    lbr = b.sb("lbr", [1, 512], F32)
    omlr = b.sb("omlr", [1, 512], F32)
    b.tt("dve", lbr[:, :], gamr[:, 512:1024], gamr[:, 0:512], ALU.subtract, [gamr], [lbr])
    b.act(lbr[:, :], lbr[:, :], AF.Sigmoid, [lbr], [lbr])
    b.ts("dve", omlr[:, :], lbr[:, :], -1.0, 1.0, ALU.mult, ALU.add, [lbr], [omlr])
    off = 0
    qTr, off = b.carve(arena, off, [128, S], F32, "qTr")
    kTr, off = b.carve(arena, off, [128, S], F32, "kTr")
    ktok, off = b.carve(arena, off, [128, 16, 128], F32, "ktok")
    vtok, off = b.carve(arena, off, [128, 16, 128], BF16, "vtok")
    logat, off = b.carve(arena, off, [128, 16, 128], F32, "logat")
    cx.Sst, off = b.carve(arena, off, [128, 256], F32, "Sst")
    cx.Smid, off = b.carve(arena, off, [128, 256], BF16, "Smid")
    cx.E1, off = b.carve(arena, off, [128, 132], F32, "E1")
    cx.enb, off = b.carve(arena, off, [128, 128], F32, "enb")
    cx.edec, off = b.carve(arena, off, [128, 128], F32, "edec")
    cx.qs, off = b.carve(arena, off, [128, 128], BF16, "qs")
    cx.ks, off = b.carve(arena, off, [128, 128], BF16, "ks")
    cx.kd, off = b.carve(arena, off, [128, 128], BF16, "kd")
    cx.AT, off = b.carve(arena, off, [128, 128], BF16, "AT")
    obS, off = b.carve(arena, off, [128, 512], F32, "obS")
    sqS, off = b.carve(arena, off, [128, 512], BF16, "sqS")
    rs2, off = b.carve(arena, off, [128, 512], F32, "rs2")
    gs2, off = b.carve(arena, off, [128, 512], F32, "gs2")
    sgk, off = b.carve(arena, off, [128, 512], F32, "sgk")
    omlb, off = b.carve(arena, off, [128, 512], F32, "omlb")
    ybs = []
    for i in range(2):
        t, off = b.carve(arena, off, [128, 512], BF16, "ybs%d" % i)
        ybs.append(t)
    assert off <= 14336, off
    cx.pP = [P[0], P[1]]
    cx.pG = [P[2], P[3], P[4]]
    cx.pGo = [P[5]]
    cx.pGs = P[6]
    yi = [0]
    for i in range(4):
        base = 1024 + i * 384
        w = ws.load(wc, base)

        def ev_q(n, bank):
            b.act(qTr[:, n * 512:(n + 1) * 512], bank[:, :], AF.Silu, [bank], [qTr])
        proj_fm(b, w, 128, hnT, 16, S, cx.pP, ev_q)
        w = ws.load(wc, base + 128)

        def ev_k(n, bank, i=i):
            b.act(kTr[:, n * 512:(n + 1) * 512], bank[:, :], AF.Sigmoid, [bank], [kTr])
            b.ts("dve", kTr[:, n * 512:(n + 1) * 512], kTr[:, n * 512:(n + 1) * 512], nomlc[:, i:i + 1], omlc[:, i:i + 1],
                 ALU.mult, ALU.add, [kTr, nomlc, omlc], [kTr])
        proj_fm(b, w, 128, hnT, 16, S, cx.pP, ev_k)
        pbc = P[7]
        for a in range(4):
            b.mm(pbc, [(pbc[:, a * 128:(a + 1) * 128], onesf[0:1, 0:128], omlr[0:1, i * 128:(i + 1) * 128])],
                 reads=[onesf, omlr], skip_group_check=True)
        b.copy("act", omlb[:, :], pbc[:, :], [pbc], [omlb])
        tb_ = 1024 + 4 * 384 + i * 256
        for gi in range(2):
            w = ws.load(wc, tb_ + gi * 128)
            for n in range(4):
                bank = cx.pP[n % 2]
                for tt_ in range(4):
                    t = n * 4 + tt_
                    items = [(bank[:, tt_ * 128:(tt_ + 1) * 128], hnT[:, c, t * 128:(t + 1) * 128], w[:, c, :]) for c in range(16)]
                    b.mm(bank, items, reads=[w, hnT], skip_group_check=True)
                kt3 = ktok[:, n * 4:(n + 1) * 4, :]
                if gi == 0:
                    b.act(sgk[:, :], bank[:, :], AF.Sigmoid, [bank], [sgk])
                    b.tt("dve", sgk[:, :], sgk[:, :], omlb[:, :], ALU.mult, [sgk, omlb], [sgk])
                    b.tt("dve", kt3, omlb[:, :].rearrange("p (a e) -> p a e", e=128), sgk[:, :].rearrange("p (a e) -> p a e", e=128),
                         ALU.subtract, [omlb, sgk], [ktok])
                    b.ts("dve", sgk[:, :].rearrange("p (a e) -> p a e", e=128), kt3, -1.0, 1.0, ALU.mult, ALU.add, [ktok], [sgk])
                    b.act(logat[:, n * 4:(n + 1) * 4, :], sgk[:, :].rearrange("p (a e) -> p a e", e=128), AF.Ln, [sgk], [logat])
                else:
                    b.copy("act", vtok[:, n * 4:(n + 1) * 4, :], bank[:, :].rearrange("p (a e) -> p a e", e=128), [bank], [vtok])
        wg = ws.load(wc, base + 256)

        def fin(n, i=i, wg=wg):
            b.copy("act", obS[:, :], cx.pGo[0][:, :], [cx.pGo[0]], [obS])
            b.act(sqS[:, :], obS[:, :], AF.Square, [obS], [sqS])
            pq = cx.pG[2]
            b.mm(pq, [(pq[:, :], cx.ones_bf[:, :], sqS[:, :])], reads=[sqS, cx.ones_bf])
            b.act(rs2[:, :], pq[:, :], AF.Sqrt, [pq, eps_t], [rs2], bias=eps_t[:, 0:1], scale=1.0 / 128.0)
            b.recip(rs2[:, :], rs2[:, :], [rs2], [rs2])
            pg = cx.pP[0]
            b.mm(pg, [(pg[:, :], wg[:, c, :], hnT[:, c, n * 512:(n + 1) * 512]) for c in range(16)], reads=[wg, hnT])
            b.act(gs2[:, :], pg[:, :], AF.Silu, [pg], [gs2])
            b.tt("dve", obS[:, :], obS[:, :], rs2[:, :], ALU.mult, [obS, rs2], [obS])
            yb_ = ybs[yi[0] % 2]
            yi[0] += 1
            b.stt(yb_[:, :], obS[:, :], hng[:, i:i + 1], gs2[:, :], ALU.mult, ALU.mult, [obS, hng, gs2], [yb_])
            outs.append(b.dma("sp", ydT[i * 128:(i + 1) * 128, n * 512:(n + 1) * 512], yb_[:, :], reads=[yb_]))
        gla_head(b, cx, qTr, kTr, ktok, vtok, logat, 128, fin)
    b.wait_all_outputs("sp", outs)
    b.emit()
    return nc


from concourse.bass_utils import run_bass_kernel_spmd

_CACHE = {}


def _prog(name, fn):
    if name not in _CACHE:
        _CACHE[name] = fn()
    return _CACHE[name]


def kernel(**inp):
    inp = {k: np.asarray(v) for k, v in inp.items()}
    hc = host_consts()
    cores = list(range(8))
    nc1 = build_l1()
    r1 = run_bass_kernel_spmd(nc1, [l1_inputs(inp, c // 2, c % 2) for c in cores], core_ids=cores).results
    y0 = [np.asarray(r["yT"]) for r in r1]
    nc2 = build_outproj(False, False)
    ng1 = np.ascontiguousarray(inp["norm_g"][1].reshape(16, 128).T)
    maps = []
    for c in cores:
        bi, th = c // 2, c % 2
        sl = slice(th * 1024, (th + 1) * 1024)
        a, b_ = y0[2 * bi], y0[2 * bi + 1]
        yfull = np.concatenate([a[0:512, sl], b_[0:512, sl], a[512:1024, sl], b_[512:1024, sl]], 0)
        maps.append({"yT": np.ascontiguousarray(yfull), "resT": np.ascontiguousarray(inp["x"][bi][sl].T),
                     "wo": inp["ab_w_out"][0], "ng": ng1, "c_ones": hc["c_ones"]})
    r2 = run_bass_kernel_spmd(nc2, maps, core_ids=cores).results
    hT = [np.asarray(r["hT"]) for r in r2]
    hnT = [np.asarray(r["hnT"]) for r in r2]
    nc3 = build_l3()
    maps = []
    for c in cores:
        bi, hh = c // 2, c % 2
        full = np.ascontiguousarray(np.concatenate([hnT[2 * bi], hnT[2 * bi + 1]], 1))
        maps.append(l3_inputs(inp, full, hh))
    r3 = run_bass_kernel_spmd(nc3, maps, core_ids=cores).results
    nc4 = build_outproj(True, True)
    ngf = np.ascontiguousarray(inp["final_g"].reshape(16, 128).T)
    bgl = np.ascontiguousarray(inp["s5_b_glu"][0].reshape(8, 128).T)
    maps = []
    for c in cores:
        bi, th = c // 2, c % 2
        sl = slice(th * 1024, (th + 1) * 1024)
        a, b_ = r3[2 * bi], r3[2 * bi + 1]
        yfull = np.concatenate([np.asarray(a["ysT"])[:, sl], np.asarray(b_["ysT"])[:, sl],
                                np.asarray(a["ydT"])[:, sl], np.asarray(b_["ydT"])[:, sl]], 0)
        sg = np.concatenate([np.asarray(a["sgT"])[:, sl], np.asarray(b_["sgT"])[:, sl]], 0)
        maps.append({"yT": np.ascontiguousarray(yfull), "sgT": np.ascontiguousarray(sg), "resT": hT[c],
                     "wo": inp["cd_w_out"][0], "ng": ngf, "wglu": inp["s5_w_glu"][0], "bglu": bgl,
                     "c_ones": hc["c_ones"]})
    r4 = run_bass_kernel_spmd(nc4, maps, core_ids=cores).results
    out = np.empty((4, S, D), np.float32)
    for c in cores:
        bi, th = c // 2, c % 2
        out[bi, th * 1024:(th + 1) * 1024, :] = np.asarray(r4[c]["outT"]).T
    return out
```

```python
from contextlib import ExitStack
import concourse.bass as bass
import concourse.mybir as mybir

F32 = mybir.dt.float32
BF16 = mybir.dt.bfloat16
I32 = mybir.dt.int32
AF = mybir.ActivationFunctionType
ALU = mybir.AluOpType

ENGS = ("pe", "act", "dve", "pool", "sp")


class T:
    __slots__ = ("t", "name", "lw", "rd", "lwx")

    def __init__(self, t, name):
        self.t = t
        self.name = name
        self.lw = None
        self.rd = []
        self.lwx = []

    def __getitem__(self, idx):
        return self.t[idx]


class B:
    def __init__(self, nc, ndma=6):
        self.nc = nc
        self.es = ExitStack()
        self.scopes = [ExitStack()]
        self.sid = 0
        self.cd = {}
        self.eng = {"pe": nc.tensor, "act": nc.scalar, "dve": nc.vector,
                    "pool": nc.gpsimd, "sp": nc.sync}
        self.ops = {e: [] for e in ENGS}
        self.sem = {}
        self.cnt = {}
        for e in ENGS:
            self.sem[e] = self.es.enter_context(nc.semaphore("s_" + e))
            self.cnt[e] = 0
        self.ndma = ndma
        self.dq = {}
        for q in ("sp", "act", "pool"):
            sems = [self.es.enter_context(nc.semaphore("d_%s%d" % (q, i))) for i in range(ndma)]
            for i, s in enumerate(sems):
                self.sem[("d", q, i)] = s
            self.dq[q] = 0
        self.seen = {e: {} for e in ENGS}
        self.final_waits = []
        self.sem["cc"] = self.es.enter_context(nc.semaphore("s_cc"))
        self.ncc = 0

    def sb(self, name, shape, dt):
        name = "%s_z%d" % (name, self.sid)
        return T(self.scopes[-1].enter_context(self.nc.sbuf_tensor(name, list(shape), dt)), name)

    def ps(self, name, shape, dt=F32):
        name = "%s_z%d" % (name, self.sid)
        return T(self.scopes[-1].enter_context(self.nc.psum_tensor(name, list(shape), dt)), name)

    def push_scope(self):
        self.sid += 1
        self.scopes.append(ExitStack())

    def pop_scope(self):
        self.barrier()
        self.flush()
        self.scopes.pop().close()

    def cc_allgather(self, groups, in_ap, out_ap):
        self.barrier()
        self.ncc += 1
        sem = self.sem["cc"]
        self.ops["pool"].append(([], lambda g: g.collective_compute("AllGather", ALU.bypass, groups, [in_ap], [out_ap]).then_inc(sem), None))
        self.barrier()
        self.flush("cc")

    def view(self, t, name=None):
        return T(t.t if isinstance(t, T) else t, name or "v")

    def _waits(self, e, reads, writes):
        w = {}

        def add(dep):
            k, v, de = dep
            if k not in w or w[k] < v:
                w[k] = v

        for t in reads:
            if t.lw is not None:
                add(t.lw)
            for x in t.lwx:
                add(x)
        for t in writes:
            if t.lw is not None:
                if not (t.lw[2] == e and e == "pe"):
                    add(t.lw)
            for x in t.lwx:
                add(x)
            for r in t.rd:
                if r[2] == e and not isinstance(r[0], tuple):
                    continue
                add(r)
        out = []
        seen = self.seen[e]
        for k, v in w.items():
            if seen.get(k, 0) >= v:
                continue
            seen[k] = v
            out.append((k, v))
        return out

    def op(self, e, fn, reads=(), writes=(), inc=True):
        waits = self._waits(e, reads, writes)
        if inc:
            self.cnt[e] += 1
            me = (e, self.cnt[e], e)
        else:
            me = None
        self.ops[e].append((waits, fn, inc))
        if inc:
            for t in reads:
                t.rd.append(me)
            for t in writes:
                t.lw = me
                t.lwx = []
                t.rd = []
        return me

    def mm(self, out_t, items, reads, start=True, stop=True, **kw):
        n = len(items)
        for i, (o, l, r) in enumerate(items):
            st = start and i == 0
            sp = stop and i == n - 1

            def fn(eng, o=o, l=l, r=r, st=st, sp=sp):
                return eng.matmul(o, l, r, start=st, stop=sp, **kw)
            if i == n - 1:
                self.op("pe", fn, reads=reads if n == 1 else (), writes=[out_t])
            else:
                self.op("pe", fn, reads=reads if i == 0 else (), writes=[out_t] if i == 0 else (), inc=False)
        if n > 1:
            me = ("pe", self.cnt["pe"], "pe")
            for t in reads:
                t.rd.append(me)

    def dma(self, q, out_ap, in_ap, reads=(), writes=(), **kw):
        j = self.dq[q]
        self.dq[q] += 1
        slot = j % self.ndma
        key = ("d", q, slot)
        val = 16 * (j // self.ndma + 1)
        waits = self._waits(q, reads, writes)
        prev = 16 * (j // self.ndma)
        if prev > 0 and self.seen[q].get(key, 0) < prev:
            self.seen[q][key] = prev
            waits.append((key, prev))
        sem = self.sem[key]

        def fn(eng, o=out_ap, i=in_ap):
            return eng.dma_start(out=o, in_=i, **kw).then_inc(sem, 16)
        self.ops[q].append((waits, fn, None))
        me = (key, val, "dma_" + q)
        for t in reads:
            t.rd.append(me)
        for t in writes:
            if t.lw is not None and isinstance(t.lw[0], tuple):
                t.lwx.append(t.lw)
            else:
                t.lwx = []
            t.lw = me
            t.rd = []
        return me

    def wait_all_outputs(self, e, deps):
        waits = []
        for d in deps:
            waits.append((d[0], d[1]))
        self.ops[e].append((waits, None, None))

    def cc_allgather_multi(self, groups, pairs):
        self.barrier()
        sem = self.sem["cc"]
        for (in_ap, out_ap) in pairs:
            self.ncc += 1
            self.ops["pool"].append(([], lambda g, i=in_ap, o=out_ap: g.collective_compute("AllGather", ALU.bypass, groups, [i], [o]).then_inc(sem), None))
        self.barrier()
        self.flush("cc")

    def barrier(self):
        tgt = []
        for e in ENGS:
            if self.cnt[e] > 0:
                tgt.append((e, self.cnt[e]))
        for q in self.dq:
            j = self.dq[q]
            for s_ in range(self.ndma):
                n = (j - s_ + self.ndma - 1) // self.ndma if j > s_ else 0
                if n > 0:
                    tgt.append((("d", q, s_), 16 * n))
        if self.ncc > 0:
            tgt.append(("cc", self.ncc))
        for e in ENGS:
            waits = []
            for k, v in tgt:
                if k == e:
                    continue
                if self.seen[e].get(k, 0) < v:
                    self.seen[e][k] = v
                    waits.append((k, v))
            if waits:
                self.ops[e].append((waits, None, None))

    def carve(self, arena, off, shape, dt, name="c"):
        n = 1
        for s_ in shape[1:]:
            n *= s_
        nbytes = n * (2 if dt == BF16 else 4)
        ncol = (nbytes + 3) // 4
        ap = arena.t[:, off:off + ncol]
        if dt != F32:
            ap = ap.bitcast(dt)
        if len(shape) == 3:
            ap = ap.rearrange("p (a b) -> p a b", b=shape[2])
        if shape[0] != 128:
            ap = ap[0:shape[0]]
        return T(ap, name), off + ncol

    def emit(self):
        self.flush()
        while self.scopes:
            self.scopes.pop().close()
        self.es.close()

    def flush(self, label=None):
        nc = self.nc
        if not any(self.ops[e] for e in ENGS):
            return
        self.nflush = getattr(self, "nflush", 0) + 1
        lab = "f%02d_%s" % (self.nflush, label or "x")
        if getattr(self, "profile_scopes", False):
            with nc.named_scope(lab):
                self._flush()
        else:
            self._flush()

    def _flush(self):
        nc = self.nc
        with nc.Block() as block:
            for e in ENGS:
                lst = self.ops[e]
                if not lst:
                    continue
                dec = {"pe": block.tensor, "act": block.scalar, "dve": block.vector,
                       "pool": block.gpsimd, "sp": block.sync}[e]
                sem_e = self.sem[e]

                def body(eng, lst=lst, sem_e=sem_e):
                    for waits, fn, inc in lst:
                        for k, v in waits:
                            eng.wait_ge(self.sem[k], v)
                        if fn is None:
                            continue
                        ins = fn(eng)
                        if inc is True:
                            ins.then_inc(sem_e, 1)
                dec(body)
        self.ops = {e: [] for e in ENGS}


def _act(b, out, in_, func, reads, writes, **kw):
    return b.op("act", lambda e: e.activation(out, in_, func, **kw), reads, writes)


def _tt(b, e, out, in0, in1, op, reads, writes):
    return b.op(e, lambda g: g.tensor_tensor(out, in0, in1, op), reads, writes)


def _ts(b, e, out, in0, s1, s2, op0, op1, reads, writes):
    if op1 is None:
        return b.op(e, lambda g: g.tensor_scalar(out, in0, s1, None, op0), reads, writes)
    return b.op(e, lambda g: g.tensor_scalar(out, in0, s1, s2, op0, op1), reads, writes)


def _stt(b, out, in0, scalar, in1, op0, op1, reads, writes):
    return b.op("dve", lambda g: g.scalar_tensor_tensor(out, in0, scalar, in1, op0, op1), reads, writes)


def _copy(b, e, out, in_, reads, writes):
    if e == "act":
        return b.op(e, lambda g: g.copy(out, in_), reads, writes)
    return b.op(e, lambda g: g.tensor_copy(out, in_), reads, writes)


def _recip(b, out, in_, reads, writes):
    return b.op("dve", lambda g: g.reciprocal(out, in_), reads, writes)


def _memset(b, e, ap, val, writes):
    return b.op(e, lambda g: g.memset(ap, val), (), writes)


B.act = _act
B.tt = _tt
B.ts = _ts
B.stt = _stt
B.copy = _copy
B.recip = _recip
B.memset = _memset

import numpy as np
import ml_dtypes
NPBF = ml_dtypes.bfloat16
S = 2048
D = 2048
EPS = 1e-6


def host_consts():
    j = np.arange(128)[:, None]
    i = np.arange(128)[None, :]
    cur = (j <= i).astype(np.float32)
    prev = (j >= i).astype(np.float32)
    mA = np.concatenate([prev, cur, prev, cur], 1)
    mB = np.concatenate([cur, prev, cur, prev], 1)
    mC = np.concatenate([cur] * 4, 1)
    mD = np.concatenate([prev] * 4, 1)
    mE = [np.concatenate([cur[:, 32 * b:32 * b + 32]] * 16, 1) for b in range(4)]
    masks = np.stack([mA, mB, mC, mD] + mE, 1)
    same = (j // 64) == (i // 64)
    tri = ((j <= i) & same).astype(np.float32)
    mid = np.where(same, ((j % 64) <= 31), False).astype(np.float32)
    TRI = np.zeros((128, 132), np.float32)
    TRI[:, :128] = tri - mid
    jj = np.arange(128)
    TRI[:, 128] = ((jj < 64) & (jj % 64 <= 31))
    TRI[:, 129] = ((jj >= 64) & (jj % 64 <= 31))
    TRI[:, 130] = (jj < 64)
    TRI[:, 131] = (jj >= 64)
    UBD = ((j > i) & same).astype(np.float32)
    mBD = ((j <= i) & same).astype(np.float32)
    return {
        "c_ones": np.ones((128, 128), NPBF),
        "c_ident": np.eye(128, dtype=np.float32).astype(NPBF),
        "c_masks": masks.astype(NPBF),
        "c_tri": TRI,
        "c_ubd": UBD,
        "c_mbd": mBD.astype(NPBF),
        "c_onesf": np.ones((128, 128), np.float32),
    }


class Ctx:
    pass


def load_consts(b, nc, names):
    hc = host_consts()
    out = {}
    for n in names:
        a = hc[n]
        dt = BF16 if a.dtype == NPBF else F32
        if n not in b.cd:
            b.cd[n] = nc.dram_tensor(n, list(a.shape), dt, kind="ExternalInput").ap()
        d = b.cd[n]
        t = b.sb("sb_" + n, a.shape, dt)
        if a.ndim == 2:
            b.dma("sp", t[:, :], d[:, :], writes=[t])
        else:
            b.dma("sp", t[:, :, :], d[:, :, :], writes=[t])
        out[n] = t
    return out


def rmsnorm_T(b, xT, g_sb, hnT, nchunk, ntok, banks, xs_ring, sq_ring, rstd, ones_bf, eps_t, t0=0, h0=0):
    nb = ntok // 512
    for c in range(nchunk):
        xs = xs_ring[c % len(xs_ring)]
        b.dma("sp", xs[:, :ntok], xT[c * 128:(c + 1) * 128, t0:t0 + ntok], writes=[xs])
        sq = sq_ring[c % len(sq_ring)]
        b.act(sq[:, :ntok], xs[:, :ntok], AF.Square, [xs], [sq])
        for n in range(nb):
            b.mm(banks[n], [(banks[n][:, :], ones_bf[:, :], sq[:, n * 512:(n + 1) * 512])],
                 reads=[sq, ones_bf], start=(c == 0), stop=(c == nchunk - 1))
    for n in range(nb):
        b.act(rstd[:, n * 512:(n + 1) * 512], banks[n][:, :], AF.Sqrt, [banks[n], eps_t], [rstd],
              bias=eps_t[:, 0:1], scale=1.0 / (nchunk * 128))
    b.recip(rstd[:, :ntok], rstd[:, :ntok], [rstd], [rstd])
    for c in range(nchunk):
        xs = xs_ring[c % len(xs_ring)]
        b.dma("sp", xs[:, :ntok], xT[c * 128:(c + 1) * 128, t0:t0 + ntok], writes=[xs])
        b.stt(hnT[:, c, h0:h0 + ntok], xs[:, :ntok], g_sb[:, c:c + 1], rstd[:, :ntok], ALU.mult, ALU.mult,
              [xs, g_sb, rstd], [hnT])


def rmsnorm_seg(b, xT, g_sb, hnT, nchunk, nseg, bank_ring, xseg_ring, sq_ring, rstd_ring, ones_bf, eps_t):
    for sg_ in range(nseg):
        xseg = xseg_ring[sg_ % len(xseg_ring)]
        bank = bank_ring[sg_ % len(bank_ring)]
        rstd = rstd_ring[sg_ % len(rstd_ring)]
        for c4 in range(nchunk // 4):
            src = xT[c4 * 512:(c4 + 1) * 512, sg_ * 512:(sg_ + 1) * 512].rearrange("(c p) n -> p c n", p=128)
            b.dma("sp", xseg[:, c4 * 4:(c4 + 1) * 4, :], src, writes=[xseg])
        for c in range(nchunk):
            sq = sq_ring[c % len(sq_ring)]
            b.act(sq[:, :], xseg[:, c, :], AF.Square, [xseg], [sq])
            b.mm(bank, [(bank[:, :], ones_bf[:, :], sq[:, :])], reads=[sq, ones_bf], start=(c == 0), stop=(c == nchunk - 1))
        b.act(rstd[:, :], bank[:, :], AF.Sqrt, [bank, eps_t], [rstd], bias=eps_t[:, 0:1], scale=1.0 / (nchunk * 128))
        b.recip(rstd[:, :], rstd[:, :], [rstd], [rstd])
        for c in range(nchunk):
            b.stt(hnT[:, c, sg_ * 512:(sg_ + 1) * 512], xseg[:, c, :], g_sb[:, c:c + 1], rstd[:, :], ALU.mult, ALU.mult,
                  [xseg, g_sb, rstd], [hnT])


class WStream:
    def __init__(self, b, nchunk, nst=2, nbf=3, name="w"):
        self.b = b
        self.nchunk = nchunk
        self.st = [b.sb("%s_st%d" % (name, i), [128, nchunk, 128], F32) for i in range(nst)]
        self.bf = [b.sb("%s_bf%d" % (name, i), [128, nchunk, 128], BF16) for i in range(nbf)]
        self.i = 0

    def load(self, w_dram, col0, ncols=128, q="sp", cast_eng="pool"):
        b = self.b
        st = self.st[self.i % len(self.st)]
        bf = self.bf[self.i % len(self.bf)]
        self.i += 1
        src = w_dram[:, col0:col0 + ncols].rearrange("(c p) n -> p c n", p=128)
        b.dma(q, st[:, :, :ncols], src, writes=[st])
        b.copy(cast_eng, bf[:, :, :ncols], st[:, :, :ncols], [st], [bf])
        return bf


def proj_fm(b, wbf, mcols, hnT, nchunk, ntok, banks, evac):
    nb = ntok // 512
    for n in range(nb):
        bank = banks[n % len(banks)]
        items = [(bank[:mcols, :], wbf[:, c, :mcols], hnT[:, c, n * 512:(n + 1) * 512]) for c in range(nchunk)]
        b.mm(bank, items, reads=[wbf, hnT])
        evac(n, bank)


NCOL_L0 = 4 * 512 + 2 * 512 + 2 * 384 + 16


def l0_weight_cols(hh):
    cols = []
    GATE = 5136
    for i in range(4):
        hg = 4 * hh + i
        for base in (0, 1024, 2048, GATE):
            cols += list(range(base + hg * 128, base + hg * 128 + 128))
    for i in range(2):
        hg = 2 * hh + i
        cols += list(range(3072 + hg * 128, 3072 + hg * 128 + 128))
        cols += list(range(3584 + hg * 128, 3584 + hg * 128 + 128))
        cols += list(range(GATE + 1024 + hg * 256, GATE + 1024 + hg * 256 + 256))
    for i in range(2):
        hg = 2 * hh + i
        cols += list(range(3584 + hg * 128, 3584 + hg * 128 + 128))
        cols += list(range(4096 + hg * 256, 4096 + hg * 256 + 256))
    cols += list(range(5120, 5136))
    assert len(cols) == NCOL_L0
    return np.array(cols)


def attention_head(b, cx, qT, kT, vT, wg, yout_cb):
    ident, ones_bf, masks = cx.ident, cx.ones_bf, cx.masks
    scale = 128.0 ** -0.5
    Vd = cx.Vd
    for pi, d in enumerate((1, 4, 16)):
        for half in range(2):
            pt = cx.pT8
            for k8 in range(8):
                tix = half * 8 + k8
                if d == 1:
                    src = vT[:, tix * 128:(tix + 1) * 128]
                elif d == 4:
                    r, n_ = tix // 4, tix % 4
                    st = n_ * 512 + r
                    src = vT[:, st:st + 509:4]
                else:
                    r = tix
                    src = vT[:, r:r + 2033:16]
                b.op("pe", lambda e, o=pt[:, k8 * 128:(k8 + 1) * 128], s=src: e.transpose(o, s, ident[:, :]),
                     reads=[vT, ident] if k8 == 0 else (), writes=[pt] if k8 == 0 else (), inc=(k8 == 7))
                if k8 == 7:
                    me = ("pe", b.cnt["pe"], "pe")
                    pt.lw = me
                    pt.rd = []
                    vT.rd.append(me)
            b.copy("dve", Vd[pi][:, half * 8:(half + 1) * 8, :], pt[:, :].rearrange("p (a e) -> p a e", e=128),
                   [pt], [Vd[pi]])
    o3, d3 = cx.o3, cx.d3
    work = []
    for rq in range(4):
        cs = []
        for r4 in range(4):
            r = rq * 4 + r4
            cs.append((kT[:, r:r + 2033:16], qT[:, r:r + 2033:16], 128, Vd[2][:, r, :], (r4 * 128, 128, 1), r4 * 128))
        work.append(dict(kind="p3", mi=2, cs=cs, rq=rq))
    for bk in range(4):
        sb = []
        A = []
        if bk > 0:
            kb = 4 * bk - 1
            A.append((kT[:, kb * 128:(kb + 1) * 128], qT[:, bk * 512:bk * 512 + 128], 128, Vd[0][:, kb, :], (0, 128, 1), 0))
        kb = 4 * bk
        A.append((kT[:, kb * 128:(kb + 1) * 128], qT[:, bk * 512:bk * 512 + 256], 256, Vd[0][:, kb, :], (0, 256, 1), 128))
        kb = 4 * bk + 3
        A.append((kT[:, kb * 128:(kb + 1) * 128], qT[:, bk * 512 + 384:bk * 512 + 512], 128, Vd[0][:, kb, :], (384, 128, 1), 384))
        sb.append((0, A))
        Bq = []
        for q_ in (1, 2):
            kb = 4 * bk + q_
            Bq.append((kT[:, kb * 128:(kb + 1) * 128], qT[:, bk * 512 + q_ * 128:bk * 512 + q_ * 128 + 256], 256,
                       Vd[0][:, kb, :], (q_ * 128, 256, 1), (q_ - 1) * 256))
        sb.append((1, Bq))
        C = []
        Dl = []
        for r in range(4):
            st = bk * 512 + r
            qa = qT[:, st:st + 509:4]
            C.append((kT[:, st:st + 509:4], qa, 128, Vd[1][:, r * 4 + bk, :], (r, 128, 4), r * 128))
            if bk > 0:
                sp_ = (bk - 1) * 512 + r
                Dl.append((kT[:, sp_:sp_ + 509:4], qa, 128, Vd[1][:, r * 4 + bk - 1, :], (r, 128, 4), r * 128))
        sb.append((2, C))
        if bk > 0:
            sb.append((3, Dl))
        for si, (mi, cs) in enumerate(sb):
            work.append(dict(kind="std", mi=mi, cs=cs, bk=bk, first=(si == 0), last=(si == len(sb) - 1)))

    def stage1(w):
        sc = cx.pS[cx.sidx % len(cx.pS)]
        pb = cx.pbuf[cx.sidx % len(cx.pbuf)]
        cx.sidx += 1
        w["pb"] = pb
        cs = w["cs"]
        lo = min(c[5] for c in cs)
        hi = max(c[5] + c[2] for c in cs)
        for ci, c in enumerate(cs):
            kap, qap, ncol, vap, ocs, soff = c
            lastc = (ci == len(cs) - 1)
            b.op("pe", lambda e, o=sc[:, soff:soff + ncol], l=kap, r=qap: e.matmul(o, l, r, start=True, stop=True, skip_group_check=True),
                 reads=[kT, qT] if ci == 0 else (), writes=[sc] if ci == 0 else (), inc=lastc)
        me = ("pe", b.cnt["pe"], "pe")
        sc.lw = me
        sc.rd = []
        kT.rd.append(me)
        qT.rd.append(me)
        b.act(pb[:, lo:hi], sc[:, lo:hi], AF.Exp, [sc], [pb], scale=scale)
        b.tt("pool", pb[:, lo:hi], pb[:, lo:hi], masks[:, w["mi"], lo:hi], ALU.mult, [pb, masks], [pb])

    def stage2(w):
        pb = w["pb"]
        cs = w["cs"]
        O, Dn = cx.pO, cx.pD
        p3 = (w["kind"] == "p3")
        for ci, c in enumerate(cs):
            kap, qap, ncol, vap, ocs, soff = c
            o0, on, ostep = ocs
            osl = slice(o0, o0 + (on - 1) * ostep + 1, ostep)
            if p3:
                st_flag, stop_flag = True, True
            else:
                st_flag = w["first"] and ci == 0
                stop_flag = w["last"] and ci == len(cs) - 1
            lastc = (ci == len(cs) - 1)
            b.op("pe", lambda e, o=O[:, osl], l=vap, r=pb[:, soff:soff + ncol], sf=st_flag, lc=stop_flag:
                 e.matmul(o, l, r, start=sf, stop=lc, skip_group_check=True),
                 reads=[pb, Vd[0], Vd[1], Vd[2]] if ci == 0 else (), writes=[O] if ci == 0 else (), inc=False)
            b.op("pe", lambda e, o=Dn[:, osl], l=ones_bf[:, :], r=pb[:, soff:soff + ncol], sf=st_flag, lc=stop_flag:
                 e.matmul(o, l, r, start=sf, stop=lc, skip_group_check=True),
                 reads=(), writes=[Dn] if ci == 0 else (), inc=lastc)
        me = ("pe", b.cnt["pe"], "pe")
        O.lw = me
        Dn.lw = me
        O.rd = []
        Dn.rd = []
        pb.rd.append(me)
        for v in Vd:
            v.rd.append(me)
        if p3:
            rq = w["rq"]
            dst_o = o3[:, :].rearrange("p (j r) -> p r j", r=16)[:, rq * 4:(rq + 1) * 4, :]
            dst_d = d3[:, :].rearrange("p (j r) -> p r j", r=16)[:, rq * 4:(rq + 1) * 4, :]
            b.copy("act", dst_o, O[:, :].rearrange("p (r j) -> p r j", j=128), [O], [o3])
            b.copy("dve", dst_d, Dn[:, :].rearrange("p (r j) -> p r j", j=128), [Dn], [d3])
        elif w["last"]:
            bk = w["bk"]
            rec = cx.rec
            osum = cx.osum
            yt = cx.ybuf[cx.yidx % len(cx.ybuf)]
            cx.yidx += 1
            b.tt("dve", rec[:, :], Dn[:, :], d3[:, bk * 512:(bk + 1) * 512], ALU.add, [Dn, d3], [rec])
            b.recip(rec[:, :], rec[:, :], [rec], [rec])
            b.tt("dve", osum[:, :], O[:, :], o3[:, bk * 512:(bk + 1) * 512], ALU.add, [O, o3], [osum])
            b.tt("dve", rec[:, :], osum[:, :], rec[:, :], ALU.mult, [osum, rec], [rec])
            pg = cx.pP[bk % 2]
            b.mm(pg, [(pg[:, :], wg[:, c, :], cx.hnT[:, c, bk * 512:(bk + 1) * 512]) for c in range(16)], reads=[wg, cx.hnT])
            gs = cx.gs
            b.act(gs[:, :], pg[:, :], AF.Silu, [pg], [gs])
            b.tt("dve", yt[:, :], rec[:, :], gs[:, :], ALU.mult, [rec, gs], [yt])
            yout_cb(bk, yt)
    stage1(work[0])
    for i in range(len(work)):
        if i + 1 < len(work):
            stage1(work[i + 1])
        stage2(work[i])


def gla_head(b, cx, qTr, kTr, ktok, vtok, logat, dv, fin_cb, nt=16):
    nv = dv // 128
    Sst = cx.Sst
    b.memset("dve", Sst[:, :dv], 0.0, [Sst])
    sk = [0]

    def front1(t):
        k = t % 2
        pF = cx.pF[k]
        E1, enb, edec, qs, ks, kd = cx.E1[k], cx.enb[k], cx.edec[k], cx.qs[k], cx.ks[k], cx.kd[k]
        b.op("pe", lambda e_: e_.matmul(pF[:, 0:132], logat[:, t, :], cx.tri[:, :], start=True, stop=True, skip_group_check=True),
             reads=[logat, cx.tri], writes=[pF], inc=False)
        b.op("pe", lambda e_: e_.matmul(pF[:, 132:260], cx.ubd[:, :], logat[:, t, :], start=True, stop=True, skip_group_check=True),
             reads=[cx.ubd], writes=[pF])
        b.act(E1[:, :132], pF[:, 0:132], AF.Exp, [pF], [E1])
        b.act(enb[:, :], pF[:, 0:128], AF.Exp, [pF], [enb], scale=-1.0)
        b.act(edec[:, :], pF[:, 132:260], AF.Exp, [pF], [edec])
        b.tt("dve", qs[:, :], qTr[:, t * 128:(t + 1) * 128], E1[:, :128], ALU.mult, [qTr, E1], [qs])
        b.tt("dve", ks[:, :], kTr[:, t * 128:(t + 1) * 128], enb[:, :], ALU.mult, [kTr, enb], [ks])
        b.tt("pool", kd[:, :], ktok[:, t, :], edec[:, :], ALU.mult, [ktok, edec], [kd])

    def front2(t):
        k = t % 2
        pF = cx.pF[k]
        qs, ks, AT = cx.qs[k], cx.ks[k], cx.AT[k]
        b.op("pe", lambda e_: e_.matmul(pF[:, 260:388], ks[:, :], qs[:, :], start=True, stop=True, skip_group_check=True),
             reads=[ks, qs], writes=[pF])
        b.tt("dve", AT[:, :], pF[:, 260:388], cx.mbd[:, :], ALU.mult, [pF, cx.mbd], [AT])

    def back(t):
        k = t % 2
        E1, qs, kd, AT = cx.E1[k], cx.qs[k], cx.kd[k], cx.AT[k]
        pOo = cx.pGo
        for cc in range(2):
            r0 = 64 * cc
            Smid = cx.Smid[sk[0] % 2]
            sk[0] += 1
            b.act(Smid[:, :dv], Sst[:, :dv], AF.Copy, [Sst, E1], [Smid], scale=E1[:, 128 + cc:129 + cc])
            pS_ = cx.pGs
            b.mm(pS_, [(pS_[:, :dv], kd[r0:r0 + 64, :], vtok[r0:r0 + 64, t, :dv])], reads=[kd, vtok])
            b.stt(Sst[:, :dv], Sst[:, :dv], E1[:, 130 + cc:131 + cc], pS_[:, :dv], ALU.mult, ALU.add,
                  [Sst, E1, pS_], [Sst])
            for vt in range(nv):
                col = (t % 4) * 128 + r0
                items = [
                    (pOo[vt][:, col:col + 64], vtok[r0:r0 + 64, t, vt * 128:(vt + 1) * 128], AT[r0:r0 + 64, r0:r0 + 64]),
                    (pOo[vt][:, col:col + 64], Smid[:, vt * 128:(vt + 1) * 128], qs[:, r0:r0 + 64]),
                ]
                b.mm(pOo[vt], items, reads=[vtok, AT, Smid, qs], skip_group_check=True)
        if t % 4 == 3:
            fin_cb(t // 4)
    front1(0)
    front2(0)
    for t in range(nt):
        if t + 1 < nt:
            front1(t + 1)
        back(t)
        if t + 1 < nt:
            front2(t + 1)


def phase1(nc, b, io):
    xT, wc, ng, wgate, bgate, gng, yT = (io[k] for k in ("xT", "wc1", "ng0", "wgate", "bgate", "gng", "y0"))
    cs = load_consts(b, nc, ["c_ones", "c_ident", "c_masks", "c_tri", "c_ubd", "c_mbd", "c_onesf"])
    cx = Ctx()
    cx.ones_bf, cx.ident, cx.masks = cs["c_ones"], cs["c_ident"], cs["c_masks"]
    cx.tri, cx.ubd, cx.mbd, onesf = cs["c_tri"], cs["c_ubd"], cs["c_mbd"], cs["c_onesf"]
    g_sb = b.sb("g_sb", [128, 16], F32)
    b.dma("sp", g_sb[:, :], ng[:, :], writes=[g_sb])
    wg_sb = b.sb("wg_sb", [16, 256], F32)
    b.dma("sp", wg_sb[:, :], wgate[:, :], writes=[wg_sb])
    bg_sb = b.sb("bg_sb", [1, 256], F32)
    b.dma("sp", bg_sb[:, :], bgate[:, :], writes=[bg_sb])
    gn_sb = b.sb("gn_sb", [128, 4], F32)
    b.dma("sp", gn_sb[:, :], gng[:, :], writes=[gn_sb])
    eps_t = b.sb("eps_t", [128, 1], F32)
    b.memset("dve", eps_t[:, :], EPS, [eps_t])
    one_t = b.sb("one_t", [128, 1], F32)
    b.memset("dve", one_t[:, :], 1.0, [one_t])
    P = [b.ps("P%d" % i, [128, 512]) for i in range(7)]
    P7 = b.ps("P7", [128, 1024], BF16)
    cx.pP = [P[0], P[1]]
    cx.pS = [P[2], P[3]]
    cx.pO, cx.pD = P[5], P[6]
    cx.pT8 = P7
    cx.pF = [P[2], P[3]]
    cx.pq = P[4]
    cx.pGo = [P[5], P[6]]
    cx.pGs = P[1]
    hnT = b.sb("hnT", [128, 16, S], BF16)
    cx.hnT = hnT
    ws = WStream(b, 16, nst=2, nbf=8)
    arena = b.sb("arena", [128, 18432], F32)
    off = 0
    xseg_ring = []
    for i in range(2):
        t, off = b.carve(arena, off, [128, 16, 512], F32, "xseg%d" % i)
        xseg_ring.append(t)
    sq_ring = []
    for i in range(2):
        t, off = b.carve(arena, off, [128, 512], BF16, "sq%d" % i)
        sq_ring.append(t)
    rstd_ring = []
    for i in range(2):
        t, off = b.carve(arena, off, [128, 512], F32, "rstd%d" % i)
        rstd_ring.append(t)
    rmsnorm_seg(b, xT, g_sb, hnT, 16, 4, P[0:2], xseg_ring, sq_ring, rstd_ring, cx.ones_bf, eps_t)
    b.barrier()
    b.flush("norm0")
    off = 0
    qT, off = b.carve(arena, off, [128, S], BF16, "qT")
    kT, off = b.carve(arena, off, [128, S], BF16, "kT")
    vT, off = b.carve(arena, off, [128, S], BF16, "vT")
    cx.Vd = []
    for i in range(3):
        t, off = b.carve(arena, off, [128, 16, 128], BF16, "Vd%d" % i)
        cx.Vd.append(t)
    cx.pbuf = []
    for i in range(3):
        t, off = b.carve(arena, off, [128, 512], BF16, "pb%d" % i)
        cx.pbuf.append(t)
    cx.o3, off = b.carve(arena, off, [128, S], F32, "o3")
    cx.d3, off = b.carve(arena, off, [128, S], F32, "d3")
    cx.osum, off = b.carve(arena, off, [128, 512], F32, "osum")
    cx.rec, off = b.carve(arena, off, [128, 512], F32, "rec")
    cx.gs, off = b.carve(arena, off, [128, 512], F32, "gs")
    cx.ybuf = []
    for i in range(2):
        t, off = b.carve(arena, off, [128, 512], BF16, "yb%d" % i)
        cx.ybuf.append(t)
    cx.sidx = 0
    cx.yidx = 0
    outs = []
    hw = {0: [ws.load(wc, gi * 128) for gi in range(4)]}
    for i in range(4):
        for gi, dst in enumerate((qT, kT, vT)):
            w = hw[i][gi]

            def ev(n, bank, dst=dst):
                b.copy("act", dst[:, n * 512:(n + 1) * 512], bank[:, :], [bank], [dst])
            proj_fm(b, w, 128, hnT, 16, S, cx.pP, ev)
        wg = hw[i][3]
        if i + 1 < 4:
            hw[i + 1] = [ws.load(wc, (i + 1) * 512 + gi * 128) for gi in range(4)]

        def yout(bk, yt, i=i):
            outs.append(b.dma("sp", yT[i * 128:(i + 1) * 128, bk * 512:(bk + 1) * 512], yt[:, :], reads=[yt]))
        attention_head(b, cx, qT, kT, vT, wg, yout)
        b.flush("attn%d" % i)
    b.barrier()
    off = 0
    qTr, off = b.carve(arena, off, [128, S], F32, "qTr")
    kTr, off = b.carve(arena, off, [128, S], F32, "kTr")
    ktok, off = b.carve(arena, off, [128, 16, 128], F32, "ktok")
    vtok, off = b.carve(arena, off, [128, 16, 256], BF16, "vtok")
    logat, off = b.carve(arena, off, [128, 16, 128], F32, "logat")
    glT, off = b.carve(arena, off, [16, S], F32, "glT")
    cx.Sst, off = b.carve(arena, off, [128, 256], F32, "Sst")
    def ring2(nm, sh, dt):
        nonlocal off
        r = []
        for i_ in range(2):
            t_, off = b.carve(arena, off, sh, dt, "%s%d" % (nm, i_))
            r.append(t_)
        return r
    cx.Smid = ring2("Smid", [128, 256], BF16)
    cx.E1 = ring2("E1", [128, 132], F32)
    cx.enb = ring2("enb", [128, 128], F32)
    cx.edec = ring2("edec", [128, 128], F32)
    cx.qs = ring2("qs", [128, 128], BF16)
    cx.ks = ring2("ks", [128, 128], BF16)
    cx.kd = ring2("kd", [128, 128], BF16)
    cx.AT = ring2("AT", [128, 128], BF16)
    obS, off = b.carve(arena, off, [128, 2, 512], F32, "obS")
    sqS, off = b.carve(arena, off, [128, 2, 512], BF16, "sqS")
    rs2, off = b.carve(arena, off, [128, 512], F32, "rs2")
    gs2, off = b.carve(arena, off, [128, 512], F32, "gs2")
    tmpE, off = b.carve(arena, off, [128, 512], F32, "tmpE")
    ybs = []
    for i in range(2):
        t, off = b.carve(arena, off, [128, 512], BF16, "ybs%d" % i)
        ybs.append(t)
    assert off <= 18432, off
    wl = ws.load(wc, 4 * 512 + 2 * 512 + 2 * 384, ncols=16)

    def ev_gl(n, bank):
        b.copy("act", glT[:, n * 512:(n + 1) * 512], bank[:16, :], [bank], [glT])
    proj_fm(b, wl, 16, hnT, 16, S, cx.pP, ev_gl)
    yi = [0]
    for i in range(2):
        base = 4 * 512 + i * 512
        w = ws.load(wc, base)

        def ev_q(n, bank):
            b.act(qTr[:, n * 512:(n + 1) * 512], bank[:, :], AF.Copy, [bank], [qTr], scale=128.0 ** -0.5)
        proj_fm(b, w, 128, hnT, 16, S, cx.pP, ev_q)
        w = ws.load(wc, base + 128)

        def ev_k(n, bank):
            b.copy("act", kTr[:, n * 512:(n + 1) * 512], bank[:, :], [bank], [kTr])
        proj_fm(b, w, 128, hnT, 16, S, cx.pP, ev_k)
        tb = 4 * 512 + 2 * 512 + i * 384
        for gi in range(3):
            w = ws.load(wc, tb + gi * 128)
            for n in range(4):
                bank = cx.pP[n % 2]
                for tt_ in range(4):
                    t = n * 4 + tt_
                    items = [(bank[:, tt_ * 128:(tt_ + 1) * 128], hnT[:, c, t * 128:(t + 1) * 128], w[:, c, :]) for c in range(16)]
                    b.mm(bank, items, reads=[w, hnT], skip_group_check=True)
                src = bank[:, :].rearrange("p (a e) -> p a e", e=128)
                if gi == 0:
                    b.copy("act", ktok[:, n * 4:(n + 1) * 4, :], src, [bank], [ktok])
                else:
                    b.copy("act", vtok[:, n * 4:(n + 1) * 4, (gi - 1) * 128:gi * 128], src, [bank], [vtok])
        for n in range(4):
            bank = cx.pP[n % 2]
            for tt_ in range(4):
                t = n * 4 + tt_
                items = [(bank[:, tt_ * 128:(tt_ + 1) * 128], glT[0:16, t * 128:(t + 1) * 128], wg_sb[0:16, i * 128:(i + 1) * 128]),
                         (bank[:, tt_ * 128:(tt_ + 1) * 128], onesf[0:1, 0:128], bg_sb[0:1, i * 128:(i + 1) * 128])]
                b.mm(bank, items, reads=[glT, wg_sb, onesf, bg_sb], skip_group_check=True)
            b.act(tmpE[:, :], bank[:, :], AF.Exp, [bank], [tmpE], scale=-1.0)
            b.act(tmpE[:, :], tmpE[:, :], AF.Ln, [tmpE, one_t], [tmpE], bias=one_t[:, 0:1])
            b.ts("dve", logat[:, n * 4:(n + 1) * 4, :], tmpE[:, :].rearrange("p (a e) -> p a e", e=128), -1.0 / 16.0, None,
                 ALU.mult, None, [tmpE], [logat])
        b.flush("glaproj%d" % i)
        wga = ws.load(wc, base + 256)
        wgb = ws.load(wc, base + 384)
        wgs = [wga, wgb]

        def fin(n, i=i, wgs=wgs):
            for vt in range(2):
                b.copy("act", obS[:, vt, :], cx.pGo[vt][:, :], [cx.pGo[vt]], [obS])
            b.act(sqS[:, :, :], obS[:, :, :], AF.Square, [obS], [sqS])
            pq = cx.pq
            b.mm(pq, [(pq[:, :], cx.ones_bf[:, :], sqS[:, vt, :]) for vt in range(2)], reads=[sqS, cx.ones_bf])
            b.act(rs2[:, :], pq[:, :], AF.Sqrt, [pq, eps_t], [rs2], bias=eps_t[:, 0:1], scale=1.0 / 256.0)
            b.recip(rs2[:, :], rs2[:, :], [rs2], [rs2])
            for vt in range(2):
                pg = cx.pP[vt]
                b.mm(pg, [(pg[:, :], wgs[vt][:, c, :], hnT[:, c, n * 512:(n + 1) * 512]) for c in range(16)], reads=[wgs[vt], hnT])
                b.act(gs2[:, :], pg[:, :], AF.Silu, [pg], [gs2])
                b.tt("dve", obS[:, vt, :], obS[:, vt, :], rs2[:, :], ALU.mult, [obS, rs2], [obS])
                yb_ = ybs[yi[0] % 2]
                yi[0] += 1
                b.stt(yb_[:, :], obS[:, vt, :], gn_sb[:, i * 2 + vt:i * 2 + vt + 1], gs2[:, :], ALU.mult, ALU.mult,
                      [obS, gn_sb, gs2], [yb_])
                r0 = 512 + i * 256 + vt * 128
                outs.append(b.dma("sp", yT[r0:r0 + 128, n * 512:(n + 1) * 512], yb_[:, :], reads=[yb_]))
        gla_head(b, cx, qTr, kTr, ktok, vtok, logat, 256, fin)
        b.flush("glarec%d" % i)
    return outs


def l1_inputs(inp, bidx, hh):
    x = np.asarray(inp["x"])[bidx]
    cols = l0_weight_cols(hh)
    d = {
        "xT": np.ascontiguousarray(x.T),
        "wc1": np.ascontiguousarray(np.asarray(inp["ab_w_in"])[0][:, cols]),
        "ng0": np.ascontiguousarray(np.asarray(inp["norm_g"])[0].reshape(16, 128).T),
        "wgate": np.ascontiguousarray(np.asarray(inp["gla_w_gate"])[0][:, hh * 256:(hh + 1) * 256]),
        "bgate": np.ascontiguousarray(np.asarray(inp["gla_b_gate"])[0][hh * 256:(hh + 1) * 256].reshape(1, 256)),
        "gng": np.ascontiguousarray(np.asarray(inp["gla_norm_g"])[0][hh * 512:(hh + 1) * 512].reshape(4, 128).T),
    }
    return d


NT2 = 1024


def phaseO(nc, b, io, final, glu):
    resT, wo, ng, sel_d = io["resT"], io["wo"], io["ng"], io["sel"]
    if glu:
        wglu, bglu = io["wglu"], io["bglu"]
    if final:
        outT = io["outT"]
    else:
        hTd, hnTd = io["hTd"], io["hnTd"]
    sel = b.sb("sel", [128, 2], F32)
    b.dma("sp", sel[:, :], sel_d[:, :], writes=[sel])
    lda = [b.sb("lda%d" % i, [128, NT2], BF16) for i in range(2)]
    ldb = [b.sb("ldb%d" % i, [128, NT2], BF16) for i in range(2)]
    lk = [0]

    def load_sel(dst_ap, dst_t, srcs):
        A, Bv = srcs
        ta_, tb_ = lda[lk[0] % 2], ldb[lk[0] % 2]
        lk[0] += 1
        b.dma("sp", ta_[:, :], A, writes=[ta_])
        b.dma("sp", tb_[:, :], Bv, writes=[tb_])
        b.ts("dve", ta_[:, :], ta_[:, :], sel[:, 0:1], None, ALU.mult, None, [ta_, sel], [ta_])
        b.stt(dst_ap, tb_[:, :], sel[:, 1:2], ta_[:, :], ALU.mult, ALU.add, [tb_, sel, ta_], [dst_t])
    cs = load_consts(b, nc, ["c_ones"])
    ones_bf = cs["c_ones"]
    g_sb = b.sb("g_sb", [128, 16], F32)
    b.dma("sp", g_sb[:, :], ng[:, :], writes=[g_sb])
    eps_t = b.sb("eps_t", [128, 1], F32)
    b.memset("dve", eps_t[:, :], EPS, [eps_t])
    P = [b.ps("P%d" % i, [128, 512]) for i in range(8)]
    ysb = b.sb("ysb", [128, 16, NT2], BF16)
    for c in range(16):
        load_sel(ysb[:, c, :], ysb, io["ysrc"](c))
    hT = b.sb("hT_sb", [128, 16, NT2], F32)
    wst = [b.sb("wst%d" % i, [128, 8, 512], F32) for i in range(2)]
    wbf = [b.sb("wbf%d" % i, [128, 16, 512], BF16) for i in range(2)]
    wk = [0, 0]

    def load_w(wd, nchunk, col0):
        bf = wbf[wk[1] % 2]
        wk[1] += 1
        for hf in range(nchunk // 8):
            st = wst[wk[0] % 2]
            ce = "pool" if wk[0] % 2 == 0 else "dve"
            wk[0] += 1
            src = wd[hf * 1024:(hf + 1) * 1024, col0:col0 + 512].rearrange("(c p) n -> p c n", p=128)
            b.dma("sp", st[:, :, :], src, writes=[st])
            b.copy(ce, bf[:, hf * 8:(hf + 1) * 8, :], st[:, :, :], [st], [bf])
        return bf
    outs = []
    if glu:
        bg_sb = b.sb("bg_sb", [128, 8], F32)
        b.dma("sp", bg_sb[:, :], bglu[:, :], writes=[bg_sb])
        ycs = b.sb("ycs", [128, 8, NT2], BF16)
        sgs = [b.sb("sgs%d" % i, [128, NT2], BF16) for i in range(2)]
        sig = [b.sb("sig%d" % i, [128, 512], F32) for i in range(2)]
        for mg in range(2):
            w = load_w(wglu, 8, mg * 512)
            for mi in range(4):
                m = mg * 4 + mi
                sg = sgs[m % 2]
                load_sel(sg[:, :], sg, io["sgsrc"](m))
                for n in range(2):
                    bank = P[(m * 2 + n) % 4]
                    sg_ = sig[(m * 2 + n) % 2]
                    b.mm(bank, [(bank[:, :], w[:, c, mi * 128:(mi + 1) * 128], ysb[:, c, n * 512:(n + 1) * 512]) for c in range(8)],
                         reads=[w, ysb])
                    b.act(sg_[:, :], bank[:, :], AF.Sigmoid, [bank, bg_sb], [sg_], bias=bg_sb[:, m:m + 1])
                    b.tt("dve", sg_[:, :], sg_[:, :], ysb[:, m, n * 512:(n + 1) * 512], ALU.mult, [sg_, ysb], [sg_])
                    b.tt("dve", ycs[:, m, n * 512:(n + 1) * 512], sg_[:, :], sg[:, n * 512:(n + 1) * 512], ALU.mult, [sg_, sg], [ycs])
    b.flush("ldglu")
    xs_ring = [b.sb("xs%d" % i, [128, NT2], F32) for i in range(2)]
    for p in range(4):
        w = load_w(wo, 16, p * 512)
        for mi in range(4):
            m = p * 4 + mi
            xs = xs_ring[m % 2]
            b.dma("sp", xs[:, :], resT[m * 128:(m + 1) * 128, :], writes=[xs])
            for n in range(2):
                bank = P[(m * 2 + n) % 8]
                items = []
                for c in range(16):
                    if glu and c < 8:
                        rhs = ycs[:, c, n * 512:(n + 1) * 512]
                    else:
                        rhs = ysb[:, c, n * 512:(n + 1) * 512]
                    items.append((bank[:, :], w[:, c, mi * 128:(mi + 1) * 128], rhs))
                b.mm(bank, items, reads=[w, ysb] + ([ycs] if glu else []))
                b.tt("dve", hT[:, m, n * 512:(n + 1) * 512], bank[:, :], xs[:, n * 512:(n + 1) * 512], ALU.add, [bank, xs], [hT])
            if not final:
                outs.append(b.dma("sp", hTd[m * 128:(m + 1) * 128, :], hT[:, m, :], reads=[hT]))
    b.flush("oproj")
    b.barrier()
    fl0 = T(wst[0].t[:, :, :].rearrange("p a b -> p (a b)"), "fl0")
    fl1 = T(wst[1].t[:, :, :].rearrange("p a b -> p (a b)"), "fl1")
    off = 0
    sq_ring = []
    for i in range(2):
        t, off = b.carve(fl0, off, [128, NT2], BF16, "sq%d" % i)
        sq_ring.append(t)
    rstd, off = b.carve(fl0, off, [128, NT2], F32, "rstd")
    for c in range(16):
        sq = sq_ring[c % 2]
        b.act(sq[:, :], hT[:, c, :], AF.Square, [hT], [sq])
        for n in range(2):
            b.mm(P[4 + n], [(P[4 + n][:, :], ones_bf[:, :], sq[:, n * 512:(n + 1) * 512])], reads=[sq, ones_bf],
                 start=(c == 0), stop=(c == 15))
    for n in range(2):
        b.act(rstd[:, n * 512:(n + 1) * 512], P[4 + n][:, :], AF.Sqrt, [P[4 + n], eps_t], [rstd],
              bias=eps_t[:, 0:1], scale=1.0 / D)
    b.recip(rstd[:, :], rstd[:, :], [rstd], [rstd])
    ob = []
    off = 0
    for i in range(2):
        t, off = b.carve(fl1, off, [128, NT2], F32 if final else BF16, "ob%d" % i)
        ob.append(t)
    for c in range(16):
        o = ob[c % 2]
        b.stt(o[:, :], hT[:, c, :], g_sb[:, c:c + 1], rstd[:, :], ALU.mult, ALU.mult, [hT, g_sb, rstd], [o])
        dst = outT if final else hnTd
        outs.append(b.dma("sp", dst[c * 128:(c + 1) * 128, :], o[:, :], reads=[o]))
    return outs
NCOL_L1 = 512 + 512 + 4 * (384 + 256)
TWO_PI = 6.283185307179586


def l1_weight_cols(hh):
    cols = list(range(512 * hh, 512 * hh + 512))
    cols += list(range(4096 + 512 * hh, 4096 + 512 * hh + 512))
    for i in range(4):
        hg = 4 * hh + i
        cols += list(range(1024 + hg * 128, 1024 + hg * 128 + 128))
        cols += list(range(2048 + hg * 128, 2048 + hg * 128 + 128))
        cols += list(range(5120 + hg * 128, 5120 + hg * 128 + 128))
    for i in range(4):
        hg = 4 * hh + i
        cols += list(range(2048 + hg * 128, 2048 + hg * 128 + 128))
        cols += list(range(3072 + hg * 128, 3072 + hg * 128 + 128))
    assert len(cols) == NCOL_L1
    return np.array(cols)


def l3_inputs(inp, hh):
    g0 = 32 * hh
    lam_re = np.asarray(inp["s5_lam_re"])[0][g0:g0 + 32]
    lam_im = np.asarray(inp["s5_lam_im"])[0][g0:g0 + 32]
    logdt = np.asarray(inp["s5_log_dt"])[0][g0:g0 + 32]
    b_re = np.asarray(inp["s5_b_re"])[0][g0:g0 + 32]
    b_im = np.asarray(inp["s5_b_im"])[0][g0:g0 + 32]
    c_re = np.asarray(inp["s5_c_re"])[0][g0:g0 + 32]
    c_im = np.asarray(inp["s5_c_im"])[0][g0:g0 + 32]
    dsk = np.asarray(inp["s5_d"])[0][g0:g0 + 32]
    dup = lambda a: np.ascontiguousarray(np.concatenate([a.T, a.T], 0))
    BA = np.zeros((128, 4, 4, 128), np.float32)
    for gl in range(32):
        ut, q = gl // 8, gl % 8
        r0 = 32 * (q // 2) + 16 * (q % 2)
        slot = (q % 2) if q // 2 < 3 else 2 + (q % 2)
        BA[r0:r0 + 16, ut, slot, 0:64] = b_re[gl].T
        BA[r0:r0 + 16, ut, slot, 64:128] = b_im[gl].T
    cT = lambda c: np.ascontiguousarray(np.concatenate([c.transpose(2, 0, 1)] * 2, 0))
    gam = np.asarray(inp["hgrn_gamma"])[:, 512 * hh:512 * hh + 512]
    d = {
        "wc3": np.ascontiguousarray(np.asarray(inp["cd_w_in"])[0][:, l1_weight_cols(hh)]),
        "lamre": dup(lam_re), "lamim": dup(lam_im),
        "logdt": np.ascontiguousarray(np.broadcast_to(logdt[None, :], (128, 32))),
        "cre": cT(c_re), "cim": cT(c_im), "BA": BA,
        "dsk": np.ascontiguousarray(dsk.reshape(4, 128).T),
        "ttf": np.ascontiguousarray(np.broadcast_to(np.arange(2048, dtype=np.float32)[None, :], (128, 2048))),
        "gamc": np.ascontiguousarray(np.stack([gam[0].reshape(4, 128).T, gam[1].reshape(4, 128).T], 1)),
        "gamr": np.ascontiguousarray(gam.reshape(1, 1024)),
        "hng": np.ascontiguousarray(np.asarray(inp["hgrn_norm_g"])[0][512 * hh:512 * hh + 512].reshape(4, 128).T),
    }
    return d


def phase3(nc, b, io):
    wc = io["wc3"]
    lamre_d, lamim_d, logdt_d, cre_d, cim_d, BA_d, dsk_d, ttf_d, gamc_d, gamr_d, hng_d = (
        io[k] for k in ("lamre", "lamim", "logdt", "cre", "cim", "BA", "dsk", "ttf", "gamc", "gamr", "hng"))
    ysT, sgT, ydT = io["ys"], io["sg"], io["yd"]
    cs = load_consts(b, nc, ["c_ones", "c_tri", "c_ubd", "c_mbd", "c_onesf"])
    cx = Ctx()
    cx.ones_bf = cs["c_ones"]
    cx.tri, cx.ubd, cx.mbd, onesf = cs["c_tri"], cs["c_ubd"], cs["c_mbd"], cs["c_onesf"]
    outs = []

    def ld(name, dap, sh, dt=F32):
        t = b.sb("s_" + name, sh, dt)
        idx = tuple(slice(None) for _ in sh)
        b.dma("sp", t[idx], dap[idx], writes=[t])
        return t
    lamre, lamim, logdt = ld("lamre", lamre_d, [128, 32]), ld("lamim", lamim_d, [128, 32]), ld("logdt", logdt_d, [128, 32])
    cre, cim = ld("cre", cre_d, [128, 32, 16]), ld("cim", cim_d, [128, 32, 16])
    dsk, ttf = ld("dsk", dsk_d, [128, 4]), ld("ttf", ttf_d, [128, S])
    gamc, hng = ld("gamc", gamc_d, [128, 2, 4]), ld("hng", hng_d, [128, 4])
    gamr = ld("gamr", gamr_d, [1, 1024])
    eps_t = b.sb("eps_t", [128, 1], F32)
    b.memset("dve", eps_t[:, :], EPS, [eps_t])
    P = [b.ps("P%d" % i, [128, 512]) for i in range(8)]
    hnT = b.sb("hnT_sb", [128, 16, S], BF16)
    cx.hnT = hnT
    for c in range(16):
        for (c0, ap_) in io["hnsrc"](c):
            b.dma("sp", hnT[:, c, c0:c0 + ap_.shape[1]], ap_, writes=[hnT])
    ws = WStream(b, 16, nst=2, nbf=2)
    arena = b.sb("arena", [128, 17408], F32)
    sm = lambda n, sh=(128, 32), dt=F32: b.sb(n, list(sh), dt)
    dt_, rho, th = sm("dt_"), sm("rho"), sm("th")
    b.act(dt_[:, :], logdt[:, :], AF.Exp, [logdt], [dt_])
    t1 = sm("t1")
    b.tt("dve", t1[:, :], lamre[:, :], dt_[:, :], ALU.mult, [lamre, dt_], [t1])
    b.act(rho[:, :], t1[:, :], AF.Exp, [t1], [rho])
    b.tt("dve", th[:, :], lamim[:, :], dt_[:, :], ALU.mult, [lamim, dt_], [th])
    thn = sm("thn")
    b.ts("dve", thn[0:64, :], th[0:64, :], 1.0 / TWO_PI, None, ALU.mult, None, [th], [thn])
    b.ts("dve", thn[64:128, :], th[64:128, :], -1.0 / TWO_PI, None, ALU.mult, None, [th], [thn])
    un, fr, sa, ca = sm("un"), sm("fr"), sm("sa"), sm("ca")
    q25 = sm("q25", (128, 1))
    b.memset("dve", q25[:, :], 0.25, [q25])
    MAGIC = 12582912.0
    b.ts("dve", un[:, :], th[:, :], 1.0 / TWO_PI, None, ALU.mult, None, [th], [un])
    b.ts("dve", fr[:, :], un[:, :], MAGIC, -MAGIC, ALU.add, ALU.add, [un], [fr])
    b.tt("dve", sa[:, :], un[:, :], fr[:, :], ALU.subtract, [un, fr], [sa])
    b.ts("dve", un[:, :], un[:, :], 0.25, None, ALU.add, None, [un], [un])
    b.ts("dve", fr[:, :], un[:, :], MAGIC, -MAGIC, ALU.add, ALU.add, [un], [fr])
    b.tt("dve", ca[:, :], un[:, :], fr[:, :], ALU.subtract, [un, fr], [ca])
    sn, cn = sm("sn"), sm("cn")
    b.act(sn[:, :], sa[:, :], AF.Sin, [sa], [sn], scale=TWO_PI)
    b.act(cn[:, :], ca[:, :], AF.Sin, [ca], [cn], scale=TWO_PI)
    cD, sD, nsD = sm("cD"), sm("sD"), sm("nsD")
    u5, f5, a5 = sm("u5"), sm("f5"), sm("a5")
    b.ts("dve", u5[:, :], thn[:, :], 512.0, None, ALU.mult, None, [thn], [u5])
    b.ts("dve", f5[:, :], u5[:, :], MAGIC, -MAGIC, ALU.add, ALU.add, [u5], [f5])
    b.tt("dve", a5[:, :], u5[:, :], f5[:, :], ALU.subtract, [u5, f5], [a5])
    b.act(sD[:, :], a5[:, :], AF.Sin, [a5], [sD], scale=TWO_PI)
    b.ts("dve", u5[:, :], u5[:, :], 0.25, None, ALU.add, None, [u5], [u5])
    b.ts("dve", f5[:, :], u5[:, :], MAGIC, -MAGIC, ALU.add, ALU.add, [u5], [f5])
    b.tt("dve", a5[:, :], u5[:, :], f5[:, :], ALU.subtract, [u5, f5], [a5])
    b.act(cD[:, :], a5[:, :], AF.Sin, [a5], [cD], scale=TWO_PI)
    b.ts("dve", nsD[:, :], sD[:, :], -1.0, None, ALU.mult, None, [sD], [nsD])
    are, aim = sm("are"), sm("aim")
    b.tt("dve", are[:, :], rho[:, :], cn[:, :], ALU.mult, [rho, cn], [are])
    b.tt("dve", aim[:, :], rho[:, :], sn[:, :], ALU.mult, [rho, sn], [aim])
    nr, inv, t2, t3, core_, coim = sm("nr"), sm("inv"), sm("t2"), sm("t3"), sm("core_"), sm("coim")
    b.ts("dve", nr[:, :], are[:, :], -1.0, None, ALU.add, None, [are], [nr])
    b.tt("dve", t2[:, :], lamre[:, :], lamre[:, :], ALU.mult, [lamre], [t2])
    b.tt("dve", t3[:, :], lamim[:, :], lamim[:, :], ALU.mult, [lamim], [t3])
    b.tt("dve", t2[:, :], t2[:, :], t3[:, :], ALU.add, [t2, t3], [t2])
    b.recip(inv[:, :], t2[:, :], [t2], [inv])
    b.tt("dve", t2[:, :], nr[:, :], lamre[:, :], ALU.mult, [nr, lamre], [t2])
    b.tt("dve", t3[:, :], aim[:, :], lamim[:, :], ALU.mult, [aim, lamim], [t3])
    b.tt("dve", t2[:, :], t2[:, :], t3[:, :], ALU.add, [t2, t3], [t2])
    b.tt("dve", core_[:, :], t2[:, :], inv[:, :], ALU.mult, [t2, inv], [core_])
    b.tt("dve", t2[:, :], aim[:, :], lamre[:, :], ALU.mult, [aim, lamre], [t2])
    b.tt("dve", t3[:, :], nr[:, :], lamim[:, :], ALU.mult, [nr, lamim], [t3])
    b.tt("dve", t2[:, :], t2[:, :], t3[:, :], ALU.subtract, [t2, t3], [t2])
    b.tt("dve", coim[:, :], t2[:, :], inv[:, :], ALU.mult, [t2, inv], [coim])
    Ya, Yb, Yc, Yd = sm("Ya"), sm("Yb"), sm("Yc"), sm("Yd")
    top, bot = slice(0, 64), slice(64, 128)
    b.copy("dve", Ya[top, :], core_[top, :], [core_], [Ya])
    b.ts("dve", Ya[bot, :], coim[bot, :], -1.0, None, ALU.mult, None, [coim], [Ya])
    b.ts("dve", Yb[top, :], coim[top, :], -1.0, None, ALU.mult, None, [coim], [Yb])
    b.ts("dve", Yb[bot, :], core_[bot, :], -1.0, None, ALU.mult, None, [core_], [Yb])
    b.ts("dve", Yc[top, :], coim[top, :], -1.0, None, ALU.mult, None, [coim], [Yc])
    b.copy("dve", Yc[bot, :], core_[bot, :], [core_], [Yc])
    b.ts("dve", Yd[top, :], core_[top, :], -1.0, None, ALU.mult, None, [core_], [Yd])
    b.ts("dve", Yd[bot, :], coim[bot, :], -1.0, None, ALU.mult, None, [coim], [Yd])
    L1 = b.sb("L1", [128, 32, 16], F32)
    L2 = b.sb("L2", [128, 32, 16], F32)
    tmpT = ws.st[1]
    tmpL = tmpT[:, 0:4, :].rearrange("p a (g h) -> p (a g) h", h=16)
    bc = lambda t_: t_[:, :].unsqueeze(2).to_broadcast([128, 32, 16])
    for (L, Y0, Y1) in ((L1, Ya, Yb), (L2, Yc, Yd)):
        b.tt("dve", L[:, :, :], cre[:, :, :], bc(Y0), ALU.mult, [cre, Y0], [L])
        b.tt("dve", tmpL, cim[:, :, :], bc(Y1), ALU.mult, [cim, Y1], [tmpT])
        b.tt("dve", L[:, :, :], L[:, :, :], tmpL, ALU.add, [L, tmpT], [L])
    LP1 = b.sb("LP1", [128, 32, 128], BF16)
    LP2 = b.sb("LP2", [128, 32, 128], BF16)
    b.memset("pool", LP1[:, :, :], 0.0, [LP1])
    b.memset("pool", LP2[:, :, :], 0.0, [LP2])
    for q in range(8):
        b.copy("dve", LP1[:, q:32:8, 16 * q:16 * q + 16], L1[:, q:32:8, :], [L1], [LP1])
        b.copy("dve", LP2[:, q:32:8, 16 * q:16 * q + 16], L2[:, q:32:8, :], [L2], [LP2])
    BAb = b.sb("BAb", [128, 4, 4, 128], BF16)
    b.dma("sp", ws.st[0][:, :, :], BA_d.rearrange("p a b c -> p (a b) c"), writes=[ws.st[0]])
    b.copy("pool", BAb[:, :, :, :].rearrange("p a b c -> p (a b) c"), ws.st[0][:, :, :], [ws.st[0]], [BAb])
    b.flush("s5prep")
    off = 0
    uT_all, off = b.carve(arena, off, [128, 4, S], BF16, "uT")
    uTs = [T(uT_all.t[:, ut, :], "uT%d" % ut) for ut in range(4)]
    for ut in range(4):
        w = ws.load(wc, ut * 128)

        def ev_u(n, bank, ut=ut):
            b.copy("act", uTs[ut][:, n * 512:(n + 1) * 512], bank[:, :], [bank], [uTs[ut]])
        proj_fm(b, w, 128, hnT, 16, S, P[0:2], ev_u)
    sgb = []
    for i in range(2):
        t, off = b.carve(arena, off, [128, 512], BF16, "sgb%d" % i)
        sgb.append(t)
    k_ = [0]
    for ut in range(4):
        w = ws.load(wc, 512 + ut * 128)

        def ev_g(n, bank, ut=ut):
            o = sgb[k_[0] % 2]
            k_[0] += 1
            b.act(o[:, :], bank[:, :], AF.Silu, [bank], [o])
            outs.append(b.dma("sp", sgT[ut * 128:(ut + 1) * 128, n * 512:(n + 1) * 512], o[:, :], reads=[o]))
        proj_fm(b, w, 128, hnT, 16, S, P[0:2], ev_g)
    b.flush("s5proj")
    def ring(nm, k, dt=F32):
        nonlocal off
        r = []
        for i in range(k):
            t, off = b.carve(arena, off, [128, 512], dt, "%s%d" % (nm, i))
            r.append(t)
        return r
    rhoT = ring("rhoT", 2)
    un5, un6, fr5 = (ring(nm, 1)[0] for nm in ("un5", "un6", "fr5"))
    CS1, CS2 = ring("CS1", 3), ring("CS2", 3)
    tC, tS = ring("tC", 2), ring("tS", 2)
    ta, tb = ring("ta", 2), ring("tb", 2)
    wst = ring("wst", 3)
    P1b, P2b = ring("P1b", 2, BF16), ring("P2b", 2, BF16)
    yp, x3, sgm = un5, un6, fr5
    yob = ring("yob", 2, BF16)
    assert off <= 17408, off
    ybanks = P[4:8]
    its = [(ut, q, n) for ut in range(4) for q in range(8) for n in range(4)]
    prevw = [None]

    def stage_T(i):
        ut, q, n = its[i]
        gl = ut * 8 + q
        c1, c2 = CS1[i % 3], CS2[i % 3]
        if n == 0:
            rt = rhoT[gl % 2]
            b.act(rt[:, :], ttf[:, 0:512], AF.Identity, [ttf, rho], [rt], scale=0.0, bias=rho[:, gl:gl + 1])
            b.act(un5[:, :], ttf[:, 0:512], AF.Copy, [ttf, thn], [un5], scale=thn[:, gl:gl + 1])
            b.act(un6[:, :], ttf[:, 0:512], AF.Identity, [ttf, thn, q25], [un6], scale=thn[:, gl:gl + 1], bias=q25[:, 0:1])
            b.ts("dve", fr5[:, :], un5[:, :], MAGIC, -MAGIC, ALU.add, ALU.add, [un5], [fr5])
            b.tt("pool", un5[:, :], un5[:, :], fr5[:, :], ALU.subtract, [un5, fr5], [un5])
            b.act(c2[:, :], un5[:, :], AF.Sin, [un5], [c2], scale=TWO_PI)
            b.ts("dve", fr5[:, :], un6[:, :], MAGIC, -MAGIC, ALU.add, ALU.add, [un6], [fr5])
            b.tt("pool", un6[:, :], un6[:, :], fr5[:, :], ALU.subtract, [un6, fr5], [un6])
            b.act(c1[:, :], un6[:, :], AF.Sin, [un6], [c1], scale=TWO_PI)
        else:
            p1, p2 = CS1[(i - 1) % 3], CS2[(i - 1) % 3]
            tc_, ts_ = tC[i % 2], tS[i % 2]
            b.act(tc_[:, :], p1[:, :], AF.Copy, [p1, cD], [tc_], scale=cD[:, gl:gl + 1])
            b.act(c1[:, :], p2[:, :], AF.Copy, [p2, nsD], [c1], scale=nsD[:, gl:gl + 1])
            b.act(ts_[:, :], p2[:, :], AF.Copy, [p2, cD], [ts_], scale=cD[:, gl:gl + 1])
            b.act(c2[:, :], p1[:, :], AF.Copy, [p1, sD], [c2], scale=sD[:, gl:gl + 1])
            b.tt("pool", c1[:, :], c1[:, :], tc_[:, :], ALU.add, [c1, tc_], [c1])
            b.tt("pool", c2[:, :], c2[:, :], ts_[:, :], ALU.add, [c2, ts_], [c2])

    def stage_M(i):
        ut, q, n = its[i]
        gl = ut * 8 + q
        if q // 2 < 3:
            rows = slice(32 * (q // 2), 32 * (q // 2) + 32)
            slot = q % 2
        else:
            rows = slice(64, 128)
            slot = 2 + q % 2
        c1, c2 = CS1[i % 3], CS2[i % 3]
        rt = rhoT[gl % 2]
        pa, pb_ = P[(i % 2) * 2], P[(i % 2) * 2 + 1]
        ur = uTs[ut][rows, n * 512:(n + 1) * 512]
        b.mm(pa, [(pa[:, :], BAb[rows, ut, slot, :], ur)], reads=[BAb, uTs[ut]])
        b.op("pe", lambda e, o=pb_[0:64, :], l=BAb[rows, ut, slot, 64:128], r=ur: e.matmul(o, l, r, start=True, stop=True, skip_group_check=True),
             reads=[BAb, uTs[ut]], writes=[pb_], inc=False)
        b.op("pe", lambda e, o=pb_[64:128, :], l=BAb[rows, ut, slot, 0:64], r=ur: e.matmul(o, l, r, start=True, stop=True, skip_group_check=True),
             reads=(), writes=[pb_])
        ta_, tb_ = ta[i % 2], tb[i % 2]
        b.tt("dve", ta_[:, :], pa[:, :], c1[:, :], ALU.mult, [pa, c1], [ta_])
        b.tt("dve", tb_[:, :], pb_[:, :], c2[:, :], ALU.mult, [pb_, c2], [tb_])
        b.tt("pool", ta_[:, :], ta_[:, :], tb_[:, :], ALU.add, [ta_, tb_], [ta_])

    def stage_M2(i):
        ut, q, n = its[i]
        gl = ut * 8 + q
        c1, c2 = CS1[i % 3], CS2[i % 3]
        rt = rhoT[gl % 2]
        ta_ = ta[i % 2]
        wt = wst[i % 3]
        if n == 0:
            b.op("dve", lambda g, wt=wt: g.tensor_tensor_scan(wt[:, :], rt[:, :], ta_[:, :], 0.0, ALU.mult, ALU.add),
                 [rt, ta_], [wt])
        else:
            prev = prevw[0]
            b.op("dve", lambda g, wt=wt, prev=prev: g.tensor_tensor_scan(wt[:, :], rt[:, :], ta_[:, :], prev[:, 511:512], ALU.mult, ALU.add),
                 [rt, ta_, prev], [wt])
        prevw[0] = wt
        p1b, p2b = P1b[i % 2], P2b[i % 2]
        b.tt("dve", p1b[:, :], wt[:, :], c1[:, :], ALU.mult, [wt, c1], [p1b])
        b.tt("dve", p2b[:, :], wt[:, :], c2[:, :], ALU.mult, [wt, c2], [p2b])
        yb = ybanks[n]
        b.mm(yb, [(yb[:, :], LP1[:, gl, :], p1b[:, :]), (yb[:, :], LP2[:, gl, :], p2b[:, :])],
             reads=[LP1, LP2, p1b, p2b], start=(q == 0), stop=(q == 7))
        if q == 7 and n == 3:
            fin_ut(ut)

    def fin_ut(ut):
        for n in range(4):
            yb = ybanks[n]
            b.stt(yp[:, :], uTs[ut][:, n * 512:(n + 1) * 512], dsk[:, ut:ut + 1], yb[:, :], ALU.mult, ALU.add, [uTs[ut], dsk, yb], [yp])
            b.tt("dve", x3[:, :], yp[:, :], yp[:, :], ALU.mult, [yp], [x3])
            b.ts("dve", x3[:, :], x3[:, :], 0.044715, 1.0, ALU.mult, ALU.add, [x3], [x3])
            b.tt("dve", x3[:, :], x3[:, :], yp[:, :], ALU.mult, [x3, yp], [x3])
            b.act(sgm[:, :], x3[:, :], AF.Sigmoid, [x3], [sgm], scale=2.0 * 0.7978845608028654)
            o = yob[(ut * 4 + n) % 2]
            b.tt("dve", o[:, :], yp[:, :], sgm[:, :], ALU.mult, [yp, sgm], [o])
            outs.append(b.dma("sp", ysT[ut * 128:(ut + 1) * 128, n * 512:(n + 1) * 512], o[:, :], reads=[o]))
    NI = len(its)
    stage_T(0)
    stage_T(1)
    stage_M(0)
    for i in range(NI):
        if i + 2 < NI:
            stage_T(i + 2)
        if i + 1 < NI:
            stage_M(i + 1)
        stage_M2(i)
    b.barrier()
    b.flush("s5rec")
    lbc, omlc, nomlc = sm("lbc", (128, 4)), sm("omlc", (128, 4)), sm("nomlc", (128, 4))
    b.tt("dve", lbc[:, :], gamc[:, 1, :], gamc[:, 0, :], ALU.subtract, [gamc], [lbc])
    b.act(lbc[:, :], lbc[:, :], AF.Sigmoid, [lbc], [lbc])
    b.ts("dve", omlc[:, :], lbc[:, :], -1.0, 1.0, ALU.mult, ALU.add, [lbc], [omlc])
    b.ts("dve", nomlc[:, :], omlc[:, :], -1.0, None, ALU.mult, None, [omlc], [nomlc])
    lbr = b.sb("lbr", [1, 512], F32)
    omlr = b.sb("omlr", [1, 512], F32)
    b.tt("dve", lbr[:, :], gamr[:, 512:1024], gamr[:, 0:512], ALU.subtract, [gamr], [lbr])
    b.act(lbr[:, :], lbr[:, :], AF.Sigmoid, [lbr], [lbr])
    b.ts("dve", omlr[:, :], lbr[:, :], -1.0, 1.0, ALU.mult, ALU.add, [lbr], [omlr])
    off = 0
    qTr, off = b.carve(arena, off, [128, S], F32, "qTr")
    kTr, off = b.carve(arena, off, [128, S], F32, "kTr")
    ktok, off = b.carve(arena, off, [128, 16, 128], F32, "ktok")
    vtok, off = b.carve(arena, off, [128, 16, 128], BF16, "vtok")
    logat, off = b.carve(arena, off, [128, 16, 128], F32, "logat")
    cx.Sst, off = b.carve(arena, off, [128, 256], F32, "Sst")
    def ring2(nm, sh, dt):
        nonlocal off
        r = []
        for i_ in range(2):
            t_, off = b.carve(arena, off, sh, dt, "%s%d" % (nm, i_))
            r.append(t_)
        return r
    cx.Smid = ring2("Smid", [128, 256], BF16)
    cx.E1 = ring2("E1", [128, 132], F32)
    cx.enb = ring2("enb", [128, 128], F32)
    cx.edec = ring2("edec", [128, 128], F32)
    cx.qs = ring2("qs", [128, 128], BF16)
    cx.ks = ring2("ks", [128, 128], BF16)
    cx.kd = ring2("kd", [128, 128], BF16)
    cx.AT = ring2("AT", [128, 128], BF16)
    obS, off = b.carve(arena, off, [128, 512], F32, "obS")
    sqS, off = b.carve(arena, off, [128, 512], BF16, "sqS")
    rs2, off = b.carve(arena, off, [128, 512], F32, "rs2")
    gs2, off = b.carve(arena, off, [128, 512], F32, "gs2")
    sgk, off = b.carve(arena, off, [128, 512], F32, "sgk")
    omlb, off = b.carve(arena, off, [128, 512], F32, "omlb")
    ybs = []
    for i in range(2):
        t, off = b.carve(arena, off, [128, 512], BF16, "ybs%d" % i)
        ybs.append(t)
    assert off <= 17408, off
    cx.pP = [P[0], P[1]]
    cx.pF = [P[2], P[3]]
    cx.pq = P[4]
    cx.pGo = [P[5]]
    cx.pGs = P[6]
    yi = [0]
    for i in range(4):
        base = 1024 + i * 384
        w = ws.load(wc, base)

        def ev_q(n, bank):
            b.act(qTr[:, n * 512:(n + 1) * 512], bank[:, :], AF.Silu, [bank], [qTr])
        proj_fm(b, w, 128, hnT, 16, S, cx.pP, ev_q)
        w = ws.load(wc, base + 128)

        def ev_k(n, bank, i=i):
            b.act(kTr[:, n * 512:(n + 1) * 512], bank[:, :], AF.Sigmoid, [bank], [kTr])
            b.ts("dve", kTr[:, n * 512:(n + 1) * 512], kTr[:, n * 512:(n + 1) * 512], nomlc[:, i:i + 1], omlc[:, i:i + 1],
                 ALU.mult, ALU.add, [kTr, nomlc, omlc], [kTr])
        proj_fm(b, w, 128, hnT, 16, S, cx.pP, ev_k)
        pbc = P[7]
        for a in range(4):
            b.mm(pbc, [(pbc[:, a * 128:(a + 1) * 128], onesf[0:1, 0:128], omlr[0:1, i * 128:(i + 1) * 128])],
                 reads=[onesf, omlr], skip_group_check=True)
        b.copy("act", omlb[:, :], pbc[:, :], [pbc], [omlb])
        tb_ = 1024 + 4 * 384 + i * 256
        for gi in range(2):
            w = ws.load(wc, tb_ + gi * 128)
            for n in range(4):
                bank = cx.pP[n % 2]
                for tt_ in range(4):
                    t = n * 4 + tt_
                    items = [(bank[:, tt_ * 128:(tt_ + 1) * 128], hnT[:, c, t * 128:(t + 1) * 128], w[:, c, :]) for c in range(16)]
                    b.mm(bank, items, reads=[w, hnT], skip_group_check=True)
                kt3 = ktok[:, n * 4:(n + 1) * 4, :]
                if gi == 0:
                    b.act(sgk[:, :], bank[:, :], AF.Sigmoid, [bank], [sgk])
                    b.tt("dve", sgk[:, :], sgk[:, :], omlb[:, :], ALU.mult, [sgk, omlb], [sgk])
                    b.tt("dve", kt3, omlb[:, :].rearrange("p (a e) -> p a e", e=128), sgk[:, :].rearrange("p (a e) -> p a e", e=128),
                         ALU.subtract, [omlb, sgk], [ktok])
                    b.ts("dve", sgk[:, :].rearrange("p (a e) -> p a e", e=128), kt3, -1.0, 1.0, ALU.mult, ALU.add, [ktok], [sgk])
                    b.act(logat[:, n * 4:(n + 1) * 4, :], sgk[:, :].rearrange("p (a e) -> p a e", e=128), AF.Ln, [sgk], [logat])
                else:
                    b.copy("act", vtok[:, n * 4:(n + 1) * 4, :], bank[:, :].rearrange("p (a e) -> p a e", e=128), [bank], [vtok])
        b.flush("hgproj%d" % i)
        wg = ws.load(wc, base + 256)

        def fin(n, i=i, wg=wg):
            b.copy("act", obS[:, :], cx.pGo[0][:, :], [cx.pGo[0]], [obS])
            b.act(sqS[:, :], obS[:, :], AF.Square, [obS], [sqS])
            pq = cx.pq
            b.mm(pq, [(pq[:, :], cx.ones_bf[:, :], sqS[:, :])], reads=[sqS, cx.ones_bf])
            b.act(rs2[:, :], pq[:, :], AF.Sqrt, [pq, eps_t], [rs2], bias=eps_t[:, 0:1], scale=1.0 / 128.0)
            b.recip(rs2[:, :], rs2[:, :], [rs2], [rs2])
            pg = cx.pP[0]
            b.mm(pg, [(pg[:, :], wg[:, c, :], hnT[:, c, n * 512:(n + 1) * 512]) for c in range(16)], reads=[wg, hnT])
            b.act(gs2[:, :], pg[:, :], AF.Silu, [pg], [gs2])
            b.tt("dve", obS[:, :], obS[:, :], rs2[:, :], ALU.mult, [obS, rs2], [obS])
            yb_ = ybs[yi[0] % 2]
            yi[0] += 1
            b.stt(yb_[:, :], obS[:, :], hng[:, i:i + 1], gs2[:, :], ALU.mult, ALU.mult, [obS, hng, gs2], [yb_])
            outs.append(b.dma("sp", ydT[i * 128:(i + 1) * 128, n * 512:(n + 1) * 512], yb_[:, :], reads=[yb_]))
        gla_head(b, cx, qTr, kTr, ktok, vtok, logat, 128, fin)
        b.flush("hgrec%d" % i)
    return outs


from concourse.bass_utils import run_bass_kernel_spmd

PAIRS = [[0, 1], [2, 3], [4, 5], [6, 7]]
L3_KEYS = ("wc3", "lamre", "lamim", "logdt", "cre", "cim", "BA", "dsk", "ttf", "gamc", "gamr", "hng")
L3_SHAPES = {"wc3": [D, NCOL_L1], "lamre": [128, 32], "lamim": [128, 32], "logdt": [128, 32], "cre": [128, 32, 16],
             "cim": [128, 32, 16], "BA": [128, 4, 4, 128], "dsk": [128, 4], "ttf": [128, S], "gamc": [128, 2, 4],
             "gamr": [1, 1024], "hng": [128, 4]}


def build_fused():
    nc = bass.Bass("TRN2", target_bir_lowering=False)
    b = B(nc)

    def din(n, sh, dt=F32):
        return nc.dram_tensor(n, list(sh), dt, kind="ExternalInput").ap()
    io = {"xT": din("xT", [D, S]), "xres": din("xres", [D, 1024]), "wc1": din("wc1", [D, NCOL_L0]),
          "ng0": din("ng0", [128, 16]), "wgate": din("wgate", [16, 256]), "bgate": din("bgate", [1, 256]),
          "gng": din("gng", [128, 4]), "wo0": din("wo0", [D, D]), "ng1": din("ng1", [128, 16]),
          "sel": din("sel", [128, 2]), "wo1": din("wo1", [D, D]), "ngf": din("ngf", [128, 16]),
          "wglu": din("wglu", [1024, 1024]), "bglu": din("bglu", [128, 8])}
    for k in L3_KEYS:
        io[k] = din(k, L3_SHAPES[k])
    outT = nc.dram_tensor("outT", [D, 1024], F32, kind="ExternalOutput").ap()
    y0m = [nc.dram_tensor("y0m%d" % k, [512, S], BF16) for k in range(2)]
    y0a = [nc.dram_tensor("y0a%d" % k, [1024, S], BF16) for k in range(2)]
    hnm = [nc.dram_tensor("hnm%d" % k, [1024, 1024], BF16) for k in range(2)]
    hna = [nc.dram_tensor("hna%d" % k, [2048, 1024], BF16) for k in range(2)]
    y1m = [nc.dram_tensor("y1m%d" % k, [512, S], BF16) for k in range(3)]
    y1a = [nc.dram_tensor("y1a%d" % k, [1024, S], BF16) for k in range(3)]
    h1s = nc.dram_tensor("h1s", [D, 1024], F32)

    class Rows:
        def __init__(self, pieces, rows_per):
            self.p = [t.ap() for t in pieces]
            self.n = rows_per

        def __getitem__(self, idx):
            rs, cs = idx
            k = rs.start // self.n
            assert (rs.stop - 1) // self.n == k
            return self.p[k][rs.start - k * self.n:rs.stop - k * self.n, cs]
    b.push_scope()
    io["y0"] = Rows(y0m, 512)
    phase1(nc, b, io)
    b.pop_scope()
    b.cc_allgather_multi(PAIRS, [(y0m[k].ap().opt(), y0a[k].ap().opt()) for k in range(2)])
    b.push_scope()

    def ysrc0(c):
        v = y0a[0 if c < 8 else 1].ap()
        r0 = ((c // 4) % 2) * 512 + (c % 4) * 128
        return (v[r0:r0 + 128, 0:1024], v[r0:r0 + 128, 1024:2048])
    phaseO(nc, b, {"ysrc": ysrc0, "sel": io["sel"], "resT": io["xres"], "wo": io["wo0"], "ng": io["ng1"],
                   "hTd": h1s.ap(), "hnTd": Rows(hnm, 1024)}, False, False)
    b.pop_scope()
    b.cc_allgather_multi(PAIRS, [(hnm[k].ap().opt(), hna[k].ap().opt()) for k in range(2)])
    b.push_scope()
    io3 = dict(io)

    def hnsrc(c):
        v = hna[c // 8].ap()
        r0 = (c % 8) * 128
        return [(0, v[r0:r0 + 128, :]), (1024, v[1024 + r0:1024 + r0 + 128, :])]
    io3["hnsrc"] = hnsrc
    io3["ys"], io3["sg"], io3["yd"] = y1m[0].ap(), y1m[1].ap(), y1m[2].ap()
    phase3(nc, b, io3)
    b.pop_scope()
    b.cc_allgather_multi(PAIRS, [(y1m[k].ap().opt(), y1a[k].ap().opt()) for k in range(3)])
    b.push_scope()

    def ysrc1(c):
        v = y1a[0 if c < 8 else 2].ap()
        r0 = ((c // 4) % 2) * 512 + (c % 4) * 128
        return (v[r0:r0 + 128, 0:1024], v[r0:r0 + 128, 1024:2048])

    def sgsrc(m):
        v = y1a[1].ap()
        r0 = (m // 4) * 512 + (m % 4) * 128
        return (v[r0:r0 + 128, 0:1024], v[r0:r0 + 128, 1024:2048])
    outs = phaseO(nc, b, {"ysrc": ysrc1, "sgsrc": sgsrc, "sel": io["sel"], "resT": h1s.ap(), "wo": io["wo1"],
                          "ng": io["ngf"], "wglu": io["wglu"], "bglu": io["bglu"], "outT": outT}, True, True)
    b.wait_all_outputs("sp", outs)
    b.pop_scope()
    b.emit()
    return nc


def kernel(**inp):
    inp = {k: np.asarray(v) for k, v in inp.items()}
    hc = host_consts()
    cores = list(range(8))
    nc = build_fused()
    shared = {
        "ng1": np.ascontiguousarray(inp["norm_g"][1].reshape(16, 128).T),
        "ngf": np.ascontiguousarray(inp["final_g"].reshape(16, 128).T),
        "bglu": np.ascontiguousarray(inp["s5_b_glu"][0].reshape(8, 128).T),
        "wo0": inp["ab_w_out"][0], "wo1": inp["cd_w_out"][0], "wglu": inp["s5_w_glu"][0],
    }
    half = {hh: dict(l3_inputs(inp, hh)) for hh in range(2)}
    l1h = {}
    maps = []
    for c in cores:
        bi, r = c // 2, c % 2
        m = dict(shared)
        m.update(half[r])
        m.update(l1_inputs(inp, bi, r))
        m["xres"] = np.ascontiguousarray(inp["x"][bi][r * 1024:(r + 1) * 1024].T)
        sel = np.zeros((128, 2), np.float32)
        sel[:, r] = 1.0
        m["sel"] = sel
        for k in ("c_ones", "c_ident", "c_masks", "c_tri", "c_ubd", "c_mbd", "c_onesf"):
            m[k] = hc[k]
        maps.append(m)
    res = run_bass_kernel_spmd(nc, maps, core_ids=cores).results
    out = np.empty((4, S, D), np.float32)
    for c in cores:
        bi, r = c // 2, c % 2
        out[bi, r * 1024:(r + 1) * 1024, :] = np.asarray(res[c]["outT"]).T
    return out
```

```python
from contextlib import ExitStack
import concourse.bass as bass
import concourse.mybir as mybir

F32 = mybir.dt.float32
BF16 = mybir.dt.bfloat16
I32 = mybir.dt.int32
AF = mybir.ActivationFunctionType
ALU = mybir.AluOpType

ENGS = ("pe", "act", "dve", "pool", "sp")


class T:
    __slots__ = ("t", "name", "lw", "rd", "lwx")

    def __init__(self, t, name):
        self.t = t
        self.name = name
        self.lw = None
        self.rd = []
        self.lwx = []

    def __getitem__(self, idx):
        return self.t[idx]


class B:
    def __init__(self, nc, ndma=6):
        self.nc = nc
        self.es = ExitStack()
        self.scopes = [ExitStack()]
        self.sid = 0
        self.cd = {}
        self.eng = {"pe": nc.tensor, "act": nc.scalar, "dve": nc.vector,
                    "pool": nc.gpsimd, "sp": nc.sync}
        self.ops = {e: [] for e in ENGS}
        self.sem = {}
        self.cnt = {}
        for e in ENGS:
            self.sem[e] = self.es.enter_context(nc.semaphore("s_" + e))
            self.cnt[e] = 0
        self.ndma = ndma
        self.dq = {}
        for q in ("sp", "act", "pool"):
            sems = [self.es.enter_context(nc.semaphore("d_%s%d" % (q, i))) for i in range(ndma)]
            for i, s in enumerate(sems):
                self.sem[("d", q, i)] = s
            self.dq[q] = 0
        self.seen = {e: {} for e in ENGS}
        self.final_waits = []
        self.sem["cc"] = self.es.enter_context(nc.semaphore("s_cc"))
        self.ncc = 0

    def sb(self, name, shape, dt):
        name = "%s_z%d" % (name, self.sid)
        return T(self.scopes[-1].enter_context(self.nc.sbuf_tensor(name, list(shape), dt)), name)

    def ps(self, name, shape, dt=F32):
        name = "%s_z%d" % (name, self.sid)
        return T(self.scopes[-1].enter_context(self.nc.psum_tensor(name, list(shape), dt)), name)

    def push_scope(self):
        self.sid += 1
        self.scopes.append(ExitStack())

    def pop_scope(self):
        self.barrier()
        self.flush()
        self.scopes.pop().close()

    def cc_allgather(self, groups, in_ap, out_ap):
        self.barrier()
        self.ncc += 1
        sem = self.sem["cc"]
        self.ops["pool"].append(([], lambda g: g.collective_compute("AllGather", ALU.bypass, groups, [in_ap], [out_ap]).then_inc(sem), None))
        self.barrier()
        self.flush("cc")

    def view(self, t, name=None):
        return T(t.t if isinstance(t, T) else t, name or "v")

    def _waits(self, e, reads, writes):
        w = {}

        def add(dep):
            k, v, de = dep
            if k not in w or w[k] < v:
                w[k] = v

        for t in reads:
            if t.lw is not None:
                add(t.lw)
            for x in t.lwx:
                add(x)
        for t in writes:
            if t.lw is not None:
                if not (t.lw[2] == e and e == "pe"):
                    add(t.lw)
            for x in t.lwx:
                add(x)
            for r in t.rd:
                if r[2] == e and not isinstance(r[0], tuple):
                    continue
                add(r)
        out = []
        seen = self.seen[e]
        for k, v in w.items():
            if seen.get(k, 0) >= v:
                continue
            seen[k] = v
            out.append((k, v))
        return out

    def op(self, e, fn, reads=(), writes=(), inc=True):
        waits = self._waits(e, reads, writes)
        if inc:
            self.cnt[e] += 1
            me = (e, self.cnt[e], e)
        else:
            me = None
        self.ops[e].append((waits, fn, inc))
        if inc:
            for t in reads:
                t.rd.append(me)
            for t in writes:
                t.lw = me
                t.lwx = []
                t.rd = []
        return me

    def mm(self, out_t, items, reads, start=True, stop=True, **kw):
        n = len(items)
        for i, (o, l, r) in enumerate(items):
            st = start and i == 0
            sp = stop and i == n - 1

            def fn(eng, o=o, l=l, r=r, st=st, sp=sp):
                return eng.matmul(o, l, r, start=st, stop=sp, **kw)
            if i == n - 1:
                self.op("pe", fn, reads=reads if n == 1 else (), writes=[out_t])
            else:
                self.op("pe", fn, reads=reads if i == 0 else (), writes=[out_t] if i == 0 else (), inc=False)
        if n > 1:
            me = ("pe", self.cnt["pe"], "pe")
            for t in reads:
                t.rd.append(me)

    def dma(self, q, out_ap, in_ap, reads=(), writes=(), **kw):
        j = self.dq[q]
        self.dq[q] += 1
        slot = j % self.ndma
        key = ("d", q, slot)
        val = 16 * (j // self.ndma + 1)
        waits = self._waits(q, reads, writes)
        prev = 16 * (j // self.ndma)
        if prev > 0 and self.seen[q].get(key, 0) < prev:
            self.seen[q][key] = prev
            waits.append((key, prev))
        sem = self.sem[key]

        def fn(eng, o=out_ap, i=in_ap):
            return eng.dma_start(out=o, in_=i, **kw).then_inc(sem, 16)
        self.ops[q].append((waits, fn, None))
        me = (key, val, "dma_" + q)
        for t in reads:
            t.rd.append(me)
        for t in writes:
            if t.lw is not None and isinstance(t.lw[0], tuple):
                t.lwx.append(t.lw)
            else:
                t.lwx = []
            t.lw = me
            t.rd = []
        return me

    def wait_all_outputs(self, e, deps):
        waits = []
        for d in deps:
            waits.append((d[0], d[1]))
        self.ops[e].append((waits, None, None))

    def cc_allgather_multi(self, groups, pairs):
        self.barrier()
        sem = self.sem["cc"]
        for (in_ap, out_ap) in pairs:
            self.ncc += 1
            self.ops["pool"].append(([], lambda g, i=in_ap, o=out_ap: g.collective_compute("AllGather", ALU.bypass, groups, [i], [o]).then_inc(sem), None))
        self.barrier()
        self.flush("cc")

    def barrier(self):
        tgt = []
        for e in ENGS:
            if self.cnt[e] > 0:
                tgt.append((e, self.cnt[e]))
        for q in self.dq:
            j = self.dq[q]
            for s_ in range(self.ndma):
                n = (j - s_ + self.ndma - 1) // self.ndma if j > s_ else 0
                if n > 0:
                    tgt.append((("d", q, s_), 16 * n))
        if self.ncc > 0:
            tgt.append(("cc", self.ncc))
        for e in ENGS:
            waits = []
            for k, v in tgt:
                if k == e:
                    continue
                if self.seen[e].get(k, 0) < v:
                    self.seen[e][k] = v
                    waits.append((k, v))
            if waits:
                self.ops[e].append((waits, None, None))

    def carve(self, arena, off, shape, dt, name="c"):
        n = 1
        for s_ in shape[1:]:
            n *= s_
        nbytes = n * (2 if dt == BF16 else 4)
        ncol = (nbytes + 3) // 4
        ap = arena.t[:, off:off + ncol]
        if dt != F32:
            ap = ap.bitcast(dt)
        if len(shape) == 3:
            ap = ap.rearrange("p (a b) -> p a b", b=shape[2])
        if shape[0] != 128:
            ap = ap[0:shape[0]]
        return T(ap, name), off + ncol

    def emit(self):
        self.flush()
        while self.scopes:
            self.scopes.pop().close()
        self.es.close()

    def flush(self, label=None):
        nc = self.nc
        if not any(self.ops[e] for e in ENGS):
            return
        self.nflush = getattr(self, "nflush", 0) + 1
        lab = "f%02d_%s" % (self.nflush, label or "x")
        if getattr(self, "profile_scopes", False):
            with nc.named_scope(lab):
                self._flush()
        else:
            self._flush()

    def _flush(self):
        nc = self.nc
        with nc.Block() as block:
            for e in ENGS:
                lst = self.ops[e]
                if not lst:
                    continue
                dec = {"pe": block.tensor, "act": block.scalar, "dve": block.vector,
                       "pool": block.gpsimd, "sp": block.sync}[e]
                sem_e = self.sem[e]

                def body(eng, lst=lst, sem_e=sem_e):
                    for waits, fn, inc in lst:
                        for k, v in waits:
                            eng.wait_ge(self.sem[k], v)
                        if fn is None:
                            continue
                        ins = fn(eng)
                        if inc is True:
                            ins.then_inc(sem_e, 1)
                dec(body)
        self.ops = {e: [] for e in ENGS}


def _act(b, out, in_, func, reads, writes, **kw):
    return b.op("act", lambda e: e.activation(out, in_, func, **kw), reads, writes)


def _tt(b, e, out, in0, in1, op, reads, writes):
    return b.op(e, lambda g: g.tensor_tensor(out, in0, in1, op), reads, writes)


def _ts(b, e, out, in0, s1, s2, op0, op1, reads, writes):
    if op1 is None:
        return b.op(e, lambda g: g.tensor_scalar(out, in0, s1, None, op0), reads, writes)
    return b.op(e, lambda g: g.tensor_scalar(out, in0, s1, s2, op0, op1), reads, writes)


def _stt(b, out, in0, scalar, in1, op0, op1, reads, writes):
    return b.op("dve", lambda g: g.scalar_tensor_tensor(out, in0, scalar, in1, op0, op1), reads, writes)


def _copy(b, e, out, in_, reads, writes):
    if e == "act":
        return b.op(e, lambda g: g.copy(out, in_), reads, writes)
    return b.op(e, lambda g: g.tensor_copy(out, in_), reads, writes)


def _recip(b, out, in_, reads, writes):
    return b.op("dve", lambda g: g.reciprocal(out, in_), reads, writes)


def _memset(b, e, ap, val, writes):
    return b.op(e, lambda g: g.memset(ap, val), (), writes)


B.act = _act
B.tt = _tt
B.ts = _ts
B.stt = _stt
B.copy = _copy
B.recip = _recip
B.memset = _memset

import numpy as np
import ml_dtypes
NPBF = ml_dtypes.bfloat16
S = 2048
D = 2048
EPS = 1e-6


def host_consts():
    j = np.arange(128)[:, None]
    i = np.arange(128)[None, :]
    cur = (j <= i).astype(np.float32)
    prev = (j >= i).astype(np.float32)
    mA = np.concatenate([prev, cur, prev, cur], 1)
    mB = np.concatenate([cur, prev, cur, prev], 1)
    mC = np.concatenate([cur] * 4, 1)
    mD = np.concatenate([prev] * 4, 1)
    mE = [np.concatenate([cur[:, 32 * b:32 * b + 32]] * 16, 1) for b in range(4)]
    masks = np.stack([mA, mB, mC, mD] + mE, 1)
    same = (j // 64) == (i // 64)
    tri = ((j <= i) & same).astype(np.float32)
    mid = np.where(same, ((j % 64) <= 31), False).astype(np.float32)
    TRI = np.zeros((128, 132), np.float32)
    TRI[:, :128] = tri - mid
    jj = np.arange(128)
    TRI[:, 128] = ((jj < 64) & (jj % 64 <= 31))
    TRI[:, 129] = ((jj >= 64) & (jj % 64 <= 31))
    TRI[:, 130] = (jj < 64)
    TRI[:, 131] = (jj >= 64)
    UBD = ((j > i) & same).astype(np.float32)
    mBD = ((j <= i) & same).astype(np.float32)
    return {
        "c_ones": np.ones((128, 128), NPBF),
        "c_ident": np.eye(128, dtype=np.float32).astype(NPBF),
        "c_masks": masks.astype(NPBF),
        "c_tri": TRI,
        "c_ubd": UBD,
        "c_mbd": mBD.astype(NPBF),
        "c_onesf": np.ones((128, 128), np.float32),
    }


class Ctx:
    pass


def load_consts(b, nc, names):
    hc = host_consts()
    out = {}
    for n in names:
        a = hc[n]
        dt = BF16 if a.dtype == NPBF else F32
        if n not in b.cd:
            b.cd[n] = nc.dram_tensor(n, list(a.shape), dt, kind="ExternalInput").ap()
        d = b.cd[n]
        t = b.sb("sb_" + n, a.shape, dt)
        if a.ndim == 2:
            b.dma("sp", t[:, :], d[:, :], writes=[t])
        else:
            b.dma("sp", t[:, :, :], d[:, :, :], writes=[t])
        out[n] = t
    return out


def rmsnorm_T(b, xT, g_sb, hnT, nchunk, ntok, banks, xs_ring, sq_ring, rstd, ones_bf, eps_t, t0=0, h0=0):
    nb = ntok // 512
    for c in range(nchunk):
        xs = xs_ring[c % len(xs_ring)]
        b.dma("sp", xs[:, :ntok], xT[c * 128:(c + 1) * 128, t0:t0 + ntok], writes=[xs])
        sq = sq_ring[c % len(sq_ring)]
        b.act(sq[:, :ntok], xs[:, :ntok], AF.Square, [xs], [sq])
        for n in range(nb):
            b.mm(banks[n], [(banks[n][:, :], ones_bf[:, :], sq[:, n * 512:(n + 1) * 512])],
                 reads=[sq, ones_bf], start=(c == 0), stop=(c == nchunk - 1))
    for n in range(nb):
        b.act(rstd[:, n * 512:(n + 1) * 512], banks[n][:, :], AF.Sqrt, [banks[n], eps_t], [rstd],
              bias=eps_t[:, 0:1], scale=1.0 / (nchunk * 128))
    b.recip(rstd[:, :ntok], rstd[:, :ntok], [rstd], [rstd])
    for c in range(nchunk):
        xs = xs_ring[c % len(xs_ring)]
        b.dma("sp", xs[:, :ntok], xT[c * 128:(c + 1) * 128, t0:t0 + ntok], writes=[xs])
        b.stt(hnT[:, c, h0:h0 + ntok], xs[:, :ntok], g_sb[:, c:c + 1], rstd[:, :ntok], ALU.mult, ALU.mult,
              [xs, g_sb, rstd], [hnT])


def rmsnorm_seg(b, xT, g_sb, hnT, nchunk, nseg, bank_ring, xseg_ring, sq_ring, rstd_ring, ones_bf, eps_t):
    for sg_ in range(nseg):
        xseg = xseg_ring[sg_ % len(xseg_ring)]
        bank = bank_ring[sg_ % len(bank_ring)]
        rstd = rstd_ring[sg_ % len(rstd_ring)]
        for c4 in range(nchunk // 4):
            src = xT[c4 * 512:(c4 + 1) * 512, sg_ * 512:(sg_ + 1) * 512].rearrange("(c p) n -> p c n", p=128)
            b.dma("sp", xseg[:, c4 * 4:(c4 + 1) * 4, :], src, writes=[xseg])
        for c in range(nchunk):
            sq = sq_ring[c % len(sq_ring)]
            b.act(sq[:, :], xseg[:, c, :], AF.Square, [xseg], [sq])
            b.mm(bank, [(bank[:, :], ones_bf[:, :], sq[:, :])], reads=[sq, ones_bf], start=(c == 0), stop=(c == nchunk - 1))
        b.act(rstd[:, :], bank[:, :], AF.Sqrt, [bank, eps_t], [rstd], bias=eps_t[:, 0:1], scale=1.0 / (nchunk * 128))
        b.recip(rstd[:, :], rstd[:, :], [rstd], [rstd])
        for c in range(nchunk):
            b.stt(hnT[:, c, sg_ * 512:(sg_ + 1) * 512], xseg[:, c, :], g_sb[:, c:c + 1], rstd[:, :], ALU.mult, ALU.mult,
                  [xseg, g_sb, rstd], [hnT])


class WStream:
    def __init__(self, b, nchunk, nst=2, nbf=3, name="w"):
        self.b = b
        self.nchunk = nchunk
        self.st = [b.sb("%s_st%d" % (name, i), [128, nchunk, 128], F32) for i in range(nst)]
        self.bf = [b.sb("%s_bf%d" % (name, i), [128, nchunk, 128], BF16) for i in range(nbf)]
        self.i = 0

    def load(self, w_dram, col0, ncols=128, q="sp", cast_eng="pool"):
        b = self.b
        st = self.st[self.i % len(self.st)]
        bf = self.bf[self.i % len(self.bf)]
        self.i += 1
        src = w_dram[:, col0:col0 + ncols].rearrange("(c p) n -> p c n", p=128)
        b.dma(q, st[:, :, :ncols], src, writes=[st])
        b.copy(cast_eng, bf[:, :, :ncols], st[:, :, :ncols], [st], [bf])
        return bf


def proj_fm(b, wbf, mcols, hnT, nchunk, ntok, banks, evac):
    nb = ntok // 512
    for n in range(nb):
        bank = banks[n % len(banks)]
        items = [(bank[:mcols, :], wbf[:, c, :mcols], hnT[:, c, n * 512:(n + 1) * 512]) for c in range(nchunk)]
        b.mm(bank, items, reads=[wbf, hnT])
        evac(n, bank)


NCOL_L0 = 4 * 512 + 2 * 512 + 2 * 384 + 16


def l0_weight_cols(hh):
    cols = []
    GATE = 5136
    for i in range(4):
        hg = 4 * hh + i
        for base in (0, 1024, 2048, GATE):
            cols += list(range(base + hg * 128, base + hg * 128 + 128))
    for i in range(2):
        hg = 2 * hh + i
        cols += list(range(3072 + hg * 128, 3072 + hg * 128 + 128))
        cols += list(range(3584 + hg * 128, 3584 + hg * 128 + 128))
        cols += list(range(GATE + 1024 + hg * 256, GATE + 1024 + hg * 256 + 256))
    for i in range(2):
        hg = 2 * hh + i
        cols += list(range(3584 + hg * 128, 3584 + hg * 128 + 128))
        cols += list(range(4096 + hg * 256, 4096 + hg * 256 + 256))
    cols += list(range(5120, 5136))
    assert len(cols) == NCOL_L0
    return np.array(cols)


def attention_head(b, cx, qT, kT, vT, wg, yout_cb):
    ident, ones_bf, masks = cx.ident, cx.ones_bf, cx.masks
    scale = 128.0 ** -0.5
    Vd = cx.Vd
    for pi, d in enumerate((1, 4, 16)):
        for half in range(2):
            pt = cx.pT8
            for k8 in range(8):
                tix = half * 8 + k8
                if d == 1:
                    src = vT[:, tix * 128:(tix + 1) * 128]
                elif d == 4:
                    r, n_ = tix // 4, tix % 4
                    st = n_ * 512 + r
                    src = vT[:, st:st + 509:4]
                else:
                    r = tix
                    src = vT[:, r:r + 2033:16]
                b.op("pe", lambda e, o=pt[:, k8 * 128:(k8 + 1) * 128], s=src: e.transpose(o, s, ident[:, :]),
                     reads=[vT, ident] if k8 == 0 else (), writes=[pt] if k8 == 0 else (), inc=(k8 == 7))
                if k8 == 7:
                    me = ("pe", b.cnt["pe"], "pe")
                    pt.lw = me
                    pt.rd = []
                    vT.rd.append(me)
            b.copy("dve", Vd[pi][:, half * 8:(half + 1) * 8, :], pt[:, :].rearrange("p (a e) -> p a e", e=128),
                   [pt], [Vd[pi]])
    o3, d3 = cx.o3, cx.d3
    work = []
    for rq in range(4):
        cs = []
        for r4 in range(4):
            r = rq * 4 + r4
            cs.append((kT[:, r:r + 2033:16], qT[:, r:r + 2033:16], 128, Vd[2][:, r, :], (r4 * 128, 128, 1), r4 * 128))
        work.append(dict(kind="p3", mi=2, cs=cs, rq=rq))
    for bk in range(4):
        sb = []
        A = []
        if bk > 0:
            kb = 4 * bk - 1
            A.append((kT[:, kb * 128:(kb + 1) * 128], qT[:, bk * 512:bk * 512 + 128], 128, Vd[0][:, kb, :], (0, 128, 1), 0))
        kb = 4 * bk
        A.append((kT[:, kb * 128:(kb + 1) * 128], qT[:, bk * 512:bk * 512 + 256], 256, Vd[0][:, kb, :], (0, 256, 1), 128))
        kb = 4 * bk + 3
        A.append((kT[:, kb * 128:(kb + 1) * 128], qT[:, bk * 512 + 384:bk * 512 + 512], 128, Vd[0][:, kb, :], (384, 128, 1), 384))
        sb.append((0, A))
        Bq = []
        for q_ in (1, 2):
            kb = 4 * bk + q_
            Bq.append((kT[:, kb * 128:(kb + 1) * 128], qT[:, bk * 512 + q_ * 128:bk * 512 + q_ * 128 + 256], 256,
                       Vd[0][:, kb, :], (q_ * 128, 256, 1), (q_ - 1) * 256))
        sb.append((1, Bq))
        C = []
        Dl = []
        for r in range(4):
            st = bk * 512 + r
            qa = qT[:, st:st + 509:4]
            C.append((kT[:, st:st + 509:4], qa, 128, Vd[1][:, r * 4 + bk, :], (r, 128, 4), r * 128))
            if bk > 0:
                sp_ = (bk - 1) * 512 + r
                Dl.append((kT[:, sp_:sp_ + 509:4], qa, 128, Vd[1][:, r * 4 + bk - 1, :], (r, 128, 4), r * 128))
        sb.append((2, C))
        if bk > 0:
            sb.append((3, Dl))
        for si, (mi, cs) in enumerate(sb):
            work.append(dict(kind="std", mi=mi, cs=cs, bk=bk, first=(si == 0), last=(si == len(sb) - 1)))

    def stage1(w):
        sc = cx.pS[cx.sidx % len(cx.pS)]
        pb = cx.pbuf[cx.sidx % len(cx.pbuf)]
        cx.sidx += 1
        w["pb"] = pb
        cs = w["cs"]
        lo = min(c[5] for c in cs)
        hi = max(c[5] + c[2] for c in cs)
        for ci, c in enumerate(cs):
            kap, qap, ncol, vap, ocs, soff = c
            lastc = (ci == len(cs) - 1)
            b.op("pe", lambda e, o=sc[:, soff:soff + ncol], l=kap, r=qap: e.matmul(o, l, r, start=True, stop=True, skip_group_check=True),
                 reads=[kT, qT] if ci == 0 else (), writes=[sc] if ci == 0 else (), inc=lastc)
        me = ("pe", b.cnt["pe"], "pe")
        sc.lw = me
        sc.rd = []
        kT.rd.append(me)
        qT.rd.append(me)
        b.act(pb[:, lo:hi], sc[:, lo:hi], AF.Exp, [sc], [pb], scale=scale)
        b.tt("dve", pb[:, lo:hi], pb[:, lo:hi], masks[:, w["mi"], lo:hi], ALU.mult, [pb, masks], [pb])

    def stage2(w):
        pb = w["pb"]
        cs = w["cs"]
        O, Dn = cx.pO, cx.pD
        p3 = (w["kind"] == "p3")
        for ci, c in enumerate(cs):
            kap, qap, ncol, vap, ocs, soff = c
            o0, on, ostep = ocs
            osl = slice(o0, o0 + (on - 1) * ostep + 1, ostep)
            if p3:
                st_flag, stop_flag = True, True
            else:
                st_flag = w["first"] and ci == 0
                stop_flag = w["last"] and ci == len(cs) - 1
            lastc = (ci == len(cs) - 1)
            b.op("pe", lambda e, o=O[:, osl], l=vap, r=pb[:, soff:soff + ncol], sf=st_flag, lc=stop_flag:
                 e.matmul(o, l, r, start=sf, stop=lc, skip_group_check=True),
                 reads=[pb, Vd[0], Vd[1], Vd[2]] if ci == 0 else (), writes=[O] if ci == 0 else (), inc=False)
            b.op("pe", lambda e, o=Dn[:, osl], l=ones_bf[:, :], r=pb[:, soff:soff + ncol], sf=st_flag, lc=stop_flag:
                 e.matmul(o, l, r, start=sf, stop=lc, skip_group_check=True),
                 reads=(), writes=[Dn] if ci == 0 else (), inc=lastc)
        me = ("pe", b.cnt["pe"], "pe")
        O.lw = me
        Dn.lw = me
        O.rd = []
        Dn.rd = []
        pb.rd.append(me)
        for v in Vd:
            v.rd.append(me)
        if p3:
            rq = w["rq"]
            dst_o = o3[:, :].rearrange("p (j r) -> p r j", r=16)[:, rq * 4:(rq + 1) * 4, :]
            dst_d = d3[:, :].rearrange("p (j r) -> p r j", r=16)[:, rq * 4:(rq + 1) * 4, :]
            b.copy("act", dst_o, O[:, :].rearrange("p (r j) -> p r j", j=128), [O], [o3])
            b.copy("dve", dst_d, Dn[:, :].rearrange("p (r j) -> p r j", j=128), [Dn], [d3])
        elif w["last"]:
            bk = w["bk"]
            rec = cx.rec
            osum = cx.osum
            yt = cx.ybuf[cx.yidx % len(cx.ybuf)]
            cx.yidx += 1
            b.tt("dve", rec[:, :], Dn[:, :], d3[:, bk * 512:(bk + 1) * 512], ALU.add, [Dn, d3], [rec])
            b.recip(rec[:, :], rec[:, :], [rec], [rec])
            b.tt("dve", osum[:, :], O[:, :], o3[:, bk * 512:(bk + 1) * 512], ALU.add, [O, o3], [osum])
            b.tt("dve", rec[:, :], osum[:, :], rec[:, :], ALU.mult, [osum, rec], [rec])
            pg = cx.pP[bk % 2]
            b.mm(pg, [(pg[:, :], wg[:, c, :], cx.hnT[:, c, bk * 512:(bk + 1) * 512]) for c in range(16)], reads=[wg, cx.hnT])
            gs = cx.gs
            b.act(gs[:, :], pg[:, :], AF.Silu, [pg], [gs])
            b.tt("dve", yt[:, :], rec[:, :], gs[:, :], ALU.mult, [rec, gs], [yt])
            yout_cb(bk, yt)
    stage1(work[0])
    for i in range(len(work)):
        if i + 1 < len(work):
            stage1(work[i + 1])
        stage2(work[i])


def gla_head(b, cx, qTr, kTr, ktok, vtok, logat, dv, fin_cb, nt=16):
    nv = dv // 128
    Sst = cx.Sst
    b.memset("dve", Sst[:, :dv], 0.0, [Sst])
    sk = [0]

    def front1(t):
        k = t % 2
        pF = cx.pF[k]
        E1, enb, edec, qs, ks, kd = cx.E1[k], cx.enb[k], cx.edec[k], cx.qs[k], cx.ks[k], cx.kd[k]
        b.op("pe", lambda e_: e_.matmul(pF[:, 0:132], logat[:, t, :], cx.tri[:, :], start=True, stop=True, skip_group_check=True),
             reads=[logat, cx.tri], writes=[pF], inc=False)
        b.op("pe", lambda e_: e_.matmul(pF[:, 132:260], cx.ubd[:, :], logat[:, t, :], start=True, stop=True, skip_group_check=True),
             reads=[cx.ubd], writes=[pF])
        b.act(E1[:, :132], pF[:, 0:132], AF.Exp, [pF], [E1])
        b.act(enb[:, :], pF[:, 0:128], AF.Exp, [pF], [enb], scale=-1.0)
        b.act(edec[:, :], pF[:, 132:260], AF.Exp, [pF], [edec])
        b.tt("dve", qs[:, :], qTr[:, t * 128:(t + 1) * 128], E1[:, :128], ALU.mult, [qTr, E1], [qs])
        b.tt("dve", ks[:, :], kTr[:, t * 128:(t + 1) * 128], enb[:, :], ALU.mult, [kTr, enb], [ks])
        b.tt("pool", kd[:, :], ktok[:, t, :], edec[:, :], ALU.mult, [ktok, edec], [kd])

    def front2(t):
        k = t % 2
        pF = cx.pF[k]
        qs, ks, AT = cx.qs[k], cx.ks[k], cx.AT[k]
        b.op("pe", lambda e_: e_.matmul(pF[:, 260:388], ks[:, :], qs[:, :], start=True, stop=True, skip_group_check=True),
             reads=[ks, qs], writes=[pF])
        b.tt("dve", AT[:, :], pF[:, 260:388], cx.mbd[:, :], ALU.mult, [pF, cx.mbd], [AT])

    def back(t):
        k = t % 2
        E1, qs, kd, AT = cx.E1[k], cx.qs[k], cx.kd[k], cx.AT[k]
        pOo = cx.pGo
        for cc in range(2):
            r0 = 64 * cc
            Smid = cx.Smid[sk[0] % 2]
            sk[0] += 1
            b.act(Smid[:, :dv], Sst[:, :dv], AF.Copy, [Sst, E1], [Smid], scale=E1[:, 128 + cc:129 + cc])
            pS_ = cx.pGs
            b.mm(pS_, [(pS_[:, :dv], kd[r0:r0 + 64, :], vtok[r0:r0 + 64, t, :dv])], reads=[kd, vtok])
            b.stt(Sst[:, :dv], Sst[:, :dv], E1[:, 130 + cc:131 + cc], pS_[:, :dv], ALU.mult, ALU.add,
                  [Sst, E1, pS_], [Sst])
            for vt in range(nv):
                col = (t % 4) * 128 + r0
                items = [
                    (pOo[vt][:, col:col + 64], vtok[r0:r0 + 64, t, vt * 128:(vt + 1) * 128], AT[r0:r0 + 64, r0:r0 + 64]),
                    (pOo[vt][:, col:col + 64], Smid[:, vt * 128:(vt + 1) * 128], qs[:, r0:r0 + 64]),
                ]
                b.mm(pOo[vt], items, reads=[vtok, AT, Smid, qs], skip_group_check=True)
        if t % 4 == 3:
            fin_cb(t // 4)
    front1(0)
    front2(0)
    for t in range(nt):
        if t + 1 < nt:
            front1(t + 1)
        back(t)
        if t + 1 < nt:
            front2(t + 1)


def phase1(nc, b, io):
    xT, wc, ng, wgate, bgate, gng, yT = (io[k] for k in ("xT", "wc1", "ng0", "wgate", "bgate", "gng", "y0"))
    cs = load_consts(b, nc, ["c_ones", "c_ident", "c_masks", "c_tri", "c_ubd", "c_mbd", "c_onesf"])
    cx = Ctx()
    cx.ones_bf, cx.ident, cx.masks = cs["c_ones"], cs["c_ident"], cs["c_masks"]
    cx.tri, cx.ubd, cx.mbd, onesf = cs["c_tri"], cs["c_ubd"], cs["c_mbd"], cs["c_onesf"]
    g_sb = b.sb("g_sb", [128, 16], F32)
    b.dma("sp", g_sb[:, :], ng[:, :], writes=[g_sb])
    wg_sb = b.sb("wg_sb", [16, 256], F32)
    b.dma("sp", wg_sb[:, :], wgate[:, :], writes=[wg_sb])
    bg_sb = b.sb("bg_sb", [1, 256], F32)
    b.dma("sp", bg_sb[:, :], bgate[:, :], writes=[bg_sb])
    gn_sb = b.sb("gn_sb", [128, 4], F32)
    b.dma("sp", gn_sb[:, :], gng[:, :], writes=[gn_sb])
    eps_t = b.sb("eps_t", [128, 1], F32)
    b.memset("dve", eps_t[:, :], EPS, [eps_t])
    one_t = b.sb("one_t", [128, 1], F32)
    b.memset("dve", one_t[:, :], 1.0, [one_t])
    P = [b.ps("P%d" % i, [128, 512]) for i in range(7)]
    P7 = b.ps("P7", [128, 1024], BF16)
    cx.pP = [P[0], P[1]]
    cx.pS = [P[2], P[3]]
    cx.pO, cx.pD = P[5], P[6]
    cx.pT8 = P7
    cx.pF = [P[2], P[3]]
    cx.pq = P[4]
    cx.pGo = [P[5], P[6]]
    cx.pGs = P[1]
    hnT = b.sb("hnT", [128, 16, S], BF16)
    cx.hnT = hnT
    ws = WStream(b, 16, nst=2, nbf=8)
    arena = b.sb("arena", [128, 18432], F32)
    off = 0
    xseg_ring = []
    for i in range(2):
        t, off = b.carve(arena, off, [128, 16, 512], F32, "xseg%d" % i)
        xseg_ring.append(t)
    sq_ring = []
    for i in range(2):
        t, off = b.carve(arena, off, [128, 512], BF16, "sq%d" % i)
        sq_ring.append(t)
    rstd_ring = []
    for i in range(2):
        t, off = b.carve(arena, off, [128, 512], F32, "rstd%d" % i)
        rstd_ring.append(t)
    rmsnorm_seg(b, xT, g_sb, hnT, 16, 4, P[0:2], xseg_ring, sq_ring, rstd_ring, cx.ones_bf, eps_t)
    b.barrier()
    b.flush("norm0")
    off = 0
    qT, off = b.carve(arena, off, [128, S], BF16, "qT")
    kT, off = b.carve(arena, off, [128, S], BF16, "kT")
    vT, off = b.carve(arena, off, [128, S], BF16, "vT")
    cx.Vd = []
    for i in range(3):
        t, off = b.carve(arena, off, [128, 16, 128], BF16, "Vd%d" % i)
        cx.Vd.append(t)
    cx.pbuf = []
    for i in range(3):
        t, off = b.carve(arena, off, [128, 512], BF16, "pb%d" % i)
        cx.pbuf.append(t)
    cx.o3, off = b.carve(arena, off, [128, S], F32, "o3")
    cx.d3, off = b.carve(arena, off, [128, S], F32, "d3")
    cx.osum, off = b.carve(arena, off, [128, 512], F32, "osum")
    cx.rec, off = b.carve(arena, off, [128, 512], F32, "rec")
    cx.gs, off = b.carve(arena, off, [128, 512], F32, "gs")
    cx.ybuf = []
    for i in range(2):
        t, off = b.carve(arena, off, [128, 512], BF16, "yb%d" % i)
        cx.ybuf.append(t)
    cx.sidx = 0
    cx.yidx = 0
    outs = []
    hw = {0: [ws.load(wc, gi * 128) for gi in range(4)]}
    for i in range(4):
        for gi, dst in enumerate((qT, kT, vT)):
            w = hw[i][gi]

            def ev(n, bank, dst=dst):
                b.copy("act", dst[:, n * 512:(n + 1) * 512], bank[:, :], [bank], [dst])
            proj_fm(b, w, 128, hnT, 16, S, cx.pP, ev)
        wg = hw[i][3]
        if i + 1 < 4:
            hw[i + 1] = [ws.load(wc, (i + 1) * 512 + gi * 128) for gi in range(4)]

        def yout(bk, yt, i=i):
            outs.append(b.dma("sp", yT[i * 128:(i + 1) * 128, bk * 512:(bk + 1) * 512], yt[:, :], reads=[yt]))
        attention_head(b, cx, qT, kT, vT, wg, yout)
        b.flush("attn%d" % i)
    b.barrier()
    off = 0
    qTr, off = b.carve(arena, off, [128, S], F32, "qTr")
    kTr, off = b.carve(arena, off, [128, S], F32, "kTr")
    ktok, off = b.carve(arena, off, [128, 16, 128], F32, "ktok")
    vtok, off = b.carve(arena, off, [128, 16, 256], BF16, "vtok")
    logat, off = b.carve(arena, off, [128, 16, 128], F32, "logat")
    glT, off = b.carve(arena, off, [16, S], F32, "glT")
    cx.Sst, off = b.carve(arena, off, [128, 256], F32, "Sst")
    def ring2(nm, sh, dt):
        nonlocal off
        r = []
        for i_ in range(2):
            t_, off = b.carve(arena, off, sh, dt, "%s%d" % (nm, i_))
            r.append(t_)
        return r
    cx.Smid = ring2("Smid", [128, 256], BF16)
    cx.E1 = ring2("E1", [128, 132], F32)
    cx.enb = ring2("enb", [128, 128], F32)
    cx.edec = ring2("edec", [128, 128], F32)
    cx.qs = ring2("qs", [128, 128], BF16)
    cx.ks = ring2("ks", [128, 128], BF16)
    cx.kd = ring2("kd", [128, 128], BF16)
    cx.AT = ring2("AT", [128, 128], BF16)
    obS, off = b.carve(arena, off, [128, 2, 512], F32, "obS")
    sqS, off = b.carve(arena, off, [128, 2, 512], BF16, "sqS")
    rs2, off = b.carve(arena, off, [128, 512], F32, "rs2")
    gs2, off = b.carve(arena, off, [128, 512], F32, "gs2")
    tmpE, off = b.carve(arena, off, [128, 512], F32, "tmpE")
    ybs = []
    for i in range(2):
        t, off = b.carve(arena, off, [128, 512], BF16, "ybs%d" % i)
        ybs.append(t)
    assert off <= 18432, off
    wl = ws.load(wc, 4 * 512 + 2 * 512 + 2 * 384, ncols=16)

    def ev_gl(n, bank):
        b.copy("act", glT[:, n * 512:(n + 1) * 512], bank[:16, :], [bank], [glT])
    proj_fm(b, wl, 16, hnT, 16, S, cx.pP, ev_gl)
    yi = [0]
    for i in range(2):
        base = 4 * 512 + i * 512
        w = ws.load(wc, base)

        def ev_q(n, bank):
            b.act(qTr[:, n * 512:(n + 1) * 512], bank[:, :], AF.Copy, [bank], [qTr], scale=128.0 ** -0.5)
        proj_fm(b, w, 128, hnT, 16, S, cx.pP, ev_q)
        w = ws.load(wc, base + 128)

        def ev_k(n, bank):
            b.copy("act", kTr[:, n * 512:(n + 1) * 512], bank[:, :], [bank], [kTr])
        proj_fm(b, w, 128, hnT, 16, S, cx.pP, ev_k)
        tb = 4 * 512 + 2 * 512 + i * 384
        for gi in range(3):
            w = ws.load(wc, tb + gi * 128)
            for n in range(4):
                bank = cx.pP[n % 2]
                for tt_ in range(4):
                    t = n * 4 + tt_
                    items = [(bank[:, tt_ * 128:(tt_ + 1) * 128], hnT[:, c, t * 128:(t + 1) * 128], w[:, c, :]) for c in range(16)]
                    b.mm(bank, items, reads=[w, hnT], skip_group_check=True)
                src = bank[:, :].rearrange("p (a e) -> p a e", e=128)
                if gi == 0:
                    b.copy("act", ktok[:, n * 4:(n + 1) * 4, :], src, [bank], [ktok])
                else:
                    b.copy("act", vtok[:, n * 4:(n + 1) * 4, (gi - 1) * 128:gi * 128], src, [bank], [vtok])
        for n in range(4):
            bank = cx.pP[n % 2]
            for tt_ in range(4):
                t = n * 4 + tt_
                items = [(bank[:, tt_ * 128:(tt_ + 1) * 128], glT[0:16, t * 128:(t + 1) * 128], wg_sb[0:16, i * 128:(i + 1) * 128]),
                         (bank[:, tt_ * 128:(tt_ + 1) * 128], onesf[0:1, 0:128], bg_sb[0:1, i * 128:(i + 1) * 128])]
                b.mm(bank, items, reads=[glT, wg_sb, onesf, bg_sb], skip_group_check=True)
            b.act(tmpE[:, :], bank[:, :], AF.Exp, [bank], [tmpE], scale=-1.0)
            b.act(tmpE[:, :], tmpE[:, :], AF.Ln, [tmpE, one_t], [tmpE], bias=one_t[:, 0:1])
            b.ts("dve", logat[:, n * 4:(n + 1) * 4, :], tmpE[:, :].rearrange("p (a e) -> p a e", e=128), -1.0 / 16.0, None,
                 ALU.mult, None, [tmpE], [logat])
        b.flush("glaproj%d" % i)
        wga = ws.load(wc, base + 256)
        wgb = ws.load(wc, base + 384)
        wgs = [wga, wgb]

        def fin(n, i=i, wgs=wgs):
            for vt in range(2):
                b.copy("act", obS[:, vt, :], cx.pGo[vt][:, :], [cx.pGo[vt]], [obS])
            b.act(sqS[:, :, :], obS[:, :, :], AF.Square, [obS], [sqS])
            pq = cx.pq
            b.mm(pq, [(pq[:, :], cx.ones_bf[:, :], sqS[:, vt, :]) for vt in range(2)], reads=[sqS, cx.ones_bf])
            b.act(rs2[:, :], pq[:, :], AF.Sqrt, [pq, eps_t], [rs2], bias=eps_t[:, 0:1], scale=1.0 / 256.0)
            b.recip(rs2[:, :], rs2[:, :], [rs2], [rs2])
            for vt in range(2):
                pg = cx.pP[vt]
                b.mm(pg, [(pg[:, :], wgs[vt][:, c, :], hnT[:, c, n * 512:(n + 1) * 512]) for c in range(16)], reads=[wgs[vt], hnT])
                b.act(gs2[:, :], pg[:, :], AF.Silu, [pg], [gs2])
                b.tt("dve", obS[:, vt, :], obS[:, vt, :], rs2[:, :], ALU.mult, [obS, rs2], [obS])
                yb_ = ybs[yi[0] % 2]
                yi[0] += 1
                b.stt(yb_[:, :], obS[:, vt, :], gn_sb[:, i * 2 + vt:i * 2 + vt + 1], gs2[:, :], ALU.mult, ALU.mult,
                      [obS, gn_sb, gs2], [yb_])
                r0 = 512 + i * 256 + vt * 128
                outs.append(b.dma("sp", yT[r0:r0 + 128, n * 512:(n + 1) * 512], yb_[:, :], reads=[yb_]))
        gla_head(b, cx, qTr, kTr, ktok, vtok, logat, 256, fin)
        b.flush("glarec%d" % i)
    return outs


def l1_inputs(inp, bidx, hh):
    x = np.asarray(inp["x"])[bidx]
    cols = l0_weight_cols(hh)
    d = {
        "xT": np.ascontiguousarray(x.T),
        "wc1": np.ascontiguousarray(np.asarray(inp["ab_w_in"])[0][:, cols]),
        "ng0": np.ascontiguousarray(np.asarray(inp["norm_g"])[0].reshape(16, 128).T),
        "wgate": np.ascontiguousarray(np.asarray(inp["gla_w_gate"])[0][:, hh * 256:(hh + 1) * 256]),
        "bgate": np.ascontiguousarray(np.asarray(inp["gla_b_gate"])[0][hh * 256:(hh + 1) * 256].reshape(1, 256)),
        "gng": np.ascontiguousarray(np.asarray(inp["gla_norm_g"])[0][hh * 512:(hh + 1) * 512].reshape(4, 128).T),
    }
    return d


NT2 = 1024


def phaseO(nc, b, io, final, glu):
    resT, wo, ng, sel_d = io["resT"], io["wo"], io["ng"], io["sel"]
    if glu:
        wglu, bglu = io["wglu"], io["bglu"]
    if final:
        outT = io["outT"]
    else:
        hTd, hnTd = io["hTd"], io["hnTd"]
    sel = b.sb("sel", [128, 2], F32)
    b.dma("sp", sel[:, :], sel_d[:, :], writes=[sel])
    lda = [b.sb("lda%d" % i, [128, NT2], BF16) for i in range(2)]
    ldb = [b.sb("ldb%d" % i, [128, NT2], BF16) for i in range(2)]
    lk = [0]

    def load_sel(dst_ap, dst_t, srcs):
        A, Bv = srcs
        ta_, tb_ = lda[lk[0] % 2], ldb[lk[0] % 2]
        lk[0] += 1
        b.dma("sp", ta_[:, :], A, writes=[ta_])
        b.dma("sp", tb_[:, :], Bv, writes=[tb_])
        b.ts("dve", ta_[:, :], ta_[:, :], sel[:, 0:1], None, ALU.mult, None, [ta_, sel], [ta_])
        b.stt(dst_ap, tb_[:, :], sel[:, 1:2], ta_[:, :], ALU.mult, ALU.add, [tb_, sel, ta_], [dst_t])
    cs = load_consts(b, nc, ["c_ones"])
    ones_bf = cs["c_ones"]
    g_sb = b.sb("g_sb", [128, 16], F32)
    b.dma("sp", g_sb[:, :], ng[:, :], writes=[g_sb])
    eps_t = b.sb("eps_t", [128, 1], F32)
    b.memset("dve", eps_t[:, :], EPS, [eps_t])
    P = [b.ps("P%d" % i, [128, 512]) for i in range(8)]
    ysb = b.sb("ysb", [128, 16, NT2], BF16)
    for c in range(16):
        load_sel(ysb[:, c, :], ysb, io["ysrc"](c))
    hT = b.sb("hT_sb", [128, 16, NT2], F32)
    wst = [b.sb("wst%d" % i, [128, 8, 512], F32) for i in range(2)]
    wbf = [b.sb("wbf%d" % i, [128, 16, 512], BF16) for i in range(2)]
    wk = [0, 0]

    def load_w(wd, nchunk, col0):
        bf = wbf[wk[1] % 2]
        wk[1] += 1
        for hf in range(nchunk // 8):
            st = wst[wk[0] % 2]
            ce = "pool" if wk[0] % 2 == 0 else "dve"
            wk[0] += 1
            src = wd[hf * 1024:(hf + 1) * 1024, col0:col0 + 512].rearrange("(c p) n -> p c n", p=128)
            b.dma("sp", st[:, :, :], src, writes=[st])
            b.copy(ce, bf[:, hf * 8:(hf + 1) * 8, :], st[:, :, :], [st], [bf])
        return bf
    outs = []
    if glu:
        bg_sb = b.sb("bg_sb", [128, 8], F32)
        b.dma("sp", bg_sb[:, :], bglu[:, :], writes=[bg_sb])
        ycs = b.sb("ycs", [128, 8, NT2], BF16)
        sgs = [b.sb("sgs%d" % i, [128, NT2], BF16) for i in range(2)]
        sig = [b.sb("sig%d" % i, [128, 512], F32) for i in range(2)]
        for mg in range(2):
            w = load_w(wglu, 8, mg * 512)
            for mi in range(4):
                m = mg * 4 + mi
                sg = sgs[m % 2]
                load_sel(sg[:, :], sg, io["sgsrc"](m))
                for n in range(2):
                    bank = P[(m * 2 + n) % 4]
                    sg_ = sig[(m * 2 + n) % 2]
                    b.mm(bank, [(bank[:, :], w[:, c, mi * 128:(mi + 1) * 128], ysb[:, c, n * 512:(n + 1) * 512]) for c in range(8)],
                         reads=[w, ysb])
                    b.act(sg_[:, :], bank[:, :], AF.Sigmoid, [bank, bg_sb], [sg_], bias=bg_sb[:, m:m + 1])
                    b.tt("dve", sg_[:, :], sg_[:, :], ysb[:, m, n * 512:(n + 1) * 512], ALU.mult, [sg_, ysb], [sg_])
                    b.tt("dve", ycs[:, m, n * 512:(n + 1) * 512], sg_[:, :], sg[:, n * 512:(n + 1) * 512], ALU.mult, [sg_, sg], [ycs])
    b.flush("ldglu")
    xs_ring = [b.sb("xs%d" % i, [128, NT2], F32) for i in range(2)]
    for p in range(4):
        w = load_w(wo, 16, p * 512)
        for mi in range(4):
            m = p * 4 + mi
            xs = xs_ring[m % 2]
            b.dma("sp", xs[:, :], resT[m * 128:(m + 1) * 128, :], writes=[xs])
            for n in range(2):
                bank = P[(m * 2 + n) % 8]
                items = []
                for c in range(16):
                    if glu and c < 8:
                        rhs = ycs[:, c, n * 512:(n + 1) * 512]
                    else:
                        rhs = ysb[:, c, n * 512:(n + 1) * 512]
                    items.append((bank[:, :], w[:, c, mi * 128:(mi + 1) * 128], rhs))
                b.mm(bank, items, reads=[w, ysb] + ([ycs] if glu else []))
                b.tt("dve", hT[:, m, n * 512:(n + 1) * 512], bank[:, :], xs[:, n * 512:(n + 1) * 512], ALU.add, [bank, xs], [hT])
            if not final:
                outs.append(b.dma("sp", hTd[m * 128:(m + 1) * 128, :], hT[:, m, :], reads=[hT]))
    b.flush("oproj")
    b.barrier()
    fl0 = T(wst[0].t[:, :, :].rearrange("p a b -> p (a b)"), "fl0")
    fl1 = T(wst[1].t[:, :, :].rearrange("p a b -> p (a b)"), "fl1")
    off = 0
    sq_ring = []
    for i in range(2):
        t, off = b.carve(fl0, off, [128, NT2], BF16, "sq%d" % i)
        sq_ring.append(t)
    rstd, off = b.carve(fl0, off, [128, NT2], F32, "rstd")
    for c in range(16):
        sq = sq_ring[c % 2]
        b.act(sq[:, :], hT[:, c, :], AF.Square, [hT], [sq])
        for n in range(2):
            b.mm(P[4 + n], [(P[4 + n][:, :], ones_bf[:, :], sq[:, n * 512:(n + 1) * 512])], reads=[sq, ones_bf],
                 start=(c == 0), stop=(c == 15))
    for n in range(2):
        b.act(rstd[:, n * 512:(n + 1) * 512], P[4 + n][:, :], AF.Sqrt, [P[4 + n], eps_t], [rstd],
              bias=eps_t[:, 0:1], scale=1.0 / D)
    b.recip(rstd[:, :], rstd[:, :], [rstd], [rstd])
    ob = []
    off = 0
    for i in range(2):
        t, off = b.carve(fl1, off, [128, NT2], F32 if final else BF16, "ob%d" % i)
        ob.append(t)
    for c in range(16):
        o = ob[c % 2]
        b.stt(o[:, :], hT[:, c, :], g_sb[:, c:c + 1], rstd[:, :], ALU.mult, ALU.mult, [hT, g_sb, rstd], [o])
        dst = outT if final else hnTd
        outs.append(b.dma("sp", dst[c * 128:(c + 1) * 128, :], o[:, :], reads=[o]))
    return outs
NCOL_L1 = 512 + 512 + 4 * (384 + 256)
TWO_PI = 6.283185307179586


def l1_weight_cols(hh):
    cols = list(range(512 * hh, 512 * hh + 512))
    cols += list(range(4096 + 512 * hh, 4096 + 512 * hh + 512))
    for i in range(4):
        hg = 4 * hh + i
        cols += list(range(1024 + hg * 128, 1024 + hg * 128 + 128))
        cols += list(range(2048 + hg * 128, 2048 + hg * 128 + 128))
        cols += list(range(5120 + hg * 128, 5120 + hg * 128 + 128))
    for i in range(4):
        hg = 4 * hh + i
        cols += list(range(2048 + hg * 128, 2048 + hg * 128 + 128))
        cols += list(range(3072 + hg * 128, 3072 + hg * 128 + 128))
    assert len(cols) == NCOL_L1
    return np.array(cols)


def l3_inputs(inp, hh):
    g0 = 32 * hh
    lam_re = np.asarray(inp["s5_lam_re"])[0][g0:g0 + 32]
    lam_im = np.asarray(inp["s5_lam_im"])[0][g0:g0 + 32]
    logdt = np.asarray(inp["s5_log_dt"])[0][g0:g0 + 32]
    b_re = np.asarray(inp["s5_b_re"])[0][g0:g0 + 32]
    b_im = np.asarray(inp["s5_b_im"])[0][g0:g0 + 32]
    c_re = np.asarray(inp["s5_c_re"])[0][g0:g0 + 32]
    c_im = np.asarray(inp["s5_c_im"])[0][g0:g0 + 32]
    dsk = np.asarray(inp["s5_d"])[0][g0:g0 + 32]
    dup = lambda a: np.ascontiguousarray(np.concatenate([a.T, a.T], 0))
    BA = np.zeros((128, 4, 4, 128), np.float32)
    for gl in range(32):
        ut, q = gl // 8, gl % 8
        r0 = 32 * (q // 2) + 16 * (q % 2)
        slot = (q % 2) if q // 2 < 3 else 2 + (q % 2)
        BA[r0:r0 + 16, ut, slot, 0:64] = b_re[gl].T
        BA[r0:r0 + 16, ut, slot, 64:128] = b_im[gl].T
    cT = lambda c: np.ascontiguousarray(np.concatenate([c.transpose(2, 0, 1)] * 2, 0))
    gam = np.asarray(inp["hgrn_gamma"])[:, 512 * hh:512 * hh + 512]
    d = {
        "wc3": np.ascontiguousarray(np.asarray(inp["cd_w_in"])[0][:, l1_weight_cols(hh)]),
        "lamre": dup(lam_re), "lamim": dup(lam_im),
        "logdt": np.ascontiguousarray(np.broadcast_to(logdt[None, :], (128, 32))),
        "cre": cT(c_re), "cim": cT(c_im), "BA": BA,
        "dsk": np.ascontiguousarray(dsk.reshape(4, 128).T),
        "ttf": np.ascontiguousarray(np.broadcast_to(np.arange(2048, dtype=np.float32)[None, :], (128, 2048))),
        "gamc": np.ascontiguousarray(np.stack([gam[0].reshape(4, 128).T, gam[1].reshape(4, 128).T], 1)),
        "gamr": np.ascontiguousarray(gam.reshape(1, 1024)),
        "hng": np.ascontiguousarray(np.asarray(inp["hgrn_norm_g"])[0][512 * hh:512 * hh + 512].reshape(4, 128).T),
    }
    return d


def phase3(nc, b, io):
    wc = io["wc3"]
    lamre_d, lamim_d, logdt_d, cre_d, cim_d, BA_d, dsk_d, ttf_d, gamc_d, gamr_d, hng_d = (
        io[k] for k in ("lamre", "lamim", "logdt", "cre", "cim", "BA", "dsk", "ttf", "gamc", "gamr", "hng"))
    ysT, sgT, ydT = io["ys"], io["sg"], io["yd"]
    cs = load_consts(b, nc, ["c_ones", "c_tri", "c_ubd", "c_mbd", "c_onesf"])
    cx = Ctx()
    cx.ones_bf = cs["c_ones"]
    cx.tri, cx.ubd, cx.mbd, onesf = cs["c_tri"], cs["c_ubd"], cs["c_mbd"], cs["c_onesf"]
    outs = []

    def ld(name, dap, sh, dt=F32):
        t = b.sb("s_" + name, sh, dt)
        idx = tuple(slice(None) for _ in sh)
        b.dma("sp", t[idx], dap[idx], writes=[t])
        return t
    lamre, lamim, logdt = ld("lamre", lamre_d, [128, 32]), ld("lamim", lamim_d, [128, 32]), ld("logdt", logdt_d, [128, 32])
    cre, cim = ld("cre", cre_d, [128, 32, 16]), ld("cim", cim_d, [128, 32, 16])
    dsk, ttf = ld("dsk", dsk_d, [128, 4]), ld("ttf", ttf_d, [128, S])
    gamc, hng = ld("gamc", gamc_d, [128, 2, 4]), ld("hng", hng_d, [128, 4])
    gamr = ld("gamr", gamr_d, [1, 1024])
    eps_t = b.sb("eps_t", [128, 1], F32)
    b.memset("dve", eps_t[:, :], EPS, [eps_t])
    P = [b.ps("P%d" % i, [128, 512]) for i in range(8)]
    hnT = b.sb("hnT_sb", [128, 16, S], BF16)
    cx.hnT = hnT
    for c in range(16):
        for (c0, ap_) in io["hnsrc"](c):
            b.dma("sp", hnT[:, c, c0:c0 + ap_.shape[1]], ap_, writes=[hnT])
    ws = WStream(b, 16, nst=2, nbf=2)
    arena = b.sb("arena", [128, 17408], F32)
    sm = lambda n, sh=(128, 32), dt=F32: b.sb(n, list(sh), dt)
    dt_, rho, th = sm("dt_"), sm("rho"), sm("th")
    b.act(dt_[:, :], logdt[:, :], AF.Exp, [logdt], [dt_])
    t1 = sm("t1")
    b.tt("dve", t1[:, :], lamre[:, :], dt_[:, :], ALU.mult, [lamre, dt_], [t1])
    b.act(rho[:, :], t1[:, :], AF.Exp, [t1], [rho])
    b.tt("dve", th[:, :], lamim[:, :], dt_[:, :], ALU.mult, [lamim, dt_], [th])
    thn = sm("thn")
    b.ts("dve", thn[0:64, :], th[0:64, :], 1.0 / TWO_PI, None, ALU.mult, None, [th], [thn])
    b.ts("dve", thn[64:128, :], th[64:128, :], -1.0 / TWO_PI, None, ALU.mult, None, [th], [thn])
    un, fr, sa, ca = sm("un"), sm("fr"), sm("sa"), sm("ca")
    q25 = sm("q25", (128, 1))
    b.memset("dve", q25[:, :], 0.25, [q25])
    MAGIC = 12582912.0
    b.ts("dve", un[:, :], th[:, :], 1.0 / TWO_PI, None, ALU.mult, None, [th], [un])
    b.ts("dve", fr[:, :], un[:, :], MAGIC, -MAGIC, ALU.add, ALU.add, [un], [fr])
    b.tt("dve", sa[:, :], un[:, :], fr[:, :], ALU.subtract, [un, fr], [sa])
    b.ts("dve", un[:, :], un[:, :], 0.25, None, ALU.add, None, [un], [un])
    b.ts("dve", fr[:, :], un[:, :], MAGIC, -MAGIC, ALU.add, ALU.add, [un], [fr])
    b.tt("dve", ca[:, :], un[:, :], fr[:, :], ALU.subtract, [un, fr], [ca])
    sn, cn = sm("sn"), sm("cn")
    b.act(sn[:, :], sa[:, :], AF.Sin, [sa], [sn], scale=TWO_PI)
    b.act(cn[:, :], ca[:, :], AF.Sin, [ca], [cn], scale=TWO_PI)
    cD, sD, nsD = sm("cD"), sm("sD"), sm("nsD")
    u5, f5, a5 = sm("u5"), sm("f5"), sm("a5")
    b.ts("dve", u5[:, :], thn[:, :], 512.0, None, ALU.mult, None, [thn], [u5])
    b.ts("dve", f5[:, :], u5[:, :], MAGIC, -MAGIC, ALU.add, ALU.add, [u5], [f5])
    b.tt("dve", a5[:, :], u5[:, :], f5[:, :], ALU.subtract, [u5, f5], [a5])
    b.act(sD[:, :], a5[:, :], AF.Sin, [a5], [sD], scale=TWO_PI)
    b.ts("dve", u5[:, :], u5[:, :], 0.25, None, ALU.add, None, [u5], [u5])
    b.ts("dve", f5[:, :], u5[:, :], MAGIC, -MAGIC, ALU.add, ALU.add, [u5], [f5])
    b.tt("dve", a5[:, :], u5[:, :], f5[:, :], ALU.subtract, [u5, f5], [a5])
    b.act(cD[:, :], a5[:, :], AF.Sin, [a5], [cD], scale=TWO_PI)
    b.ts("dve", nsD[:, :], sD[:, :], -1.0, None, ALU.mult, None, [sD], [nsD])
    are, aim = sm("are"), sm("aim")
    b.tt("dve", are[:, :], rho[:, :], cn[:, :], ALU.mult, [rho, cn], [are])
    b.tt("dve", aim[:, :], rho[:, :], sn[:, :], ALU.mult, [rho, sn], [aim])
    nr, inv, t2, t3, core_, coim = sm("nr"), sm("inv"), sm("t2"), sm("t3"), sm("core_"), sm("coim")
    b.ts("dve", nr[:, :], are[:, :], -1.0, None, ALU.add, None, [are], [nr])
    b.tt("dve", t2[:, :], lamre[:, :], lamre[:, :], ALU.mult, [lamre], [t2])
    b.tt("dve", t3[:, :], lamim[:, :], lamim[:, :], ALU.mult, [lamim], [t3])
    b.tt("dve", t2[:, :], t2[:, :], t3[:, :], ALU.add, [t2, t3], [t2])
    b.recip(inv[:, :], t2[:, :], [t2], [inv])
    b.tt("dve", t2[:, :], nr[:, :], lamre[:, :], ALU.mult, [nr, lamre], [t2])
    b.tt("dve", t3[:, :], aim[:, :], lamim[:, :], ALU.mult, [aim, lamim], [t3])
    b.tt("dve", t2[:, :], t2[:, :], t3[:, :], ALU.add, [t2, t3], [t2])
    b.tt("dve", core_[:, :], t2[:, :], inv[:, :], ALU.mult, [t2, inv], [core_])
    b.tt("dve", t2[:, :], aim[:, :], lamre[:, :], ALU.mult, [aim, lamre], [t2])
    b.tt("dve", t3[:, :], nr[:, :], lamim[:, :], ALU.mult, [nr, lamim], [t3])
    b.tt("dve", t2[:, :], t2[:, :], t3[:, :], ALU.subtract, [t2, t3], [t2])
    b.tt("dve", coim[:, :], t2[:, :], inv[:, :], ALU.mult, [t2, inv], [coim])
    Ya, Yb, Yc, Yd = sm("Ya"), sm("Yb"), sm("Yc"), sm("Yd")
    top, bot = slice(0, 64), slice(64, 128)
    b.copy("dve", Ya[top, :], core_[top, :], [core_], [Ya])
    b.ts("dve", Ya[bot, :], coim[bot, :], -1.0, None, ALU.mult, None, [coim], [Ya])
    b.ts("dve", Yb[top, :], coim[top, :], -1.0, None, ALU.mult, None, [coim], [Yb])
    b.ts("dve", Yb[bot, :], core_[bot, :], -1.0, None, ALU.mult, None, [core_], [Yb])
    b.ts("dve", Yc[top, :], coim[top, :], -1.0, None, ALU.mult, None, [coim], [Yc])
    b.copy("dve", Yc[bot, :], core_[bot, :], [core_], [Yc])
    b.ts("dve", Yd[top, :], core_[top, :], -1.0, None, ALU.mult, None, [core_], [Yd])
    b.ts("dve", Yd[bot, :], coim[bot, :], -1.0, None, ALU.mult, None, [coim], [Yd])
    L1 = b.sb("L1", [128, 32, 16], F32)
    L2 = b.sb("L2", [128, 32, 16], F32)
    tmpT = ws.st[1]
    tmpL = tmpT[:, 0:4, :].rearrange("p a (g h) -> p (a g) h", h=16)
    bc = lambda t_: t_[:, :].unsqueeze(2).to_broadcast([128, 32, 16])
    for (L, Y0, Y1) in ((L1, Ya, Yb), (L2, Yc, Yd)):
        b.tt("dve", L[:, :, :], cre[:, :, :], bc(Y0), ALU.mult, [cre, Y0], [L])
        b.tt("dve", tmpL, cim[:, :, :], bc(Y1), ALU.mult, [cim, Y1], [tmpT])
        b.tt("dve", L[:, :, :], L[:, :, :], tmpL, ALU.add, [L, tmpT], [L])
    LP1 = b.sb("LP1", [128, 32, 128], BF16)
    LP2 = b.sb("LP2", [128, 32, 128], BF16)
    b.memset("pool", LP1[:, :, :], 0.0, [LP1])
    b.memset("pool", LP2[:, :, :], 0.0, [LP2])
    for q in range(8):
        b.copy("dve", LP1[:, q:32:8, 16 * q:16 * q + 16], L1[:, q:32:8, :], [L1], [LP1])
        b.copy("dve", LP2[:, q:32:8, 16 * q:16 * q + 16], L2[:, q:32:8, :], [L2], [LP2])
    BAb = b.sb("BAb", [128, 4, 4, 128], BF16)
    b.dma("sp", ws.st[0][:, :, :], BA_d.rearrange("p a b c -> p (a b) c"), writes=[ws.st[0]])
    b.copy("pool", BAb[:, :, :, :].rearrange("p a b c -> p (a b) c"), ws.st[0][:, :, :], [ws.st[0]], [BAb])
    b.flush("s5prep")
    off = 0
    uT_all, off = b.carve(arena, off, [128, 4, S], BF16, "uT")
    uTs = [T(uT_all.t[:, ut, :], "uT%d" % ut) for ut in range(4)]
    for ut in range(4):
        w = ws.load(wc, ut * 128)

        def ev_u(n, bank, ut=ut):
            b.copy("act", uTs[ut][:, n * 512:(n + 1) * 512], bank[:, :], [bank], [uTs[ut]])
        proj_fm(b, w, 128, hnT, 16, S, P[0:2], ev_u)
    sgb = []
    for i in range(2):
        t, off = b.carve(arena, off, [128, 512], BF16, "sgb%d" % i)
        sgb.append(t)
    k_ = [0]
    for ut in range(4):
        w = ws.load(wc, 512 + ut * 128)

        def ev_g(n, bank, ut=ut):
            o = sgb[k_[0] % 2]
            k_[0] += 1
            b.act(o[:, :], bank[:, :], AF.Silu, [bank], [o])
            outs.append(b.dma("sp", sgT[ut * 128:(ut + 1) * 128, n * 512:(n + 1) * 512], o[:, :], reads=[o]))
        proj_fm(b, w, 128, hnT, 16, S, P[0:2], ev_g)
    b.flush("s5proj")
    def ring(nm, k, dt=F32):
        nonlocal off
        r = []
        for i in range(k):
            t, off = b.carve(arena, off, [128, 512], dt, "%s%d" % (nm, i))
            r.append(t)
        return r
    rhoT = ring("rhoT", 2)
    un5, un6, fr5 = (ring(nm, 1)[0] for nm in ("un5", "un6", "fr5"))
    CS1, CS2 = ring("CS1", 3), ring("CS2", 3)
    tC, tS = ring("tC", 2), ring("tS", 2)
    ta, tb = ring("ta", 2), ring("tb", 2)
    wst = ring("wst", 3)
    P1b, P2b = ring("P1b", 2, BF16), ring("P2b", 2, BF16)
    yp, x3, sgm = un5, un6, fr5
    yob = ring("yob", 2, BF16)
    assert off <= 17408, off
    ybanks = P[4:8]
    its = [(ut, q, n) for ut in range(4) for q in range(8) for n in range(4)]
    prevw = [None]

    def stage_T(i):
        ut, q, n = its[i]
        gl = ut * 8 + q
        c1, c2 = CS1[i % 3], CS2[i % 3]
        if n == 0:
            rt = rhoT[gl % 2]
            b.ts("dve", rt[:, :], ttf[:, 0:512], 0.0, rho[:, gl:gl + 1], ALU.mult, ALU.add, [ttf, rho], [rt])
            b.act(un5[:, :], ttf[:, 0:512], AF.Copy, [ttf, thn], [un5], scale=thn[:, gl:gl + 1])
            b.act(un6[:, :], ttf[:, 0:512], AF.Identity, [ttf, thn, q25], [un6], scale=thn[:, gl:gl + 1], bias=q25[:, 0:1])
            b.ts("dve", fr5[:, :], un5[:, :], MAGIC, -MAGIC, ALU.add, ALU.add, [un5], [fr5])
            b.tt("dve", un5[:, :], un5[:, :], fr5[:, :], ALU.subtract, [un5, fr5], [un5])
            b.act(c2[:, :], un5[:, :], AF.Sin, [un5], [c2], scale=TWO_PI)
            b.ts("dve", fr5[:, :], un6[:, :], MAGIC, -MAGIC, ALU.add, ALU.add, [un6], [fr5])
            b.tt("dve", un6[:, :], un6[:, :], fr5[:, :], ALU.subtract, [un6, fr5], [un6])
            b.act(c1[:, :], un6[:, :], AF.Sin, [un6], [c1], scale=TWO_PI)
        else:
            p1, p2 = CS1[(i - 1) % 3], CS2[(i - 1) % 3]
            tc_, ts_ = tC[i % 2], tS[i % 2]
            b.act(tc_[:, :], p1[:, :], AF.Copy, [p1, cD], [tc_], scale=cD[:, gl:gl + 1])
            b.act(c1[:, :], p2[:, :], AF.Copy, [p2, nsD], [c1], scale=nsD[:, gl:gl + 1])
            b.act(ts_[:, :], p2[:, :], AF.Copy, [p2, cD], [ts_], scale=cD[:, gl:gl + 1])
            b.act(c2[:, :], p1[:, :], AF.Copy, [p1, sD], [c2], scale=sD[:, gl:gl + 1])
            b.tt("pool", c1[:, :], c1[:, :], tc_[:, :], ALU.add, [c1, tc_], [c1])
            b.tt("pool", c2[:, :], c2[:, :], ts_[:, :], ALU.add, [c2, ts_], [c2])

    def stage_M(i):
        ut, q, n = its[i]
        gl = ut * 8 + q
        if q // 2 < 3:
            rows = slice(32 * (q // 2), 32 * (q // 2) + 32)
            slot = q % 2
        else:
            rows = slice(64, 128)
            slot = 2 + q % 2
        c1, c2 = CS1[i % 3], CS2[i % 3]
        rt = rhoT[gl % 2]
        pa, pb_ = P[(i % 2) * 2], P[(i % 2) * 2 + 1]
        ur = uTs[ut][rows, n * 512:(n + 1) * 512]
        b.mm(pa, [(pa[:, :], BAb[rows, ut, slot, :], ur)], reads=[BAb, uTs[ut]])
        b.op("pe", lambda e, o=pb_[0:64, :], l=BAb[rows, ut, slot, 64:128], r=ur: e.matmul(o, l, r, start=True, stop=True, skip_group_check=True),
             reads=[BAb, uTs[ut]], writes=[pb_], inc=False)
        b.op("pe", lambda e, o=pb_[64:128, :], l=BAb[rows, ut, slot, 0:64], r=ur: e.matmul(o, l, r, start=True, stop=True, skip_group_check=True),
             reads=(), writes=[pb_])
        ta_, tb_ = ta[i % 2], tb[i % 2]
        b.tt("dve", ta_[:, :], pa[:, :], c1[:, :], ALU.mult, [pa, c1], [ta_])
        b.tt("dve", tb_[:, :], pb_[:, :], c2[:, :], ALU.mult, [pb_, c2], [tb_])
        b.tt("pool", ta_[:, :], ta_[:, :], tb_[:, :], ALU.add, [ta_, tb_], [ta_])

    def stage_M2(i):
        ut, q, n = its[i]
        gl = ut * 8 + q
        c1, c2 = CS1[i % 3], CS2[i % 3]
        rt = rhoT[gl % 2]
        ta_ = ta[i % 2]
        wt = wst[i % 3]
        if n == 0:
            b.op("dve", lambda g, wt=wt: g.tensor_tensor_scan(wt[:, :], rt[:, :], ta_[:, :], 0.0, ALU.mult, ALU.add),
                 [rt, ta_], [wt])
        else:
            prev = prevw[0]
            b.op("dve", lambda g, wt=wt, prev=prev: g.tensor_tensor_scan(wt[:, :], rt[:, :], ta_[:, :], prev[:, 511:512], ALU.mult, ALU.add),
                 [rt, ta_, prev], [wt])
        prevw[0] = wt
        p1b, p2b = P1b[i % 2], P2b[i % 2]
        b.tt("dve", p1b[:, :], wt[:, :], c1[:, :], ALU.mult, [wt, c1], [p1b])
        b.tt("dve", p2b[:, :], wt[:, :], c2[:, :], ALU.mult, [wt, c2], [p2b])
        yb = ybanks[n]
        b.mm(yb, [(yb[:, :], LP1[:, gl, :], p1b[:, :]), (yb[:, :], LP2[:, gl, :], p2b[:, :])],
             reads=[LP1, LP2, p1b, p2b], start=(q == 0), stop=(q == 7))
        if q == 7 and n == 3:
            fin_ut(ut)

    def fin_ut(ut):
        for n in range(4):
            yb = ybanks[n]
            b.stt(yp[:, :], uTs[ut][:, n * 512:(n + 1) * 512], dsk[:, ut:ut + 1], yb[:, :], ALU.mult, ALU.add, [uTs[ut], dsk, yb], [yp])
            b.tt("dve", x3[:, :], yp[:, :], yp[:, :], ALU.mult, [yp], [x3])
            b.ts("dve", x3[:, :], x3[:, :], 0.044715, 1.0, ALU.mult, ALU.add, [x3], [x3])
            b.tt("dve", x3[:, :], x3[:, :], yp[:, :], ALU.mult, [x3, yp], [x3])
            b.act(sgm[:, :], x3[:, :], AF.Sigmoid, [x3], [sgm], scale=2.0 * 0.7978845608028654)
            o = yob[(ut * 4 + n) % 2]
            b.tt("dve", o[:, :], yp[:, :], sgm[:, :], ALU.mult, [yp, sgm], [o])
            outs.append(b.dma("sp", ysT[ut * 128:(ut + 1) * 128, n * 512:(n + 1) * 512], o[:, :], reads=[o]))
    NI = len(its)
    stage_T(0)
    stage_T(1)
    stage_M(0)
    for i in range(NI):
        if i + 2 < NI:
            stage_T(i + 2)
        if i + 1 < NI:
            stage_M(i + 1)
        stage_M2(i)
    b.barrier()
    b.flush("s5rec")
    lbc, omlc, nomlc = sm("lbc", (128, 4)), sm("omlc", (128, 4)), sm("nomlc", (128, 4))
    b.tt("dve", lbc[:, :], gamc[:, 1, :], gamc[:, 0, :], ALU.subtract, [gamc], [lbc])
    b.act(lbc[:, :], lbc[:, :], AF.Sigmoid, [lbc], [lbc])
    b.ts("dve", omlc[:, :], lbc[:, :], -1.0, 1.0, ALU.mult, ALU.add, [lbc], [omlc])
    b.ts("dve", nomlc[:, :], omlc[:, :], -1.0, None, ALU.mult, None, [omlc], [nomlc])
    lbr = b.sb("lbr", [1, 512], F32)
    omlr = b.sb("omlr", [1, 512], F32)
    b.tt("dve", lbr[:, :], gamr[:, 512:1024], gamr[:, 0:512], ALU.subtract, [gamr], [lbr])
    b.act(lbr[:, :], lbr[:, :], AF.Sigmoid, [lbr], [lbr])
    b.ts("dve", omlr[:, :], lbr[:, :], -1.0, 1.0, ALU.mult, ALU.add, [lbr], [omlr])
    off = 0
    qTr, off = b.carve(arena, off, [128, S], F32, "qTr")
    kTr, off = b.carve(arena, off, [128, S], F32, "kTr")
    ktok, off = b.carve(arena, off, [128, 16, 128], F32, "ktok")
    vtok, off = b.carve(arena, off, [128, 16, 128], BF16, "vtok")
    logat, off = b.carve(arena, off, [128, 16, 128], F32, "logat")
    cx.Sst, off = b.carve(arena, off, [128, 256], F32, "Sst")
    def ring2(nm, sh, dt):
        nonlocal off
        r = []
        for i_ in range(2):
            t_, off = b.carve(arena, off, sh, dt, "%s%d" % (nm, i_))
            r.append(t_)
        return r
    cx.Smid = ring2("Smid", [128, 256], BF16)
    cx.E1 = ring2("E1", [128, 132], F32)
    cx.enb = ring2("enb", [128, 128], F32)
    cx.edec = ring2("edec", [128, 128], F32)
    cx.qs = ring2("qs", [128, 128], BF16)
    cx.ks = ring2("ks", [128, 128], BF16)
    cx.kd = ring2("kd", [128, 128], BF16)
    cx.AT = ring2("AT", [128, 128], BF16)
    obS, off = b.carve(arena, off, [128, 512], F32, "obS")
    sqS, off = b.carve(arena, off, [128, 512], BF16, "sqS")
    rs2, off = b.carve(arena, off, [128, 512], F32, "rs2")
    gs2, off = b.carve(arena, off, [128, 512], F32, "gs2")
    sgk, off = b.carve(arena, off, [128, 512], F32, "sgk")
    omlb, off = b.carve(arena, off, [128, 512], F32, "omlb")
    ybs = []
    for i in range(2):
        t, off = b.carve(arena, off, [128, 512], BF16, "ybs%d" % i)
        ybs.append(t)
    assert off <= 17408, off
    cx.pP = [P[0], P[1]]
    cx.pF = [P[2], P[3]]
    cx.pq = P[4]
    cx.pGo = [P[5]]
    cx.pGs = P[6]
    yi = [0]
    for i in range(4):
        base = 1024 + i * 384
        w = ws.load(wc, base)

        def ev_q(n, bank):
            b.act(qTr[:, n * 512:(n + 1) * 512], bank[:, :], AF.Silu, [bank], [qTr])
        proj_fm(b, w, 128, hnT, 16, S, cx.pP, ev_q)
        w = ws.load(wc, base + 128)

        def ev_k(n, bank, i=i):
            b.act(kTr[:, n * 512:(n + 1) * 512], bank[:, :], AF.Sigmoid, [bank], [kTr])
            b.ts("dve", kTr[:, n * 512:(n + 1) * 512], kTr[:, n * 512:(n + 1) * 512], nomlc[:, i:i + 1], omlc[:, i:i + 1],
                 ALU.mult, ALU.add, [kTr, nomlc, omlc], [kTr])
        proj_fm(b, w, 128, hnT, 16, S, cx.pP, ev_k)
        pbc = P[7]
        for a in range(4):
            b.mm(pbc, [(pbc[:, a * 128:(a + 1) * 128], onesf[0:1, 0:128], omlr[0:1, i * 128:(i + 1) * 128])],
                 reads=[onesf, omlr], skip_group_check=True)
        b.copy("act", omlb[:, :], pbc[:, :], [pbc], [omlb])
        tb_ = 1024 + 4 * 384 + i * 256
        for gi in range(2):
            w = ws.load(wc, tb_ + gi * 128)
            for n in range(4):
                bank = cx.pP[n % 2]
                for tt_ in range(4):
                    t = n * 4 + tt_
                    items = [(bank[:, tt_ * 128:(tt_ + 1) * 128], hnT[:, c, t * 128:(t + 1) * 128], w[:, c, :]) for c in range(16)]
                    b.mm(bank, items, reads=[w, hnT], skip_group_check=True)
                kt3 = ktok[:, n * 4:(n + 1) * 4, :]
                if gi == 0:
                    b.act(sgk[:, :], bank[:, :], AF.Sigmoid, [bank], [sgk])
                    b.tt("dve", sgk[:, :], sgk[:, :], omlb[:, :], ALU.mult, [sgk, omlb], [sgk])
                    b.tt("dve", kt3, omlb[:, :].rearrange("p (a e) -> p a e", e=128), sgk[:, :].rearrange("p (a e) -> p a e", e=128),
                         ALU.subtract, [omlb, sgk], [ktok])
                    b.ts("dve", sgk[:, :].rearrange("p (a e) -> p a e", e=128), kt3, -1.0, 1.0, ALU.mult, ALU.add, [ktok], [sgk])
                    b.act(logat[:, n * 4:(n + 1) * 4, :], sgk[:, :].rearrange("p (a e) -> p a e", e=128), AF.Ln, [sgk], [logat])
                else:
                    b.copy("act", vtok[:, n * 4:(n + 1) * 4, :], bank[:, :].rearrange("p (a e) -> p a e", e=128), [bank], [vtok])
        b.flush("hgproj%d" % i)
        wg = ws.load(wc, base + 256)

        def fin(n, i=i, wg=wg):
            b.copy("act", obS[:, :], cx.pGo[0][:, :], [cx.pGo[0]], [obS])
            b.act(sqS[:, :], obS[:, :], AF.Square, [obS], [sqS])
            pq = cx.pq
            b.mm(pq, [(pq[:, :], cx.ones_bf[:, :], sqS[:, :])], reads=[sqS, cx.ones_bf])
            b.act(rs2[:, :], pq[:, :], AF.Sqrt, [pq, eps_t], [rs2], bias=eps_t[:, 0:1], scale=1.0 / 128.0)
            b.recip(rs2[:, :], rs2[:, :], [rs2], [rs2])
            pg = cx.pP[0]
            b.mm(pg, [(pg[:, :], wg[:, c, :], hnT[:, c, n * 512:(n + 1) * 512]) for c in range(16)], reads=[wg, hnT])
            b.act(gs2[:, :], pg[:, :], AF.Silu, [pg], [gs2])
            b.tt("dve", obS[:, :], obS[:, :], rs2[:, :], ALU.mult, [obS, rs2], [obS])
            yb_ = ybs[yi[0] % 2]
            yi[0] += 1
            b.stt(yb_[:, :], obS[:, :], hng[:, i:i + 1], gs2[:, :], ALU.mult, ALU.mult, [obS, hng, gs2], [yb_])
            outs.append(b.dma("sp", ydT[i * 128:(i + 1) * 128, n * 512:(n + 1) * 512], yb_[:, :], reads=[yb_]))
        gla_head(b, cx, qTr, kTr, ktok, vtok, logat, 128, fin)
        b.flush("hgrec%d" % i)
    return outs


from concourse.bass_utils import run_bass_kernel_spmd

PAIRS = [[0, 1], [2, 3], [4, 5], [6, 7]]
L3_KEYS = ("wc3", "lamre", "lamim", "logdt", "cre", "cim", "BA", "dsk", "ttf", "gamc", "gamr", "hng")
L3_SHAPES = {"wc3": [D, NCOL_L1], "lamre": [128, 32], "lamim": [128, 32], "logdt": [128, 32], "cre": [128, 32, 16],
             "cim": [128, 32, 16], "BA": [128, 4, 4, 128], "dsk": [128, 4], "ttf": [128, S], "gamc": [128, 2, 4],
             "gamr": [1, 1024], "hng": [128, 4]}


def build_fused():
    nc = bass.Bass("TRN2", target_bir_lowering=False)
    b = B(nc)

    def din(n, sh, dt=F32):
        return nc.dram_tensor(n, list(sh), dt, kind="ExternalInput").ap()
    io = {"xT": din("xT", [D, S]), "xres": din("xres", [D, 1024]), "wc1": din("wc1", [D, NCOL_L0]),
          "ng0": din("ng0", [128, 16]), "wgate": din("wgate", [16, 256]), "bgate": din("bgate", [1, 256]),
          "gng": din("gng", [128, 4]), "wo0": din("wo0", [D, D]), "ng1": din("ng1", [128, 16]),
          "sel": din("sel", [128, 2]), "wo1": din("wo1", [D, D]), "ngf": din("ngf", [128, 16]),
          "wglu": din("wglu", [1024, 1024]), "bglu": din("bglu", [128, 8])}
    for k in L3_KEYS:
        io[k] = din(k, L3_SHAPES[k])
    outT = nc.dram_tensor("outT", [D, 1024], F32, kind="ExternalOutput").ap()
    y0m = [nc.dram_tensor("y0m%d" % k, [512, S], BF16) for k in range(2)]
    y0a = [nc.dram_tensor("y0a%d" % k, [1024, S], BF16) for k in range(2)]
    hnm = [nc.dram_tensor("hnm%d" % k, [1024, 1024], BF16) for k in range(2)]
    hna = [nc.dram_tensor("hna%d" % k, [2048, 1024], BF16) for k in range(2)]
    y1m = [nc.dram_tensor("y1m%d" % k, [512, S], BF16) for k in range(3)]
    y1a = [nc.dram_tensor("y1a%d" % k, [1024, S], BF16) for k in range(3)]
    h1s = nc.dram_tensor("h1s", [D, 1024], F32)

    class Rows:
        def __init__(self, pieces, rows_per):
            self.p = [t.ap() for t in pieces]
            self.n = rows_per

        def __getitem__(self, idx):
            rs, cs = idx
            k = rs.start // self.n
            assert (rs.stop - 1) // self.n == k
            return self.p[k][rs.start - k * self.n:rs.stop - k * self.n, cs]
    b.push_scope()
    io["y0"] = Rows(y0m, 512)
    phase1(nc, b, io)
    b.pop_scope()
    b.cc_allgather_multi(PAIRS, [(y0m[k].ap().opt(), y0a[k].ap().opt()) for k in range(2)])
    b.push_scope()

    def ysrc0(c):
        v = y0a[0 if c < 8 else 1].ap()
        r0 = ((c // 4) % 2) * 512 + (c % 4) * 128
        return (v[r0:r0 + 128, 0:1024], v[r0:r0 + 128, 1024:2048])
    phaseO(nc, b, {"ysrc": ysrc0, "sel": io["sel"], "resT": io["xres"], "wo": io["wo0"], "ng": io["ng1"],
                   "hTd": h1s.ap(), "hnTd": Rows(hnm, 1024)}, False, False)
    b.pop_scope()
    b.cc_allgather_multi(PAIRS, [(hnm[k].ap().opt(), hna[k].ap().opt()) for k in range(2)])
    b.push_scope()
    io3 = dict(io)

    def hnsrc(c):
        v = hna[c // 8].ap()
        r0 = (c % 8) * 128
        return [(0, v[r0:r0 + 128, :]), (1024, v[1024 + r0:1024 + r0 + 128, :])]
    io3["hnsrc"] = hnsrc
    io3["ys"], io3["sg"], io3["yd"] = y1m[0].ap(), y1m[1].ap(), y1m[2].ap()
    phase3(nc, b, io3)
    b.pop_scope()
    b.cc_allgather_multi(PAIRS, [(y1m[k].ap().opt(), y1a[k].ap().opt()) for k in range(3)])
    b.push_scope()

    def ysrc1(c):
        v = y1a[0 if c < 8 else 2].ap()
        r0 = ((c // 4) % 2) * 512 + (c % 4) * 128
        return (v[r0:r0 + 128, 0:1024], v[r0:r0 + 128, 1024:2048])

    def sgsrc(m):
        v = y1a[1].ap()
        r0 = (m // 4) * 512 + (m % 4) * 128
        return (v[r0:r0 + 128, 0:1024], v[r0:r0 + 128, 1024:2048])
    outs = phaseO(nc, b, {"ysrc": ysrc1, "sgsrc": sgsrc, "sel": io["sel"], "resT": h1s.ap(), "wo": io["wo1"],
                          "ng": io["ngf"], "wglu": io["wglu"], "bglu": io["bglu"], "outT": outT}, True, True)
    b.wait_all_outputs("sp", outs)
    b.pop_scope()
    b.emit()
    return nc


def kernel(**inp):
    inp = {k: np.asarray(v) for k, v in inp.items()}
    hc = host_consts()
    cores = list(range(8))
    nc = build_fused()
    shared = {
        "ng1": np.ascontiguousarray(inp["norm_g"][1].reshape(16, 128).T),
        "ngf": np.ascontiguousarray(inp["final_g"].reshape(16, 128).T),
        "bglu": np.ascontiguousarray(inp["s5_b_glu"][0].reshape(8, 128).T),
        "wo0": inp["ab_w_out"][0], "wo1": inp["cd_w_out"][0], "wglu": inp["s5_w_glu"][0],
    }
    half = {hh: dict(l3_inputs(inp, hh)) for hh in range(2)}
    l1h = {}
    maps = []
    for c in cores:
        bi, r = c // 2, c % 2
        m = dict(shared)
        m.update(half[r])
        m.update(l1_inputs(inp, bi, r))
        m["xres"] = np.ascontiguousarray(inp["x"][bi][r * 1024:(r + 1) * 1024].T)
        sel = np.zeros((128, 2), np.float32)
        sel[:, r] = 1.0
        m["sel"] = sel
        for k in ("c_ones", "c_ident", "c_masks", "c_tri", "c_ubd", "c_mbd", "c_onesf"):
            m[k] = hc[k]
        maps.append(m)
    res = run_bass_kernel_spmd(nc, maps, core_ids=cores).results
    out = np.empty((4, S, D), np.float32)
    for c in cores:
        bi, r = c // 2, c % 2
        out[bi, r * 1024:(r + 1) * 1024, :] = np.asarray(res[c]["outT"]).T
    return out
```
